# Optimizing a Trainium2 kernel written in Bass

```python
import math
import jax
import jax.numpy as jnp
from jax import lax
import numpy as np

D_MODEL = 2048
BATCH = 4
SEQ = 4096
DEPTH = 2

GRID_W = 64
CTX_LEN = 256
N_HEADS = 8
HEAD_DIM = 128
BRANCH = N_HEADS * HEAD_DIM
DIFF_QK = HEAD_DIM // 2
AXIS_DIM = DIFF_QK // 2
N_BRANCHES = 3
CHUNK = 64
Q_BLOCK = 128
ROPE_BASE = 10000.0
EPS = 1e-6
LB_FLOOR = 1e-20

(HG_Q, HG_FF, HG_FB, HG_I, HG_G, DA_Q, DA_K, DA_V, DA_G, RT_Q, RT_K, RT_V, RT_G, MERGE) = range(14)
N_BRANCH_COLS = 13
SPLIT_AT = [BRANCH * i for i in range(1, N_BRANCH_COLS + 1)]
IN_COLS = N_BRANCH_COLS * BRANCH + N_BRANCHES * D_MODEL

kernel_name = "hybrid_hgrn2_diffattn_retention_block"


def rmsnorm(t, gain):
    t32 = t.astype(jnp.float32)
    t32 = t32 * lax.rsqrt(jnp.mean(t32 * t32, axis=-1, keepdims=True) + EPS)
    return (t32 * gain.astype(jnp.float32)).astype(t.dtype)


def to_heads(t):
    return t.reshape(t.shape[:-1] + (N_HEADS, HEAD_DIM))


def diff_heads(t):
    return t.reshape(t.shape[:-1] + (N_HEADS, 2, DIFF_QK))


def head_rmsnorm(o, gain):
    return rmsnorm(o, gain.reshape(N_HEADS, HEAD_DIM)).reshape(o.shape[:-2] + (BRANCH,))


def rope_table(pos, dim, n_inner):
    inv_freq = ROPE_BASE ** (-jnp.arange(0, dim, 2, dtype=jnp.float32) / dim)
    ang = pos[:, None] * inv_freq[None, :]
    shape = (pos.shape[0],) + (1,) * n_inner + (dim // 2,)
    return (jnp.cos(ang).reshape(shape), jnp.sin(ang).reshape(shape))


def rope(x, cos, sin):
    half = x.shape[-1] // 2
    x1, x2 = x[..., :half], x[..., half:]
    return jnp.concatenate([x1 * cos - x2 * sin, x1 * sin + x2 * cos], axis=-1).astype(x.dtype)


def axial_rope(x, tables):
    cos_r, sin_r, cos_c, sin_c = tables
    half = x.shape[-1] // 2
    return jnp.concatenate([rope(x[..., :half], cos_r, sin_r), rope(x[..., half:], cos_c, sin_c)], axis=-1)


def chunk_gla(q, k, v, logf, s0):
    bsz, length, nh, _ = q.shape
    dv = v.shape[-1]
    n_chunks = length // CHUNK

    def blocks(t):
        return t.astype(jnp.float32).reshape(bsz, n_chunks, CHUNK, nh, t.shape[-1]).transpose(1, 0, 3, 2, 4)

    lower = jnp.tril(jnp.ones((CHUNK, CHUNK), dtype=bool))[:, :, None]

    def step(s, blk):
        qb, kb, vb, gb = blk
        cum = jnp.cumsum(gb, axis=2)
        diff = cum[:, :, :, None, :] - cum[:, :, None, :, :]
        decay = jnp.where(lower, jnp.exp(jnp.minimum(diff, 0.0)), 0.0)
        scores = jnp.einsum('bhtd,bhsd,bhtsd->bhts', qb, kb, decay)
        out = (jnp.einsum('bhts,bhsv->bhtv', scores, vb)
               + jnp.einsum('bhtd,bhdv->bhtv', qb * jnp.exp(cum), s))
        last = cum[:, :, -1:, :]
        s_new = (s * jnp.exp(last[:, :, 0, :, None])
                 + jnp.einsum('bhsd,bhsv->bhdv', kb * jnp.exp(last - cum), vb))
        return s_new, out

    s_fin, outs = lax.scan(step, s0, (blocks(q), blocks(k), blocks(v), blocks(logf)))
    o = outs.transpose(1, 0, 3, 2, 4).reshape(bsz, length, nh, dv)
    return o.astype(v.dtype), s_fin


def two_way_scan(ctx_f, ctx_b, lat_f, lat_b):
    q, _, v, _ = ctx_f
    s0 = jnp.zeros((q.shape[0], q.shape[2], q.shape[3], v.shape[3]), jnp.float32)
    flip = lambda t: jnp.flip(t, axis=1)
    oc_f, sc_f = chunk_gla(*ctx_f, s0)
    ox_f, _ = chunk_gla(*lat_f, sc_f)
    oc_b, sc_b = chunk_gla(*[flip(t) for t in ctx_b], s0)
    ox_b, _ = chunk_gla(*[flip(t) for t in lat_b], sc_b)
    return oc_f + flip(oc_b), ox_f + flip(ox_b)


def hgrn2_inputs(q, f_fwd, f_bwd, i, lower):
    q = to_heads(q) * HEAD_DIM ** -0.5
    v = to_heads(i)

    def forget(z, lb):
        z = to_heads(z).astype(jnp.float32)
        lb = lb.astype(jnp.float32).reshape(N_HEADS, HEAD_DIM)
        log_lb = jnp.log(jnp.maximum(lb, LB_FLOOR))
        log_f = jnp.logaddexp(log_lb, jnp.log1p(-lb) + jax.nn.log_sigmoid(z))
        key = (1.0 - lb) * jax.nn.sigmoid(-z)
        return key, log_f

    k_f, g_f = forget(f_fwd, lower[0])
    k_b, g_b = forget(f_bwd, lower[1])
    return (q, k_f, v, g_f), (q, k_b, v, g_b)


def retention_inputs(q, k, v, table, log_decay):
    cos, sin = table
    q = rope(to_heads(q), cos, sin)
    k = rope(to_heads(k), cos, sin) * HEAD_DIM ** -0.5
    v = to_heads(v)
    g_f = jnp.broadcast_to(log_decay[0][:, None], q.shape)
    g_b = jnp.broadcast_to(log_decay[1][:, None], q.shape)
    return (q, k, v, g_f), (q, k, v, g_b)


def diff_softmax_attn(q, k, v, lam):
    s = jnp.einsum('bqhcd,bkhcd->bhcqk', q, k, preferred_element_type=jnp.float32) * DIFF_QK ** -0.5
    p = jax.nn.softmax(s, axis=-1)
    a = p[:, :, 0] - lam * p[:, :, 1]
    return jnp.einsum('bhqk,bkhd->bqhd', a, v.astype(jnp.float32)).astype(v.dtype)


def blocked_diff_attn(q, k, v, lam):
    bsz, seq = q.shape[:2]
    n_blocks = seq // Q_BLOCK
    qb = q.reshape((bsz, n_blocks, Q_BLOCK) + q.shape[2:]).swapaxes(0, 1)
    ob = lax.map(lambda blk: diff_softmax_attn(blk, k, v, lam), qb)
    return ob.swapaxes(0, 1).reshape((bsz, seq) + ob.shape[3:])


def hybrid_layer(x, cx, c_act, cc_act, w_mod, b_mod, g_pre, g_post, w_in, lower, decay_logit,
                 lam_vec, lam_init, g_branch, w_branch, w_out, axial_x, rope_c, rope_x, update_ctx):
    d = D_MODEL
    mod_x = (c_act @ w_mod + b_mod)[:, None, :]
    mod_c = cc_act @ w_mod + b_mod
    w_parts = jnp.split(w_in, SPLIT_AT, axis=-1)

    def project(t, mod):
        h = rmsnorm(t, g_pre) * (1.0 + mod[..., d:2 * d]) + mod[..., :d]
        return [h @ w for w in w_parts]

    zx, zc = project(x, mod_x), project(cx, mod_c)

    hg_oc, hg_ox = two_way_scan(*hgrn2_inputs(zc[HG_Q], zc[HG_FF], zc[HG_FB], zc[HG_I], lower),
                                *hgrn2_inputs(zx[HG_Q], zx[HG_FF], zx[HG_FB], zx[HG_I], lower))

    log_decay = jax.nn.log_sigmoid(decay_logit.astype(jnp.float32))
    rt_oc, rt_ox = two_way_scan(*retention_inputs(zc[RT_Q], zc[RT_K], zc[RT_V], rope_c, log_decay),
                                *retention_inputs(zx[RT_Q], zx[RT_K], zx[RT_V], rope_x, log_decay))

    lam_vec = lam_vec.astype(jnp.float32)
    lam = jnp.exp(jnp.sum(lam_vec[0] * lam_vec[1])) - jnp.exp(jnp.sum(lam_vec[2] * lam_vec[3])) + lam_init
    kc, vc = diff_heads(zc[DA_K]), to_heads(zc[DA_V])
    k_all = jnp.concatenate([kc, axial_rope(diff_heads(zx[DA_K]), axial_x)], axis=1)
    v_all = jnp.concatenate([vc, to_heads(zx[DA_V])], axis=1)
    da_ox = blocked_diff_attn(axial_rope(diff_heads(zx[DA_Q]), axial_x), k_all, v_all, lam)

    def branch_outputs(hg, da, rt, z):
        return (head_rmsnorm(hg, g_branch[0]) * jax.nn.silu(z[HG_G]),
                head_rmsnorm(da, g_branch[1]) * (1.0 - lam_init) * jax.nn.silu(z[DA_G]),
                head_rmsnorm(rt, g_branch[2]) * jax.nn.silu(z[RT_G]))

    def update(t, ys, z, mod):
        merged = sum(jax.nn.sigmoid(z[MERGE][..., br * d:(br + 1) * d]) * (ys[br] @ w_branch[br])
                     for br in range(N_BRANCHES))
        return t + mod[..., 2 * d:] * rmsnorm(merged @ w_out, g_post)

    x_new = update(x, branch_outputs(hg_ox, da_ox, rt_ox, zx), zx, mod_x)
    if not update_ctx:
        return x_new, cx
    da_oc = diff_softmax_attn(diff_heads(zc[DA_Q]), kc, vc, lam)
    return x_new, update(cx, branch_outputs(hg_oc, da_oc, rt_oc, zc), zc, mod_c)


def setup_inputs(seed: int = 0) -> dict:
    key = jax.random.key(seed)
    ks = jax.random.split(key, 16)
    nrm = lambda k, shape, s: jax.random.normal(k, shape, jnp.float32) * s
    gam = 1.0 - 2.0 ** (-5.0 - jnp.arange(N_HEADS, dtype=jnp.float32))
    decay_logit = jnp.log(gam) - jnp.log1p(-gam)
    return {
        "x": nrm(ks[0], (BATCH, SEQ, D_MODEL), 1.0),
        "c": nrm(ks[1], (BATCH, D_MODEL), 1.0),
        "ctx": nrm(ks[2], (BATCH, CTX_LEN, D_MODEL), 1.0),
        "c_ctx": nrm(ks[3], (D_MODEL,), 1.0),
        "w_mod": nrm(ks[4], (DEPTH, D_MODEL, 3 * D_MODEL), 0.5 * D_MODEL ** -0.5),
        "b_mod": nrm(ks[5], (DEPTH, 3 * D_MODEL), 0.02),
        "g_pre": 1.0 + nrm(ks[6], (DEPTH, D_MODEL), 0.02),
        "g_post": 1.0 + nrm(ks[7], (DEPTH, D_MODEL), 0.02),
        "w_in": nrm(ks[8], (DEPTH, D_MODEL, IN_COLS), D_MODEL ** -0.5),
        "hg_lower": nrm(ks[9], (2, DEPTH, BRANCH), 0.5),
        "ret_decay": decay_logit + nrm(ks[10], (DEPTH, 2, N_HEADS), 0.1),
        "diff_lambda": nrm(ks[11], (DEPTH, 4, DIFF_QK), 0.1),
        "g_branch": 1.0 + nrm(ks[12], (DEPTH, N_BRANCHES, BRANCH), 0.02),
        "w_branch": nrm(ks[13], (DEPTH, N_BRANCHES, BRANCH, D_MODEL), BRANCH ** -0.5),
        "w_out": nrm(ks[14], (DEPTH, D_MODEL, D_MODEL), D_MODEL ** -0.5),
    }


def reference(x, c, ctx, c_ctx, w_mod, b_mod, g_pre, g_post, w_in, hg_lower,
              ret_decay, diff_lambda, g_branch, w_branch, w_out):
    seq = x.shape[1]
    ctx_len = ctx.shape[1]
    n_rows = seq // GRID_W
    row_pos = jnp.repeat(jnp.arange(n_rows, dtype=jnp.float32), GRID_W)
    col_pos = jnp.tile(jnp.arange(GRID_W, dtype=jnp.float32), n_rows)
    axial_x = rope_table(row_pos, AXIS_DIM, 2) + rope_table(col_pos, AXIS_DIM, 2)
    rope_c = rope_table(jnp.arange(ctx_len, dtype=jnp.float32), HEAD_DIM, 1)
    rope_x = rope_table(ctx_len + jnp.arange(seq, dtype=jnp.float32), HEAD_DIM, 1)
    lb_w = jax.nn.softmax(hg_lower.astype(jnp.float32), axis=1)
    lower_bounds = jnp.cumsum(lb_w, axis=1) - lb_w[:, :1]
    c_act = jax.nn.silu(c)
    cc_act = jax.nn.silu(c_ctx)
    for layer in range(DEPTH):
        x, ctx = hybrid_layer(x, ctx, c_act, cc_act, w_mod[layer], b_mod[layer], g_pre[layer],
                              g_post[layer], w_in[layer], lower_bounds[:, layer], ret_decay[layer],
                              diff_lambda[layer], 0.8 - 0.6 * math.exp(-0.3 * layer),
                              g_branch[layer], w_branch[layer], w_out[layer],
                              axial_x, rope_c, rope_x, layer < DEPTH - 1)
    return x
```

```python
import contextlib
import math
import numpy as np
import ml_dtypes
import concourse.bass as bass
import concourse.mybir as mybir
from concourse.bass_utils import run_bass_kernel_spmd

F32 = mybir.dt.float32
BF16 = mybir.dt.bfloat16
I32 = mybir.dt.int32
AF = mybir.ActivationFunctionType
ALU = mybir.AluOpType
AX = mybir.AxisListType

D = 2048
KC = 16
TCX = 256
TL = 2048
TT = 2304
NPAIR = 18
NCH = 36
NH = 8
IN_COLS = 19456
(HG_Q, HG_FF, HG_FB, HG_I, HG_G, DA_Q, DA_K, DA_V, DA_G, RT_Q, RT_K, RT_V, RT_G) = range(13)
MERGE0 = 13 * 1024
LAM_INIT = [0.8 - 0.6 * math.exp(-0.3 * l) for l in range(2)]
EPS = 1e-6
TG = [(0, 256), (256, 512), (768, 512), (1280, 512), (1792, 512)]
SAME_ENGINE_WAITS = True
NO_SELF_WAIT = ("pe", "act")
(K_HGQF, K_HGQB, K_HGKF, K_HGKB, K_HGG, K_RTQF, K_RTQB, K_RTKF, K_RTKB, K_RTG, K_DAQ, K_DAG, K_DAK) = range(13)


class Buf:
    __slots__ = ("name", "t", "writers", "readers", "dirty_read", "semkey", "is_dram")

    def __init__(self, name, t=None, is_dram=False):
        self.name = name
        self.t = t
        self.is_dram = is_dram
        self.writers = {}
        self.readers = {}
        self.dirty_read = False
        self.semkey = "d_" + name

    def reset(self):
        self.writers = {}
        self.readers = {}
        self.dirty_read = False

    def __getitem__(self, idx):
        return self.t[idx]


_UID = [0]


class Phase:
    ENGINES = ("pe", "act", "dve", "pool", "sp")

    def __init__(self, nc, persist):
        self.nc = nc
        self.persist = persist
        self.stack = contextlib.ExitStack()
        self.streams = {e: [] for e in self.ENGINES}
        self.semcnt = {}
        self.semh = {}
        self.waited = {e: {} for e in self.ENGINES}
        for e in ("pe", "act", "dve", "pool"):
            self.new_sem(e)
        self.n = 0

    def new_sem(self, key):
        self.semcnt[key] = 0
        _UID[0] += 1
        self.semh[key] = self.nc.alloc_semaphore(name="s%d_%s" % (_UID[0], key))

    def sbuf(self, name, shape, dtype):
        _UID[0] += 1
        name = "%s_u%d" % (name, _UID[0])
        t = self.stack.enter_context(self.nc.sbuf_tensor(name, list(shape), dtype))
        return Buf(name, t)

    def psum(self, name, shape, dtype):
        _UID[0] += 1
        name = "%s_u%d" % (name, _UID[0])
        t = self.stack.enter_context(self.nc.psum_tensor(name, list(shape), dtype))
        return Buf(name, t)

    def _collect(self, eng, reads, writes):
        deps = {}

        def add(d):
            for k, v in d.items():
                if deps.get(k, 0) < v:
                    deps[k] = v
        for r in reads:
            add(r.writers)
        for w in writes:
            add(w.readers)
            if not w.is_dram:
                add(w.writers)
        waits = []
        wd = self.waited[eng]
        for k, v in deps.items():
            if k == eng and (eng in NO_SELF_WAIT or not SAME_ENGINE_WAITS):
                continue
            if wd.get(k, 0) < v:
                wd[k] = v
                waits.append((k, v))
        return waits

    def _commit(self, ev, reads, writes):
        k, v = ev
        for r in reads:
            if r.readers.get(k, 0) < v:
                r.readers[k] = v
            r.dirty_read = True
        for w in writes:
            if w.dirty_read:
                w.writers = {}
                w.readers = {}
                w.dirty_read = False
            if w.writers.get(k, 0) < v:
                w.writers[k] = v

    def op(self, eng, fn, R=(), W=()):
        waits = self._collect(eng, R, W)
        self.semcnt[eng] += 1
        self._commit((eng, self.semcnt[eng]), R, W)
        self.streams[eng].append((waits, fn, (eng, 1)))

    def dma(self, out, in_, R=(), W=(), owner=None, q="sp"):
        key = owner.semkey
        for w in W:
            if w.is_dram:
                key = key + "_" + w.name
        if key not in self.semh:
            self.new_sem(key)
        waits = self._collect(q, R, W)
        self.semcnt[key] += 16
        self._commit((key, self.semcnt[key]), R, W)
        self.streams[q].append((waits, lambda e: e.dma_start(out=out, in_=in_), (key, 16)))

    def custom(self, eng, fn, key, inc, R=(), W=()):
        if key not in self.semh:
            self.new_sem(key)
        waits = self._collect(eng, R, W)
        self.semcnt[key] += inc
        self._commit((key, self.semcnt[key]), R, W)
        self.streams[eng].append((waits, fn, (key, inc)))

    def close(self):
        nc = self.nc
        semh = self.semh
        streams = self.streams
        final = [(k, v) for k, v in self.semcnt.items() if v > 0]
        for eng in ("sp", "pool"):
            streams[eng].append(([(k, v) for k, v in final if self.waited[eng].get(k, 0) < v], None, None))

        def run(engobj, lst):
            for waits, fn, inc in lst:
                for k, v in waits:
                    engobj.wait_ge(semh[k], v)
                if fn is not None:
                    fn(engobj).then_inc(semh[inc[0]], inc[1])

        with nc.Block() as block:
            @block.tensor
            def _(e):
                run(e, streams["pe"])

            @block.scalar
            def _(e):
                run(e, streams["act"])

            @block.vector
            def _(e):
                run(e, streams["dve"])

            @block.gpsimd
            def _(e):
                run(e, streams["pool"])

            @block.sync
            def _(e):
                run(e, streams["sp"])
        nc.all_engine_barrier()
        nc.clear_and_free_semaphores(list(self.semh.values()))
        nc.all_engine_barrier()
        self.stack.close()
        for b in self.persist:
            b.reset()

    def act(self, out, in_, func, R, W, scale=1.0, bias=0.0, accum=None):
        if accum is None:
            self.op("act", lambda e: e.activation(out=out, in_=in_, func=func, bias=bias, scale=scale), R, W)
        else:
            self.op("act", lambda e: e.activation(out=out, in_=in_, func=func, bias=bias, scale=scale,
                                                  accum_out=accum), R, W)

    def ts(self, out, in0, s1, s2, op0, op1, R, W, eng="dve"):
        if s2 is None:
            self.op(eng, lambda e: e.tensor_scalar(out=out, in0=in0, scalar1=s1, scalar2=None, op0=op0), R, W)
        else:
            self.op(eng, lambda e: e.tensor_scalar(out=out, in0=in0, scalar1=s1, scalar2=s2, op0=op0, op1=op1), R, W)

    def tt(self, out, in0, in1, op, R, W, eng="dve"):
        self.op(eng, lambda e: e.tensor_tensor(out=out, in0=in0, in1=in1, op=op), R, W)

    def stt(self, out, in0, scalar, in1, op0, op1, R, W):
        self.op("dve", lambda e: e.scalar_tensor_tensor(out=out, in0=in0, scalar=scalar, in1=in1, op0=op0, op1=op1),
                R, W)

    def copy(self, out, in_, R, W, eng="dve"):
        self.op(eng, lambda e: e.tensor_copy(out=out, in_=in_), R, W)

    def mm(self, out, lhsT, rhs, start, stop, R, W):
        self.op("pe", lambda e: e.matmul(out=out, lhsT=lhsT, rhs=rhs, start=start, stop=stop, skip_group_check=True),
                R, W)

    def tr(self, out, in_, ident, R, W):
        self.op("pe", lambda e: e.transpose(out=out, in_=in_, identity=ident), R, W)


class Rot:
    def __init__(self, bufs):
        self.bufs = bufs
        self.i = 0

    def next(self):
        b = self.bufs[self.i % len(self.bufs)]
        self.i += 1
        return b


def build_program(stop_after=None, dbg=False, opts=None):
    opts = opts or {}
    NHD = opts.get('heads', NH)
    SEC = opts.get('sections', 'hg,rt,da,mg')
    nc = bass.Bass("TRN2", target_bir_lowering=False)
    persist = []
    pstack = contextlib.ExitStack()

    def dram(name, shape, dtype, kind="Internal"):
        if dbg and kind == "Internal" and name in opts.get("dump", ()):
            kind = "ExternalOutput"
        b = Buf(name, nc.dram_tensor(name, list(shape), dtype, kind=kind), is_dram=True)
        persist.append(b)
        return b

    def psb(name, shape, dtype):
        b = Buf(name, pstack.enter_context(nc.sbuf_tensor(name, list(shape), dtype)))
        persist.append(b)
        return b

    EI = "ExternalInput"
    x_in = dram("x", [TL, D], F32, EI)
    ctx_in = dram("ctx", [TCX, D], F32, EI)
    cvec_d = dram("cvec", [128, KC * 2], F32, EI)
    wmod_d = dram("w_mod", [2, D, 3 * D], F32, EI)
    bmod_d = dram("bmod_fm", [128, 2 * 48], F32, EI)
    gpre_d = dram("gpre_fm", [128, 2 * KC], F32, EI)
    gpost_d = dram("gpost_fm", [128, 2 * KC], F32, EI)
    win_d = dram("w_in", [2, D, IN_COLS], F32, EI)
    hgl_d = dram("hgl_fm", [128, 2 * 2 * NH], F32, EI)
    rdec_d = dram("rdec_b", [128, 32], F32, EI)
    dlam_d = dram("dlam_b", [128, 512], F32, EI)
    gbr_d = dram("gbr_fm", [128, 2 * 3 * NH], F32, EI)
    wbr_d = dram("w_branch", [2, 3, 1024, D], F32, EI)
    wout_d = dram("w_out", [2, D, D], F32, EI)
    ident_d = dram("ident", [128, 128], BF16, EI)
    identf_d = dram("identf", [128, 128], F32, EI)
    maskf_d = dram("maskF", [128, 128], I32, EI)
    maskb_d = dram("maskB", [128, 128], I32, EI)
    permrt_d = dram("perm_rt", [128, 128], BF16, EI)
    permda_d = dram("perm_da", [128, 128], BF16, EI)
    tabs_d = dram("tabs", [4, 128, TT], BF16, EI)
    tau_d = dram("tau", [128, 512], F32, EI)
    flags_d = dram("flags", [128, 2], F32, EI)
    out_d = dram("out", [TL, D], F32, "ExternalOutput")

    x1_d = dram("x1", [TL, D], F32)
    c1_d = dram("c1", [TCX, D], F32)
    spf_d = dram("spf", [13, NH, 128, TT], BF16)
    spkp_d = dram("spkp", [2, 2, NH, TT, 128], BF16)
    spv_d = dram("spv", [3, NH, TT, 128], BF16)
    spdec_d = dram("spdec", [2, 2, NH, 128, 2 * NCH], F32)
    gm_d = dram("gm", [48, 128, TT], BF16)
    ys_d = dram("ys", [24, 128, TT], BF16)
    xk_in = [dram("xk_in%d" % i, [4 * 128, TL], BF16) for i in range(2)]
    xk_out = [dram("xk_out%d" % i, [2 * 4 * 128, TL], BF16) for i in range(2)]
    xv_in = [dram("xv_in%d" % i, [4 * TL, 128], BF16) for i in range(2)]
    xv_out = [dram("xv_out%d" % i, [2 * 4 * TL, 128], BF16) for i in range(2)]
    xs_in = dram("xs_in", [2 * 2 * NH * 128, 128], F32)
    xs_out = dram("xs_out", [2 * 2 * 2 * NH * 128, 128], F32)
    wbrb_d = dram("wbrb", [3, KC, 128, 8 * 128], BF16)
    woutb_d = dram("woutb", [128, KC * D], BF16)
    dbg_d = dram("dbg", [128, TT], F32, "ExternalOutput") if dbg else None

    ident = psb("ident_s", [128, 128], BF16)
    identf = psb("identf_s", [128, 128], F32)
    maskF = psb("maskF_s", [128, 128], I32)
    maskB = psb("maskB_s", [128, 128], I32)
    ones_b = psb("ones_b", [128, 128], BF16)
    onesm_b = psb("onesm_b", [128, 128], BF16)
    flags = psb("flags_s", [128, 2], F32)
    oms = psb("oms", [128, 2], F32)
    Avec = psb("Avec", [128, 2, 2, KC], F32)
    Bvec = psb("Bvec", [128, 2, 2, KC], F32)
    GGvec = psb("GGvec", [128, 2, 2, KC], F32)
    LBF = psb("LBF", [128, 2, 2, NH], F32)
    OML = psb("OML", [128, 2, 2, NH], F32)
    NOML = psb("NOML", [128, 2, 2, NH], F32)
    LD = psb("LD", [128, 5, 32], F32)
    NLAM = psb("NLAM", [128, 2], F32)
    GBR = psb("GBR", [128, 2, 3, NH], F32)

    def stopped(tag):
        return stop_after is not None and tag == stop_after

    S = Phase(nc, persist)
    cv = S.sbuf("cv", [128, KC, 2], F32)
    cact = S.sbuf("cact", [128, KC, 2], F32)
    bm = S.sbuf("bm", [128, 2, 48], F32)
    gpre = S.sbuf("gpre", [128, 2, KC], F32)
    gpost = S.sbuf("gpost", [128, 2, KC], F32)
    hgl = S.sbuf("hgl", [128, 2, 2, NH], F32)
    rdec = S.sbuf("rdec", [128, 32], F32)
    dlam = S.sbuf("dlam", [128, 2, 4, 64], F32)
    tmp0 = S.sbuf("tmp0", [128, 64], F32)
    sm0 = S.sbuf("sm0", [128, 4], F32)
    modT = S.sbuf("modT", [128, 2, 48, 2], F32)
    wm = Rot([S.sbuf("wm%d" % i, [128, 3 * D], F32) for i in range(2)])
    psm = S.psum("psm", [128, 512], F32)

    def ld(dst, src_ap, srcb, dst_ap=None):
        S.dma(dst[:] if dst_ap is None else dst_ap, src_ap, R=[srcb], W=[dst], owner=dst)
    ld(ident, ident_d[:, :], ident_d)
    ld(identf, identf_d[:, :], identf_d)
    ld(maskF, maskf_d[:, :], maskf_d)
    ld(maskB, maskb_d[:, :], maskb_d)
    ld(flags, flags_d[:, :], flags_d)
    ld(cv, cvec_d[:, :], cvec_d, cv[:].rearrange("p a b -> p (a b)"))
    ld(bm, bmod_d[:, :], bmod_d, bm[:].rearrange("p a b -> p (a b)"))
    ld(gpre, gpre_d[:, :], gpre_d, gpre[:].rearrange("p a b -> p (a b)"))
    ld(gpost, gpost_d[:, :], gpost_d, gpost[:].rearrange("p a b -> p (a b)"))
    ld(hgl, hgl_d[:, :], hgl_d, hgl[:].rearrange("p a b c -> p (a b c)"))
    ld(rdec, rdec_d[:, :], rdec_d)
    ld(dlam, dlam_d[:, :], dlam_d, dlam[:].rearrange("p a b c -> p (a b c)"))
    ld(GBR, gbr_d[:, :], gbr_d, GBR[:].rearrange("p a b c -> p (a b c)"))
    S.op("pool", lambda e: e.memset(ones_b[:], 1.0), W=[ones_b])
    S.op("pool", lambda e: e.memset(onesm_b[:], 1.0 / 128), W=[onesm_b])
    S.ts(oms[:], flags[:], -1.0, 1.0, ALU.mult, ALU.add, [flags], [oms])
    S.act(cact[:], cv[:], AF.Sigmoid, [cv], [cact])
    S.tt(cact[:], cact[:], cv[:], ALU.mult, [cv, cact], [cact])
    S.op("pool", lambda e: e.memset(LBF[:, 0], 1e-20), W=[LBF])
    S.op("pool", lambda e: e.memset(OML[:, 0], 1.0), W=[OML])
    S.tt(OML[:, 1], hgl[:, :, 1, :], hgl[:, :, 0, :], ALU.subtract, [hgl], [OML])
    S.act(LBF[:, 1], OML[:, 1], AF.Sigmoid, [OML], [LBF])
    S.ts(OML[:, 1], LBF[:, 1], -1.0, 1.0, ALU.mult, ALU.add, [LBF], [OML])
    S.ts(LBF[:, 1], LBF[:, 1], 1e-20, None, ALU.max, None, [LBF], [LBF])
    S.ts(NOML[:], OML[:], -1.0, None, ALU.mult, None, [OML], [NOML])
    S.act(LD[:, 1, :], rdec[:], AF.Exp, [rdec], [LD], scale=-1.0)
    S.act(LD[:, 1, :], LD[:, 1, :], AF.Ln, [LD], [LD], bias=1.0)
    S.ts(LD[:, 0, :], LD[:, 1, :], -1.0, None, ALU.mult, None, [LD], [LD])
    S.ts(LD[:, 2, :], LD[:, 0, :], 63.0, None, ALU.mult, None, [LD], [LD])
    S.ts(LD[:, 3, :], LD[:, 0, :], 64.0, None, ALU.mult, None, [LD], [LD])
    S.ts(LD[:, 4, :], LD[:, 0, :], -64.0, None, ALU.mult, None, [LD], [LD])
    for L in range(2):
        for j in range(2):
            S.tt(tmp0[:], dlam[:, L, 2 * j, :], dlam[:, L, 2 * j + 1, :], ALU.mult, [dlam], [tmp0])
            S.op("dve", lambda e, L=L, j=j: e.reduce_sum(out=sm0[:, 2 * L + j:2 * L + j + 1], in_=tmp0[:], axis=AX.X),
                 [tmp0], [sm0])
    S.act(sm0[:], sm0[:], AF.Exp, [sm0], [sm0])
    for L in range(2):
        S.tt(NLAM[:, L:L + 1], sm0[:, 2 * L + 1:2 * L + 2], sm0[:, 2 * L:2 * L + 1], ALU.subtract, [sm0], [NLAM])
        S.ts(NLAM[:, L:L + 1], NLAM[:, L:L + 1], -LAM_INIT[L], None, ALU.add, None, [NLAM], [NLAM])
        S.ts(GBR[:, L, 1, :], GBR[:, L, 1, :], 1.0 - LAM_INIT[L], None, ALU.mult, None, [GBR], [GBR])
    for L in range(2):
        for kc in range(KC):
            w = wm.next()
            S.dma(w[:, 0:3072], wmod_d[L, kc * 128:(kc + 1) * 128, 0:3072], R=[wmod_d], W=[w], owner=w)
            S.dma(w[:, 3072:6144], wmod_d[L, kc * 128:(kc + 1) * 128, 3072:6144], R=[wmod_d], W=[w], owner=w)
            for cc in range(48):
                S.mm(psm[:, 2 * cc:2 * cc + 2], w[:, cc * 128:(cc + 1) * 128], cact[:, kc, :],
                     (kc == 0 and cc == 0), (kc == KC - 1), [w, cact], [psm])
        for j in range(2):
            S.tt(modT[:, L, :, j], psm[:, 0:96].rearrange("p (a b) -> p a b", b=2)[:, :, j], bm[:, L, :], ALU.add,
                 [psm, bm], [modT])
        for j in range(2):
            S.stt(Avec[:, L, j, :], modT[:, L, 16:32, j], 1.0, gpre[:, L, :], ALU.add, ALU.mult, [modT, gpre], [Avec])
            S.copy(Bvec[:, L, j, :], modT[:, L, 0:16, j], [modT], [Bvec])
            S.tt(GGvec[:, L, j, :], modT[:, L, 32:48, j], gpost[:, L, :], ALU.mult, [modT, gpost], [GGvec])
    S.close()

    for L in opts.get('layers', range(2)):
        xsrc = x_in if (L == 0 or 'layers' in opts) else x1_d
        csrc = ctx_in if (L == 0 or 'layers' in opts) else c1_d
        lstack = contextlib.ExitStack()
        hT = Buf("hT", lstack.enter_context(nc.sbuf_tensor("hT%d_%d" % (L, _UID[0]), [128, KC, TT], BF16)))
        persist.append(hT)

        S = Phase(nc, persist)
        xt = Rot([S.sbuf("xt%d" % i, [128, D], F32) for i in range(2)])
        xn = Rot([S.sbuf("xn%d" % i, [128, D], BF16) for i in range(2)])
        junk = S.sbuf("junk", [128, D], BF16)
        ssq = Rot([S.sbuf("ssq%d" % i, [128, 1], F32) for i in range(2)])
        pst = Rot([S.psum("pst%d" % i, [128, 1024], BF16) for i in range(4)])
        for i in range(NPAIR):
            a = xt.next()
            if i < 2:
                S.dma(a[:], csrc[i * 128:(i + 1) * 128, :], R=[csrc], W=[a], owner=a)
            else:
                S.dma(a[:], xsrc[(i - 2) * 128:(i - 1) * 128, :], R=[xsrc], W=[a], owner=a)
            s = ssq.next()
            S.act(junk[:], a[:], AF.Square, [a], [junk, s], accum=s[:])
            S.ts(s[:], s[:], 1.0 / D, EPS, ALU.mult, ALU.add, [s], [s])
            S.act(s[:], s[:], AF.Sqrt, [s], [s])
            S.op("dve", lambda e, s=s: e.reciprocal(out=s[:], in_=s[:]), [s], [s])
            b = xn.next()
            S.ts(b[:], a[:], s[:], None, ALU.mult, None, [a, s], [b])
            j = 1 if i < 2 else 0
            for q in range(4):
                p = pst.next()
                for r in range(4):
                    kc = q * 4 + r
                    S.tr(p[:, r * 128:(r + 1) * 128], b[:, kc * 128:(kc + 1) * 128], ident[:], [b, ident], [p])
                for r in range(4):
                    kc = q * 4 + r
                    S.act(hT[:, kc, i * 128:(i + 1) * 128], p[:, r * 128:(r + 1) * 128], AF.Identity, [p, Avec, Bvec],
                          [hT], scale=Avec[:, L, j, kc:kc + 1], bias=Bvec[:, L, j, kc:kc + 1])
        S.close()
        if stopped("S1_%d" % L):
            lstack.close()
            break

        S = Phase(nc, persist)
        tabt = Rot([S.sbuf("tabt%d" % i, [128, 512], BF16) for i in range(4)])
        tau = S.sbuf("tau_s", [128, 64], F32)
        ones64 = S.sbuf("ones64", [128, 64], F32)
        S.op("pool", lambda e: e.memset(ones64[:], 1.0), W=[ones64])
        permrt = S.sbuf("permrt", [128, 128], BF16)
        permda = S.sbuf("permda", [128, 128], BF16)
        S.dma(tau[:], tau_d[:, 0:64], R=[tau_d], W=[tau], owner=tau)
        S.dma(permrt[:], permrt_d[:, :], R=[permrt_d], W=[permrt], owner=permrt)
        S.dma(permda[:], permda_d[:, :], R=[permda_d], W=[permda], owner=permda)
        wst = Rot([S.sbuf("wst%d" % i, [128, KC, 128], F32) for i in range(2)])
        wbf = Rot([S.sbuf("wbf%d" % i, [128, KC, 128], BF16) for i in range(2)])
        wstM = Rot([S.sbuf("wstM%d" % i, [128, KC, 128], F32) for i in range(1)])
        wbfM = Rot([S.sbuf("wbfM%d" % i, [128, KC, 128], BF16) for i in range(2)])
        sig = [S.sbuf("sig%d" % i, [128, TT], F32) for i in range(2)]
        zq = S.sbuf("zq", [128, TT], BF16)
        zr = zq
        rq = S.sbuf("rq", [128, TT], BF16)
        rk = S.sbuf("rk", [128, TT], BF16)
        t512 = Rot([S.sbuf("t512_%d" % i, [128, 512], F32) for i in range(7)])
        o512 = Rot([S.sbuf("o512_%d" % i, [128, 512], BF16) for i in range(8)])
        o2304 = Rot([S.sbuf("o2304_%d" % i, [128, TT], BF16) for i in range(2)])
        etab = [S.sbuf("etab%d" % i, [128, 64], F32) for i in range(6)]
        vtok = Rot([S.sbuf("vtok%d" % i, [128, NPAIR, 128], BF16) for i in range(2)])
        kpt = [Rot([S.sbuf("kpt%d_%d" % (d_, i), [128, NPAIR, 128], BF16) for i in range(1)]) for d_ in range(2)]
        dec = [Rot([S.sbuf("dec%d_%d" % (d_, i), [128, 2 * NCH], F32) for i in range(2)]) for d_ in range(2)]
        ntot = Rot([S.sbuf("ntot%d" % i, [128, 8], F32) for i in range(4)])
        Sst = [[S.sbuf("Sst%d_%d" % (d_, i), [128, 128], F32) for i in range(2)] for d_ in range(2)]
        psA = Rot([S.psum("psA%d" % i, [128, 512], F32) for i in range(5)])
        psM = Rot([S.psum("psM%d" % i, [128, 512], F32) for i in range(2)])
        psT = Rot([S.psum("psT%d" % i, [128, 1024], BF16) for i in range(1)])

        def load_w(col0, wst=wst, wbf=wbf):
            a = wst.next()
            S.dma(a[:, 0:8, :], win_d[L, 0:1024, col0:col0 + 128].rearrange("(k p) c -> p k c", p=128),
                  R=[win_d], W=[a], owner=a)
            S.dma(a[:, 8:16, :], win_d[L, 1024:2048, col0:col0 + 128].rearrange("(k p) c -> p k c", p=128),
                  R=[win_d], W=[a], owner=a)
            b = wbf.next()
            S.copy(b[:], a[:], [a], [b], eng="pool")
            return b

        def proj_fm(w, evac):
            for (t0, wd) in TG:
                p = psA.next()
                for kc in range(KC):
                    S.mm(p[:, 0:wd], w[:, kc, :], hT[:, kc, t0:t0 + wd], kc == 0, kc == KC - 1, [w, hT], [p])
                evac(p, t0, wd)

        def proj_tm(w, dst):
            for i4 in range(0, NPAIR, 4):
                n = min(4, NPAIR - i4)
                p = psA.next()
                for r in range(n):
                    i = i4 + r
                    for kc in range(KC):
                        S.mm(p[:, r * 128:(r + 1) * 128], hT[:, kc, i * 128:(i + 1) * 128], w[:, kc, :],
                             (kc == 0 and r == 0), kc == KC - 1, [w, hT], [p])
                S.copy(dst[:, i4:i4 + n, :], p[:, 0:n * 128].rearrange("p (a b) -> p a b", b=128), [p], [dst])

        def spill_fm(kind, h, src, t0=0, wd=TT, src_ap=None):
            S.dma(spf_d[kind, h, :, t0:t0 + wd], src[:, 0:wd] if src_ap is None else src_ap,
                  R=[src], W=[spf_d], owner=src)

        def kp_transpose(src, t0, wd, dst):
            p = psT.next()
            n = wd // 128
            for r in range(n):
                S.tr(p[:, r * 128:(r + 1) * 128], src[:, r * 128:(r + 1) * 128], ident[:], [src, ident], [p])
            S.copy(dst[:, t0 // 128:t0 // 128 + n, :], p[:, 0:n * 128].rearrange("p (a b) -> p a b", b=128),
                   [p], [dst])

        def rope(dst, tcos, tsin, perm, post_scale=None):
            for (t0, wd) in TG:
                p = psA.next()
                S.mm(p[:, 0:wd], perm[:], zr[:, t0:t0 + wd], True, True, [perm, zr], [p])
                tc_ = tabt.next()
                S.dma(tc_[:, 0:wd], tabs_d[tcos, :, t0:t0 + wd], R=[tabs_d], W=[tc_], owner=tc_)
                tsn = tabt.next()
                S.dma(tsn[:, 0:wd], tabs_d[tsin, :, t0:t0 + wd], R=[tabs_d], W=[tsn], owner=tsn)
                a = t512.next()
                S.tt(a[:, 0:wd], zr[:, t0:t0 + wd], tc_[:, 0:wd], ALU.mult, [zr, tc_], [a])
                b = t512.next()
                S.tt(b[:, 0:wd], p[:, 0:wd], tsn[:, 0:wd], ALU.mult, [p, tsn], [b])
                if post_scale is None:
                    S.tt(dst[:, t0:t0 + wd], a[:, 0:wd], b[:, 0:wd], ALU.add, [a, b], [dst], eng="pool")
                else:
                    S.tt(a[:, 0:wd], a[:, 0:wd], b[:, 0:wd], ALU.add, [a, b], [a], eng="pool")
                    S.ts(dst[:, t0:t0 + wd], a[:, 0:wd], post_scale, None, ALU.mult, None, [a], [dst], eng="pool")

        def run1(mix, h, kp, v, dc):
            orders = [list(range(NCH)), [3, 2, 1, 0] + list(range(NCH - 1, 3, -1))]
            cur = [0, 0]
            for d_ in range(2):
                S.op("pool", lambda e, d_=d_: e.memset(Sst[d_][0][:], 0.0), W=[Sst[d_][0]])
            for st in range(0, NCH, 4):
                if (st // 4) % 2 == 0:
                    bgstep()
                for d_ in range(2):
                    pp = [psA.next(), psA.next()]
                    for r in range(4):
                        j = orders[d_][st + r]
                        pr, hf = j // 2, (j % 2) * 64
                        p = pp[j % 2]
                        S.mm(p[:, r * 128:(r + 1) * 128], kp[d_][hf:hf + 64, pr, :], v[hf:hf + 64, pr, :],
                             r < 2, True, [kp[d_], v], [p])
                    for r in range(4):
                        j = orders[d_][st + r]
                        p = pp[j % 2]
                        a, b = Sst[d_][cur[d_]], Sst[d_][1 - cur[d_]]
                        S.stt(b[:], a[:], dc[d_][:, j:j + 1], p[:, r * 128:(r + 1) * 128], ALU.mult, ALU.add,
                              [a, dc[d_], p], [b])
                        cur[d_] = 1 - cur[d_]
            for d_ in range(2):
                a = Sst[d_][cur[d_]]
                r0 = ((mix * 2 + d_) * NH + h) * 128
                S.dma(xs_in[r0:r0 + 128, :], a[:], R=[a], W=[xs_in], owner=a)

        jobs = []
        for h in range(NH):
            for g in (HG_FF, HG_FB, HG_Q, HG_I, HG_G, RT_Q, RT_K, RT_V, RT_G, DA_Q, DA_K, DA_V, DA_G):
                jobs.append(g * 1024 + h * 128)
        wq = []

        def merge_gen():
            if 'mg' not in SEC:
                return
            wcur = load_w(MERGE0, wstM, wbfM)
            pend = None

            def evac(pd):
                p, t0, wd, fc = pd
                o = o512.next()
                S.copy(o[:, 0:wd], p[:, 0:wd], [p], [o])
                S.dma(gm_d[fc, :, t0:t0 + wd], o[:, 0:wd], R=[o], W=[gm_d], owner=o)
            for fc in range(48):
                w = wcur
                if fc + 1 < 48:
                    wcur = load_w(MERGE0 + (fc + 1) * 128, wstM, wbfM)
                for (t0, wd) in TG:
                    p = psM.next()
                    for kc in range(KC):
                        S.mm(p[:, 0:wd], w[:, kc, :], hT[:, kc, t0:t0 + wd], kc == 0, kc == KC - 1, [w, hT], [p])
                    if pend is not None:
                        evac(pend)
                    pend = (p, t0, wd, fc)
                    yield
            evac(pend)
        bg = merge_gen()

        def bgstep(n=1):
            for _ in range(n):
                try:
                    next(bg)
                except StopIteration:
                    return

        def next_w():
            if not wq and jobs:
                wq.append(load_w(jobs.pop(0)))
            w = wq.pop(0)
            if jobs:
                wq.append(load_w(jobs.pop(0)))
            return w

        for h in range(NHD):
            li = L * 16
            for d_ in range(2):
                w = next_w()
                proj_fm(w, lambda p, t0, wd, d_=d_: S.act(sig[d_][:, t0:t0 + wd], p[:, 0:wd], AF.Sigmoid, [p], [sig[d_]]))
            w = next_w()
            proj_fm(w, lambda p, t0, wd: S.act(zq[:, t0:t0 + wd], p[:, 0:wd], AF.Copy, [p], [zq], scale=128 ** -0.5))
            w = next_w()
            vt = vtok.next()
            proj_tm(w, vt)
            S.dma(spv_d[0, h].rearrange("(a p) c -> p a c", p=128), vt[:], R=[vt], W=[spv_d], owner=vt)
            kp = [kpt[0].next(), kpt[1].next()]
            dc = [dec[0].next(), dec[1].next()]
            if opts.get('upto') == 'hg_proj':
                continue
            for (t0, wd) in TG:
                nch = wd // 64
                c0 = t0 // 64
                for d_ in range(2):
                    bgstep()
                    oml = OML[:, L, d_, h:h + 1]
                    noml = NOML[:, L, d_, h:h + 1]
                    lbf = LBF[:, L, d_, h:h + 1]
                    fg = t512.next()
                    S.ts(fg[:, 0:wd], sig[d_][:, t0:t0 + wd], oml, lbf, ALU.mult, ALU.add, [sig[d_], OML, LBF], [fg])
                    kk = t512.next()
                    S.ts(kk[:, 0:wd], sig[d_][:, t0:t0 + wd], noml, oml, ALU.mult, ALU.add, [sig[d_], OML, NOML], [kk])
                    S.act(fg[:, 0:wd], fg[:, 0:wd], AF.Ln, [fg], [fg])
                    cum = t512.next()
                    for c in range(nch):
                        S.op("dve", lambda e, c=c, cum=cum, fg=fg: e.tensor_tensor_scan(
                            out=cum[:, c * 64:(c + 1) * 64], data0=ones64[:], data1=fg[:, c * 64:(c + 1) * 64],
                            initial=0.0, op0=ALU.mult, op1=ALU.add), [fg, ones64], [cum])
                    tot = cum[:, 0:wd].rearrange("p (a b) -> p a b", b=64)[:, :, 63]
                    S.act(dc[d_][:, c0:c0 + nch], tot, AF.Exp, [cum], [dc[d_]])
                    e1 = t512.next()
                    e2 = t512.next()
                    oq = o512.next()
                    ok = o512.next()
                    okp = o512.next()
                    if d_ == 0:
                        midv = cum[:, 0:wd].rearrange("p (a b) -> p a b", b=64)[:, :, 31]
                        nm = ntot.next()
                        S.ts(nm[:, 0:nch], midv, -1.0, None, ALU.mult, None, [cum], [nm])
                        S.act(dc[d_][:, NCH + c0:NCH + c0 + nch], midv, AF.Exp, [cum], [dc[d_]])
                        for c in range(nch):
                            S.act(e1[:, c * 64:(c + 1) * 64], cum[:, c * 64:(c + 1) * 64], AF.Exp, [cum, nm], [e1],
                                  bias=nm[:, c:c + 1])
                            S.act(e2[:, c * 64:(c + 1) * 64], cum[:, c * 64:(c + 1) * 64], AF.Exp, [cum], [e2],
                                  scale=-1.0, bias=cum[:, c * 64 + 31:c * 64 + 32])
                        S.tt(oq[:, 0:wd], zq[:, t0:t0 + wd], e1[:, 0:wd], ALU.mult, [zq, e1], [oq])
                        S.tt(ok[:, 0:wd], kk[:, 0:wd], e2[:, 0:wd], ALU.mult, [kk, e2], [ok])
                        e3 = t512.next()
                        for c in range(nch):
                            S.act(e3[:, c * 64:(c + 1) * 64], cum[:, c * 64:(c + 1) * 64], AF.Exp, [cum], [e3],
                                  scale=-1.0, bias=cum[:, c * 64 + 63:c * 64 + 64])
                        S.tt(okp[:, 0:wd], kk[:, 0:wd], e3[:, 0:wd], ALU.mult, [kk, e3], [okp])
                    else:
                        u = t512.next()
                        S.stt(u[:, 0:wd], cum[:, 0:wd], -1.0, fg[:, 0:wd], ALU.mult, ALU.add, [cum, fg], [u])
                        umid = u[:, 0:wd].rearrange("p (a b) -> p a b", b=64)[:, :, 32]
                        nm = ntot.next()
                        S.ts(nm[:, 0:nch], umid, -1.0, None, ALU.mult, None, [u], [nm])
                        em = ntot.next()
                        S.tt(em[:, 0:nch], umid, tot, ALU.add, [u, cum], [em])
                        S.act(dc[d_][:, NCH + c0:NCH + c0 + nch], em[:, 0:nch], AF.Exp, [em], [dc[d_]])
                        for c in range(nch):
                            S.act(e1[:, c * 64:(c + 1) * 64], u[:, c * 64:(c + 1) * 64], AF.Exp, [u, nm], [e1],
                                  bias=nm[:, c:c + 1])
                            S.act(e2[:, c * 64:(c + 1) * 64], u[:, c * 64:(c + 1) * 64], AF.Exp, [u], [e2],
                                  scale=-1.0, bias=u[:, c * 64 + 32:c * 64 + 33])
                        S.tt(oq[:, 0:wd], zq[:, t0:t0 + wd], e1[:, 0:wd], ALU.mult, [zq, e1], [oq])
                        S.tt(ok[:, 0:wd], kk[:, 0:wd], e2[:, 0:wd], ALU.mult, [kk, e2], [ok])
                        e3 = t512.next()
                        S.act(e3[:, 0:wd], u[:, 0:wd], AF.Exp, [u], [e3], scale=-1.0)
                        S.tt(okp[:, 0:wd], kk[:, 0:wd], e3[:, 0:wd], ALU.mult, [kk, e3], [okp])
                    spill_fm(K_HGQF + d_, h, oq, t0, wd)
                    spill_fm(K_HGKF + d_, h, ok, t0, wd)
                    kp_transpose(okp, t0, wd, kp[d_])
            for d_ in range(2):
                S.dma(spkp_d[0, d_, h].rearrange("(a p) c -> p a c", p=128), kp[d_][:], R=[kp[d_]], W=[spkp_d],
                      owner=kp[d_])
                S.dma(spdec_d[0, d_, h], dc[d_][:], R=[dc[d_]], W=[spdec_d], owner=dc[d_])
            if opts.get('upto') == 'hg_prep':
                continue
            run1(0, h, kp, vt, dc)
            w = next_w()
            og = o2304.next()
            proj_fm(w, lambda p, t0, wd, og=og: S.act(og[:, t0:t0 + wd], p[:, 0:wd], AF.Silu, [p], [og]))
            spill_fm(K_HGG, h, og)
            if opts.get('upto') == 'hg':
                continue
            w = next_w()
            proj_fm(w, lambda p, t0, wd: S.act(zr[:, t0:t0 + wd], p[:, 0:wd], AF.Copy, [p], [zr]))
            rope(rq, 0, 1, permrt)
            w = next_w()
            proj_fm(w, lambda p, t0, wd: S.act(zr[:, t0:t0 + wd], p[:, 0:wd], AF.Copy, [p], [zr], scale=128 ** -0.5))
            rope(rk, 0, 1, permrt)
            w = next_w()
            vt = vtok.next()
            proj_tm(w, vt)
            S.dma(spv_d[2, h].rearrange("(a p) c -> p a c", p=128), vt[:], R=[vt], W=[spv_d], owner=vt)
            kp = [kpt[0].next(), kpt[1].next()]
            dc = [dec[0].next(), dec[1].next()]
            for d_ in range(2):
                ix = li + d_ * 8 + h
                ldc, nld, ld63, ld64, nld64 = [LD[:, q, ix:ix + 1] for q in range(5)]
                if d_ == 0:
                    S.act(etab[0][:], tau[:], AF.Exp, [tau, LD], [etab[0]], scale=ldc, bias=ldc)
                    S.act(etab[1][:], tau[:], AF.Exp, [tau, LD], [etab[1]], scale=nld, bias=nld)
                    S.act(etab[2][:], tau[:], AF.Exp, [tau, LD], [etab[2]], scale=nld, bias=ld63)
                else:
                    S.act(etab[3][:], tau[:], AF.Exp, [tau, LD], [etab[3]], scale=nld, bias=ld64)
                    S.act(etab[4][:], tau[:], AF.Exp, [tau, LD], [etab[4]], scale=ldc, bias=nld64)
                    S.act(etab[5][:], tau[:], AF.Exp, [tau, LD], [etab[5]], scale=ldc)
                S.act(dc[d_][:, 0:NCH], tau[:, 0:NCH], AF.Exp, [tau, LD], [dc[d_]], scale=0.0, bias=ld64)
                S.op("pool", lambda e, d_=d_, dc=dc: e.memset(dc[d_][:, NCH:2 * NCH], 1.0), W=[dc[d_]])
                for (t0, wd) in TG:
                    bgstep()
                    oq = o512.next()
                    ok = o512.next()
                    okp = o512.next()
                    nch = wd // 64

                    def v3d(ap):
                        return ap.rearrange("p (a b) -> p a b", b=64)

                    def eb(i):
                        return etab[i][:, :].unsqueeze(1).to_broadcast([128, nch, 64])
                    S.tt(v3d(oq[:, 0:wd]), v3d(rq[:, t0:t0 + wd]), eb(3 * d_), ALU.mult, [rq, etab[3 * d_]], [oq])
                    S.tt(v3d(ok[:, 0:wd]), v3d(rk[:, t0:t0 + wd]), eb(3 * d_ + 1), ALU.mult, [rk, etab[3 * d_ + 1]], [ok])
                    S.tt(v3d(okp[:, 0:wd]), v3d(rk[:, t0:t0 + wd]), eb(3 * d_ + 2), ALU.mult, [rk, etab[3 * d_ + 2]],
                         [okp])
                    spill_fm(K_RTQF + d_, h, oq, t0, wd)
                    spill_fm(K_RTKF + d_, h, ok, t0, wd)
                    kp_transpose(okp, t0, wd, kp[d_])
                S.dma(spkp_d[1, d_, h].rearrange("(a p) c -> p a c", p=128), kp[d_][:], R=[kp[d_]], W=[spkp_d],
                      owner=kp[d_])
                S.dma(spdec_d[1, d_, h], dc[d_][:], R=[dc[d_]], W=[spdec_d], owner=dc[d_])
            run1(1, h, kp, vt, dc)
            w = next_w()
            og = o2304.next()
            proj_fm(w, lambda p, t0, wd, og=og: S.act(og[:, t0:t0 + wd], p[:, 0:wd], AF.Silu, [p], [og]))
            spill_fm(K_RTG, h, og)
            if opts.get('upto') == 'rt':
                continue
            w = next_w()
            proj_fm(w, lambda p, t0, wd: S.act(zr[:, t0:t0 + wd], p[:, 0:wd], AF.Copy, [p], [zr]))
            og = o2304.next()
            rope(og, 2, 3, permda)
            spill_fm(K_DAQ, h, og)
            w = next_w()
            proj_fm(w, lambda p, t0, wd: S.act(zr[:, t0:t0 + wd], p[:, 0:wd], AF.Copy, [p], [zr]))
            og = o2304.next()
            rope(og, 2, 3, permda)
            spill_fm(K_DAK, h, og)
            S.dma(xk_in[h // 4][(h % 4) * 128:(h % 4 + 1) * 128, :], og[:, TCX:TT], R=[og], W=[xk_in[h // 4]], owner=og)
            w = next_w()
            vt = vtok.next()
            proj_tm(w, vt)
            S.dma(spv_d[1, h].rearrange("(a p) c -> p a c", p=128), vt[:], R=[vt], W=[spv_d], owner=vt)
            S.dma(xv_in[h // 4][(h % 4) * TL:(h % 4 + 1) * TL, :].rearrange("(a p) c -> p a c", p=128), vt[:, 2:NPAIR, :],
                  R=[vt], W=[xv_in[h // 4]], owner=vt)
            w = next_w()
            og = o2304.next()
            proj_fm(w, lambda p, t0, wd, og=og: S.act(og[:, t0:t0 + wd], p[:, 0:wd], AF.Silu, [p], [og]))
            spill_fm(K_DAG, h, og)
        for _ in bg:
            pass
        RG = [[0, 1], [2, 3], [4, 5], [6, 7]]
        for (a, b, key) in (((xk_in[0], xk_out[0], "cc1"), (xk_in[1], xk_out[1], "cc1"), (xv_in[0], xv_out[0], "cc1"),
                             (xv_in[1], xv_out[1], "cc1"), (xs_in, xs_out, "cc1")) if not opts.get("no_cc") else ()):
            S.custom("pool", lambda e, a=a, b=b: e.collective_compute("AllGather", ALU.bypass, replica_groups=RG,
                                                                      ins=[a.t.ap().opt()], outs=[b.t.ap().opt()]),
                     key, 1, R=[a], W=[b])
        if opts.get("no_cc"):
            ccb = Buf("ccb")
            for (a, b) in ((xk_in[0], xk_out[0]), (xk_in[1], xk_out[1]), (xv_in[0], xv_out[0]), (xv_in[1], xv_out[1]),
                           (xs_in, xs_out)):
                n = a.t.shape[0]
                for sl in range(2):
                    S.dma(b[sl * n:(sl + 1) * n, :], a[:, :], R=[a], W=[b], owner=ccb)
        S.close()
        lstack.close()
        persist.remove(hT)
        if stopped("S2_%d" % L):
            break

        S = Phase(nc, persist)
        ctx_on = (L == 0)
        qk = [[S.sbuf("qk%d_%d" % (a, d_), [128, TT], BF16) for d_ in range(2)] for a in range(2)]
        kp3 = [S.sbuf("kp3_%d" % d_, [128, NPAIR, 128], BF16) for d_ in range(2)]
        v3 = S.sbuf("v3", [128, NPAIR, 128], BF16)
        g3 = S.sbuf("g3", [128, TT], BF16)
        dc3 = [S.sbuf("dc3_%d" % d_, [128, 2 * NCH], F32) for d_ in range(2)]
        Sall = [S.sbuf("Sall%d" % d_, [128, NCH + 1, 128], BF16) for d_ in range(2)]
        Sst = [[S.sbuf("S3st%d_%d" % (d_, i), [128, 128], F32) for i in range(2)] for d_ in range(2)]
        Rst = [S.sbuf("Rst%d" % d_, [128, 128], F32) for d_ in range(2)]
        ATf = Rot([S.sbuf("ATf%d" % i, [128, 128], BF16) for i in range(4)])
        ATb = Rot([S.sbuf("ATb%d" % i, [128, 128], BF16) for i in range(4)])
        for a_ in ATf.bufs + ATb.bufs:
            S.op("pool", lambda e, a_=a_: e.memset(a_[:], 0.0), W=[a_])
        sq = Rot([S.sbuf("sq%d" % i, [128, 512], BF16) for i in range(2)])
        f512 = Rot([S.sbuf("f512_%d" % i, [128, 512], F32) for i in range(8)])
        yo = Rot([S.sbuf("yo%d" % i, [128, 512], BF16) for i in range(3)])
        kall = S.sbuf("kall", [128, TCX + 2 * TL], BF16)
        vall = S.sbuf("vall", [128, 34, 128], BF16)
        pexp = Rot([S.sbuf("pexp%d" % i, [128, 512], BF16) for i in range(6)])
        qz = [S.sbuf("qz%d" % m, [128, TT], BF16) for m in range(2)]
        S.op("pool", lambda e: e.memset(qz[0][64:128, :], 0.0), W=[qz[0]])
        S.op("pool", lambda e: e.memset(qz[1][0:64, :], 0.0), W=[qz[1]])
        psS = Rot([S.psum("psS%d" % i, [128, 512], F32) for i in range(4)])
        psO = [S.psum("psO%d" % i, [128, 512], F32) for i in range(4)]
        tgs = TG if ctx_on else TG[1:]
        pok = [0]

        def mkset(tag):
            return dict(
                qk=[[S.sbuf("qk%s%d_%d" % (tag, a, d_), [128, TT], BF16) for d_ in range(2)] for a in range(2)],
                kp3=[S.sbuf("kp3%s_%d" % (tag, d_), [128, NPAIR, 128], BF16) for d_ in range(2)],
                v3=S.sbuf("v3%s" % tag, [128, NPAIR, 128], BF16),
                g3=S.sbuf("g3%s" % tag, [128, TT], BF16),
                dc3=[S.sbuf("dc3%s_%d" % (tag, d_), [128, 2 * NCH], F32) for d_ in range(2)],
                Sall=[S.sbuf("Sall%s%d" % (tag, d_), [128, NCH + 1, 128], BF16) for d_ in range(2)],
                Sst=[[S.sbuf("S3st%s%d_%d" % (tag, d_, i), [128, 128], F32) for i in range(2)] for d_ in range(2)],
                Rst=[S.sbuf("Rst%s%d" % (tag, d_), [128, 128], F32) for d_ in range(2)])
        sets = [dict(qk=qk, kp3=kp3, v3=v3, g3=g3, dc3=dc3, Sall=Sall, Sst=Sst, Rst=Rst), mkset("B")]
        gda = S.sbuf("gda", [128, TT], BF16)

        def gla_load(h, mix):
            T = sets[mix]
            kq = (K_HGQF, K_HGKF) if mix == 0 else (K_RTQF, K_RTKF)
            for d_ in range(2):
                S.dma(T["qk"][0][d_][:], spf_d[kq[0] + d_, h], R=[spf_d], W=[T["qk"][0][d_]], owner=T["qk"][0][d_])
                S.dma(T["qk"][1][d_][:], spf_d[kq[1] + d_, h], R=[spf_d], W=[T["qk"][1][d_]], owner=T["qk"][1][d_])
                S.dma(T["kp3"][d_][:], spkp_d[mix, d_, h].rearrange("(a p) c -> p a c", p=128), R=[spkp_d],
                      W=[T["kp3"][d_]], owner=T["kp3"][d_])
                S.dma(T["dc3"][d_][:], spdec_d[mix, d_, h], R=[spdec_d], W=[T["dc3"][d_]], owner=T["dc3"][d_])
                r0 = (((d_ * 2 + mix) * 2 + d_) * NH + h) * 128
                S.dma(T["Rst"][d_][:], xs_out[r0:r0 + 128, :], R=[xs_out], W=[T["Rst"][d_]], owner=T["Rst"][d_])
            S.dma(T["v3"][:], spv_d[0 if mix == 0 else 2, h].rearrange("(a p) c -> p a c", p=128), R=[spv_d],
                  W=[T["v3"]], owner=T["v3"])
            S.dma(T["g3"][:], spf_d[K_HGG if mix == 0 else K_RTG, h], R=[spf_d], W=[T["g3"]], owner=T["g3"])

        def gla_run2(h):
            orders = [list(range(NCH)), [3, 2, 1, 0] + list(range(NCH - 1, 3, -1))]
            cur = [[0, 0], [0, 0]]
            for mix in range(2):
                for d_ in range(2):
                    S.op("pool", lambda e, mix=mix, d_=d_: e.memset(sets[mix]["Sst"][d_][0][:], 0.0),
                         W=[sets[mix]["Sst"][d_][0]])
            for st in range(0, NCH, 4):
                for mix in range(2):
                    T = sets[mix]
                    for d_ in range(2):
                        Sst_, Rst_, dc_ = T["Sst"][d_], T["Rst"][d_], T["dc3"][d_]
                        if st == 4:
                            a, b = Sst_[cur[mix][d_]], Sst_[1 - cur[mix][d_]]
                            sel = flags[:, 0:1] if d_ == 0 else oms[:, 0:1]
                            nsel = oms[:, 0:1] if d_ == 0 else flags[:, 0:1]
                            S.ts(Rst_[:], Rst_[:], sel, None, ALU.mult, None, [Rst_, flags, oms], [Rst_])
                            S.stt(b[:], a[:], nsel, Rst_[:], ALU.mult, ALU.add, [a, Rst_, flags, oms], [b])
                            cur[mix][d_] = 1 - cur[mix][d_]
                        pp = [psS.next(), psS.next()]
                        for r in range(4):
                            j = orders[d_][st + r]
                            pr, hf = j // 2, (j % 2) * 64
                            p = pp[j % 2]
                            S.mm(p[:, r * 128:(r + 1) * 128], T["kp3"][d_][hf:hf + 64, pr, :], T["v3"][hf:hf + 64, pr, :],
                                 r < 2, True, [T["kp3"][d_], T["v3"]], [p])
                        for r in range(4):
                            j = orders[d_][st + r]
                            p = pp[j % 2]
                            a, b = Sst_[cur[mix][d_]], Sst_[1 - cur[mix][d_]]
                            S.act(T["Sall"][d_][:, j, :], a[:], AF.Identity, [a, dc_], [T["Sall"][d_]],
                                  scale=dc_[:, NCH + j:NCH + j + 1])
                            S.stt(b[:], a[:], dc_[:, j:j + 1], p[:, r * 128:(r + 1) * 128], ALU.mult, ALU.add,
                                  [a, dc_, p], [b])
                            cur[mix][d_] = 1 - cur[mix][d_]

        def gla_pass2(h, mix):
            T = sets[mix]
            qk_, v3_, g3_, Sall_ = T["qk"], T["v3"], T["g3"], T["Sall"]
            for (t0, wd) in tgs:
                po = psO[pok[0] % 4]
                pok[0] += 1
                prs = list(range(t0 // 128, (t0 + wd) // 128))
                pend = {}
                first = True
                for idx in range(len(prs) + 1):
                    if idx < len(prs):
                        c0 = prs[idx] * 128
                        ats = []
                        for d_ in range(2):
                            p = psS.next()
                            S.mm(p[:, 0:128], qk_[1][d_][:, c0:c0 + 128], qk_[0][d_][:, c0:c0 + 128], True, True,
                                 [qk_[1][d_], qk_[0][d_]], [p])
                            a = (ATf if d_ == 0 else ATb).next()
                            mk = maskF if d_ == 0 else maskB
                            S.op("dve", lambda e, a=a, p=p, mk=mk: e.copy_predicated(out=a[:], mask=mk[:], data=p[:, 0:128]),
                                 [p, mk], [a])
                            ats.append(a)
                        pend[idx] = ats
                    if idx >= 1:
                        pr = prs[idx - 1]
                        ats = pend.pop(idx - 1)
                        c0 = pr * 128
                        oc = c0 - t0
                        for d_ in range(2):
                            S.mm(po[:, oc:oc + 128], v3_[:, pr, :], ats[d_][:], first, False, [v3_, ats[d_]], [po])
                            first = False
                        for c in range(2):
                            j = pr * 2 + c
                            for d_ in range(2):
                                S.mm(po[:, oc + c * 64:oc + c * 64 + 64], Sall_[d_][:, j, :],
                                     qk_[0][d_][:, c0 + c * 64:c0 + c * 64 + 64], False, (c == 1 and d_ == 1),
                                     [Sall_[d_], qk_[0][d_]], [po])
                br = 0 if mix == 0 else 2
                s_ = sq.next()
                S.act(s_[:, 0:wd], po[:, 0:wd], AF.Square, [po], [s_])
                p = psS.next()
                S.mm(p[:, 0:wd], onesm_b[:], s_[:, 0:wd], True, True, [onesm_b, s_], [p])
                r = f512.next()
                S.act(r[:, 0:wd], p[:, 0:wd], AF.Ln, [p], [r], bias=EPS)
                S.act(r[:, 0:wd], r[:, 0:wd], AF.Exp, [r], [r], scale=-0.5)
                y = f512.next()
                S.tt(y[:, 0:wd], po[:, 0:wd], r[:, 0:wd], ALU.mult, [po, r], [y])
                o = yo.next()
                S.stt(o[:, 0:wd], y[:, 0:wd], GBR[:, L, br, h:h + 1], g3_[:, t0:t0 + wd], ALU.mult, ALU.mult,
                      [y, GBR, g3_], [o])
                S.dma(ys_d[br * 8 + h, :, t0:t0 + wd], o[:, 0:wd], R=[o], W=[ys_d], owner=o)

        def da_load(h):
            for m in range(2):
                S.dma(qz[m][m * 64:(m + 1) * 64, :], spf_d[K_DAQ, h, m * 64:(m + 1) * 64, :], R=[spf_d], W=[qz[m]],
                      owner=qz[m])
            S.dma(gda[:], spf_d[K_DAG, h], R=[spf_d], W=[gda], owner=gda)
            S.dma(kall[:, 0:TCX], spf_d[K_DAK, h, :, 0:TCX], R=[spf_d], W=[kall], owner=kall)
            for sl in range(2):
                S.dma(kall[:, TCX + sl * TL:TCX + (sl + 1) * TL],
                      xk_out[h // 4][(sl * 4 + h % 4) * 128:(sl * 4 + h % 4 + 1) * 128, :],
                      R=[xk_out[h // 4]], W=[kall], owner=kall)
                S.dma(vall[:, 2 + sl * 16:2 + (sl + 1) * 16, :],
                      xv_out[h // 4][(sl * 4 + h % 4) * TL:(sl * 4 + h % 4 + 1) * TL, :].rearrange("(a p) c -> p a c", p=128),
                      R=[xv_out[h // 4]], W=[vall], owner=vall)
            S.dma(vall[:, 0:2, :], spv_d[1, h, 0:TCX, :].rearrange("(a p) c -> p a c", p=128), R=[spv_d], W=[vall],
                  owner=vall)

        gla_load(0, 0)
        gla_load(0, 1)
        for h in range(NH):
            gla_run2(h)
            da_load(h)
            gla_pass2(h, 0)
            gla_pass2(h, 1)
            if h + 1 < NH:
                gla_load(h + 1, 0)
                gla_load(h + 1, 1)
            for (t0, wd) in tgs:
                nkt = 2 if t0 == 0 else 34
                steps = [(kt, m) for kt in range(nkt) for m in range(2)]
                LOOK = 3
                pend = {}
                for i in range(len(steps) + LOOK):
                    if i < len(steps):
                        kt, m = steps[i]
                        p = psS.next()
                        S.mm(p[:, 0:wd], kall[:, kt * 128:(kt + 1) * 128], qz[m][:, t0:t0 + wd], True, True,
                             [kall, qz[m]], [p])
                        e = pexp.next()
                        S.act(e[:, 0:wd], p[:, 0:wd], AF.Exp, [p], [e], scale=0.125)
                        pend[i] = e
                    if i >= LOOK:
                        kt, m = steps[i - LOOK]
                        e = pend.pop(i - LOOK)
                        S.mm(psO[m][:, 0:wd], vall[:, kt, :], e[:, 0:wd], kt == 0, kt == nkt - 1, [vall, e], [psO[m]])
                        S.mm(psO[2 + m][:, 0:wd], ones_b[:], e[:, 0:wd], kt == 0, kt == nkt - 1, [ones_b, e], [psO[2 + m]])
                r1 = f512.next()
                S.act(r1[:, 0:wd], psO[2][:, 0:wd], AF.Ln, [psO[2]], [r1])
                S.act(r1[:, 0:wd], r1[:, 0:wd], AF.Exp, [r1], [r1], scale=-1.0)
                r2 = f512.next()
                S.act(r2[:, 0:wd], psO[3][:, 0:wd], AF.Ln, [psO[3]], [r2])
                S.act(r2[:, 0:wd], r2[:, 0:wd], AF.Exp, [r2], [r2], scale=-1.0)
                S.tt(r1[:, 0:wd], psO[0][:, 0:wd], r1[:, 0:wd], ALU.mult, [psO[0], r1], [r1])
                S.tt(r2[:, 0:wd], psO[1][:, 0:wd], r2[:, 0:wd], ALU.mult, [psO[1], r2], [r2])
                oo = f512.next()
                S.stt(oo[:, 0:wd], r2[:, 0:wd], NLAM[:, L:L + 1], r1[:, 0:wd], ALU.mult, ALU.add, [r1, r2, NLAM], [oo])
                s = sq.next()
                S.act(s[:, 0:wd], oo[:, 0:wd], AF.Square, [oo], [s])
                p = psS.next()
                S.mm(p[:, 0:wd], onesm_b[:], s[:, 0:wd], True, True, [onesm_b, s], [p])
                r = f512.next()
                S.act(r[:, 0:wd], p[:, 0:wd], AF.Ln, [p], [r], bias=EPS)
                S.act(r[:, 0:wd], r[:, 0:wd], AF.Exp, [r], [r], scale=-0.5)
                S.tt(oo[:, 0:wd], oo[:, 0:wd], r[:, 0:wd], ALU.mult, [oo, r], [oo])
                o = yo.next()
                S.stt(o[:, 0:wd], oo[:, 0:wd], GBR[:, L, 1, h:h + 1], gda[:, t0:t0 + wd], ALU.mult, ALU.mult,
                      [oo, GBR, gda], [o])
                S.dma(ys_d[8 + h, :, t0:t0 + wd], o[:, 0:wd], R=[o], W=[ys_d], owner=o)
        S.close()
        if stopped("S3_%d" % L):
            break

        S = Phase(nc, persist)
        wstg = Rot([S.sbuf("wstg%d" % i, [128, 8, 128], F32) for i in range(2)])
        wcb = Rot([S.sbuf("wcb%d" % i, [128, 8, 128], BF16) for i in range(2)])
        wob = S.sbuf("wob", [128, KC, D], BF16)
        wos = Rot([S.sbuf("wos%d" % i, [128, D], F32) for i in range(1)])
        for br in range(3):
            for fc in range(KC):
                a = wstg.next()
                S.dma(a[:], wbr_d[L, br, :, fc * 128:(fc + 1) * 128].rearrange("(k p) c -> p k c", p=128),
                      R=[wbr_d], W=[a], owner=a)
                b = wcb.next()
                if (br * KC + fc) % 2 == 0:
                    S.act(b[:], a[:], AF.Copy, [a], [b])
                else:
                    S.copy(b[:], a[:], [a], [b])
                S.dma(wbrb_d[br, fc], b[:].rearrange("p a b -> p (a b)"), R=[b], W=[wbrb_d], owner=b)
        for kc in range(KC):
            a = wos.next()
            S.dma(a[:], wout_d[L, kc * 128:(kc + 1) * 128, :], R=[wout_d], W=[a], owner=a)
            if kc % 2:
                S.act(wob[:, kc, :], a[:], AF.Copy, [a], [wob])
            else:
                S.copy(wob[:, kc, :], a[:], [a], [wob])
        GGb = [S.sbuf("GGb%d" % j, [128, D], F32) for j in range(2)]
        onesf4 = S.sbuf("onesf4", [128, 128], F32)
        S.op("pool", lambda e: e.memset(onesf4[:], 1.0), W=[onesf4])
        dg = Rot([S.sbuf("dg%d" % i, [128, 128], F32) for i in range(2)])
        ps4 = Rot([S.psum("ps4_%d" % i, [128, 512], F32) for i in range(8)])
        for j in range(2):
            for q in range(4):
                p = ps4.next()
                for r in range(4):
                    kc = q * 4 + r
                    dgt = dg.next()
                    S.ts(dgt[:], identf[:], GGvec[:, L, j, kc:kc + 1], None, ALU.mult, None, [identf, GGvec], [dgt])
                    S.mm(p[:, r * 128:(r + 1) * 128], onesf4[:], dgt[:], r == 0, True, [dgt, onesf4], [p])
                S.copy(GGb[j][:, q * 512:(q + 1) * 512], p[:], [p], [GGb[j]])
        ysT = Rot([S.sbuf("ysT%d" % i, [128, 24, 512], BF16) for i in range(1)])
        gmt = Rot([S.sbuf("gmt%d" % i, [128, 3, 512], BF16) for i in range(3)])
        wbt = Rot([S.sbuf("wbt%d" % i, [128, 3, 8, 128], BF16) for i in range(2)])
        mrg = Rot([S.sbuf("mrg%d" % i, [128, KC, 512], BF16) for i in range(1)])
        m512 = Rot([S.sbuf("m512_%d" % i, [128, 512], F32) for i in range(4)])
        xres = Rot([S.sbuf("xres%d" % i, [128, D], F32) for i in range(1)])
        yout = Rot([S.sbuf("yout%d" % i, [128, D], F32) for i in range(2)])
        ss4 = Rot([S.sbuf("ss4_%d" % i, [128, 4], F32) for i in range(2)])
        junk4 = S.sbuf("junk4", [128, 512], BF16)
        for (t0, wd) in (TG if ctx_on else TG[1:]):
            if opts.get('upto') == 's4_prep':
                break
            if opts.get('upto') in ('s4_proj', 's4_tile') and t0 > 0:
                break
            j = 1 if t0 == 0 else 0
            yt = ysT.next()
            for br in range(3):
                S.dma(yt[:, br * 8:(br + 1) * 8, 0:wd], ys_d[br * 8:(br + 1) * 8, :, t0:t0 + wd].rearrange("a p t -> p a t"),
                      R=[ys_d], W=[yt], owner=yt)
            mg = mrg.next()
            for fc in range(KC):
                wb = wbt.next()
                S.dma(wb[:].rearrange("p a b c -> p a (b c)"), wbrb_d[:, fc].rearrange("a p x -> p a x"), R=[wbrb_d],
                      W=[wb], owner=wb)
                gt = gmt.next()
                S.dma(gt[:, :, 0:wd], gm_d[:, :, t0:t0 + wd].rearrange("(a f) p t -> f p a t", f=KC)[fc],
                      R=[gm_d], W=[gt], owner=gt)
                for br in range(3):
                    S.act(gt[:, br, 0:wd], gt[:, br, 0:wd], AF.Sigmoid, [gt], [gt])
                ms = []
                for br in range(3):
                    p = ps4.next()
                    for k8 in range(8):
                        S.mm(p[:, 0:wd], wb[:, br, k8, :], yt[:, br * 8 + k8, 0:wd], k8 == 0, k8 == 7, [wb, yt], [p])
                    m = m512.next()
                    S.tt(m[:, 0:wd], p[:, 0:wd], gt[:, br, 0:wd], ALU.mult, [p, gt], [m])
                    ms.append(m)
                S.tt(ms[0][:, 0:wd], ms[0][:, 0:wd], ms[1][:, 0:wd], ALU.add, [ms[0], ms[1]], [ms[0]])
                S.tt(mg[:, fc, 0:wd], ms[0][:, 0:wd], ms[2][:, 0:wd], ALU.add, [ms[0], ms[2]], [mg])
            for ts_ in range(wd // 128):
                if opts.get('upto') == 's4_proj':
                    break
                tok0 = t0 + ts_ * 128
                xr = xres.next()
                if tok0 < TCX:
                    S.dma(xr[:], csrc[tok0:tok0 + 128, :], R=[csrc], W=[xr], owner=xr)
                else:
                    S.dma(xr[:], xsrc[tok0 - TCX:tok0 - TCX + 128, :], R=[xsrc], W=[xr], owner=xr)
                yo_ = yout.next()
                s4 = ss4.next()
                for q in range(4):
                    p = ps4.next()
                    for kc in range(KC):
                        S.mm(p[:], mg[:, kc, ts_ * 128:(ts_ + 1) * 128], wob[:, kc, q * 512:(q + 1) * 512], kc == 0,
                             kc == KC - 1, [mg, wob], [p])
                    S.copy(yo_[:, q * 512:(q + 1) * 512], p[:], [p], [yo_])
                    S.act(junk4[:], yo_[:, q * 512:(q + 1) * 512], AF.Square, [yo_], [junk4, s4], accum=s4[:, q:q + 1])
                S.op("dve", lambda e, s4=s4: e.reduce_sum(out=s4[:, 0:1], in_=s4[:, 0:4], axis=AX.X), [s4], [s4])
                S.ts(s4[:, 0:1], s4[:, 0:1], 1.0 / D, EPS, ALU.mult, ALU.add, [s4], [s4])
                S.act(s4[:, 0:1], s4[:, 0:1], AF.Sqrt, [s4], [s4])
                S.op("dve", lambda e, s4=s4: e.reciprocal(out=s4[:, 0:1], in_=s4[:, 0:1]), [s4], [s4])
                S.stt(yo_[:], yo_[:], s4[:, 0:1], GGb[j][:], ALU.mult, ALU.mult, [yo_, s4, GGb[j]], [yo_])
                S.tt(yo_[:], yo_[:], xr[:], ALU.add, [yo_, xr], [yo_])
                if tok0 < TCX:
                    dst, r0 = c1_d, tok0
                else:
                    dst, r0 = (x1_d if L == 0 else out_d), tok0 - TCX
                S.dma(dst[r0:r0 + 128, :], yo_[:], R=[yo_], W=[dst], owner=yo_)
        S.close()
        if stopped("S4_%d" % L):
            break
    pstack.close()
    return nc


def _rope_tables(half):
    f32 = np.float32
    pos_rt = np.concatenate([np.arange(TCX, dtype=f32), TCX + half * TL + np.arange(TL, dtype=f32)])
    inv = (10000.0 ** (-np.arange(0, 128, 2, dtype=f32) / 128)).astype(f32)
    ang = pos_rt[None, :] * np.concatenate([inv, inv])[:, None]
    rt_cos = np.cos(ang)
    rt_sin = np.sin(ang) * np.concatenate([-np.ones(64), np.ones(64)])[:, None]
    n = half * TL + np.arange(TL)
    row = (n // 64).astype(f32)
    col = (n % 64).astype(f32)
    inv_a = (10000.0 ** (-np.arange(0, 32, 2, dtype=f32) / 32)).astype(f32)
    da_cos = np.ones((128, TT), f32)
    da_sin = np.zeros((128, TT), f32)
    for i in range(128):
        d = i % 64
        pos = row if d < 32 else col
        dd = d % 32
        a = pos * inv_a[dd % 16]
        da_cos[i, TCX:] = np.cos(a)
        da_sin[i, TCX:] = np.sin(a) * (-1.0 if dd < 16 else 1.0)
    return np.stack([rt_cos, rt_sin, da_cos, da_sin]).astype(f32).astype(ml_dtypes.bfloat16)


def _consts():
    bf = ml_dtypes.bfloat16
    ident = np.eye(128, dtype=np.float32)
    s = np.arange(128)[:, None]
    t = np.arange(128)[None, :]
    same = (s // 64) == (t // 64)
    maskF = (same & (s <= t)).astype(np.int32)
    maskB = (same & (s >= t)).astype(np.int32)
    perm_rt = np.zeros((128, 128), np.float32)
    perm_da = np.zeros((128, 128), np.float32)
    for m in range(128):
        perm_rt[(m + 64) % 128, m] = 1.0
        dd = m % 32
        partner = m + 16 if dd < 16 else m - 16
        perm_da[partner, m] = 1.0
    tau = np.tile((np.arange(512) % 64).astype(np.float32)[None, :], (128, 1))
    return dict(ident=ident.astype(bf), identf=ident, maskF=maskF, maskB=maskB, perm_rt=perm_rt.astype(bf),
                perm_da=perm_da.astype(bf), tau=tau)


def _fm(v, nchunk):
    v = np.asarray(v, np.float32)
    lead = v.shape[:-1]
    r = v.reshape(lead + (nchunk, 128))
    r = np.moveaxis(r, -1, 0)
    return np.ascontiguousarray(r)


_NC_CACHE = {}


def make_in_maps(x, c, ctx, c_ctx, w_mod, b_mod, g_pre, g_post, w_in, hg_lower, ret_decay, diff_lambda, g_branch,
                 w_branch, w_out):
    f32 = np.float32
    consts = _consts()
    shared = dict(
        w_mod=np.ascontiguousarray(w_mod, f32), w_in=np.ascontiguousarray(w_in, f32),
        w_branch=np.ascontiguousarray(w_branch, f32), w_out=np.ascontiguousarray(w_out, f32),
        bmod_fm=_fm(b_mod, 48).reshape(128, 96), gpre_fm=_fm(g_pre, KC).reshape(128, 32),
        gpost_fm=_fm(g_post, KC).reshape(128, 32),
        hgl_fm=_fm(hg_lower, NH).reshape(128, 32),
        rdec_b=np.ascontiguousarray(np.broadcast_to(np.asarray(ret_decay, f32).reshape(1, 32), (128, 32))),
        dlam_b=np.ascontiguousarray(np.broadcast_to(np.asarray(diff_lambda, f32).reshape(1, 512), (128, 512))),
        gbr_fm=_fm(g_branch, NH).reshape(128, 48), **consts)
    tabs = [_rope_tables(0), _rope_tables(1)]
    in_maps = []
    for core in range(8):
        b, half = core // 2, core % 2
        cv = np.stack([np.asarray(c[b], f32), np.asarray(c_ctx, f32)], axis=-1)
        cv = np.ascontiguousarray(cv.reshape(KC, 128, 2).transpose(1, 0, 2)).reshape(128, 32)
        fl = np.zeros((128, 2), f32)
        fl[:, 0] = half
        fl[:, 1] = half
        m = dict(shared)
        m.update(x=np.ascontiguousarray(x[b, half * TL:(half + 1) * TL], f32), ctx=np.ascontiguousarray(ctx[b], f32),
                 cvec=cv, tabs=tabs[half], flags=fl)
        in_maps.append(m)
    return in_maps


def kernel(x, c, ctx, c_ctx, w_mod, b_mod, g_pre, g_post, w_in, hg_lower, ret_decay, diff_lambda, g_branch,
           w_branch, w_out):
    in_maps = make_in_maps(x, c, ctx, c_ctx, w_mod, b_mod, g_pre, g_post, w_in, hg_lower, ret_decay, diff_lambda,
                           g_branch, w_branch, w_out)
    if "nc" not in _NC_CACHE:
        _NC_CACHE["nc"] = build_program()
    res = run_bass_kernel_spmd(_NC_CACHE["nc"], in_maps, core_ids=list(range(8)))
    out = np.empty((4, 4096, D), np.float32)
    for core in range(8):
        b, half = core // 2, core % 2
        out[b, half * TL:(half + 1) * TL] = res.results[core]["out"]
    return out
```

```python
import contextlib
import math
import numpy as np
import ml_dtypes
import concourse.bass as bass
import concourse.mybir as mybir
from concourse.bass_utils import run_bass_kernel_spmd

F32 = mybir.dt.float32
BF16 = mybir.dt.bfloat16
I32 = mybir.dt.int32
AF = mybir.ActivationFunctionType
ALU = mybir.AluOpType
AX = mybir.AxisListType

D = 2048
KC = 16
TCX = 256
TL = 2048
TT = 2304
NPAIR = 18
NCH = 36
NH = 8
IN_COLS = 19456
(HG_Q, HG_FF, HG_FB, HG_I, HG_G, DA_Q, DA_K, DA_V, DA_G, RT_Q, RT_K, RT_V, RT_G) = range(13)
MERGE0 = 13 * 1024
LAM_INIT = [0.8 - 0.6 * math.exp(-0.3 * l) for l in range(2)]
EPS = 1e-6
TG = [(0, 256), (256, 512), (768, 512), (1280, 512), (1792, 512)]
SAME_ENGINE_WAITS = True
NO_SELF_WAIT = ("pe", "act")
(K_HGQF, K_HGQB, K_HGKF, K_HGKB, K_HGG, K_RTQF, K_RTQB, K_RTKF, K_RTKB, K_RTG, K_DAQ, K_DAG, K_DAK) = range(13)


class Buf:
    __slots__ = ("name", "t", "writers", "readers", "dirty_read", "semkey", "is_dram")

    def __init__(self, name, t=None, is_dram=False):
        self.name = name
        self.t = t
        self.is_dram = is_dram
        self.writers = {}
        self.readers = {}
        self.dirty_read = False
        self.semkey = "d_" + name

    def reset(self):
        self.writers = {}
        self.readers = {}
        self.dirty_read = False

    def __getitem__(self, idx):
        return self.t[idx]


_UID = [0]


class Phase:
    ENGINES = ("pe", "act", "dve", "pool", "sp")

    def __init__(self, nc, persist):
        self.nc = nc
        self.persist = persist
        self.stack = contextlib.ExitStack()
        self.streams = {e: [] for e in self.ENGINES}
        self.semcnt = {}
        self.semh = {}
        self.waited = {e: {} for e in self.ENGINES}
        for e in ("pe", "act", "dve", "pool"):
            self.new_sem(e)
        self.n = 0

    def new_sem(self, key):
        self.semcnt[key] = 0
        _UID[0] += 1
        self.semh[key] = self.nc.alloc_semaphore(name="s%d_%s" % (_UID[0], key))

    def sbuf(self, name, shape, dtype):
        _UID[0] += 1
        name = "%s_u%d" % (name, _UID[0])
        t = self.stack.enter_context(self.nc.sbuf_tensor(name, list(shape), dtype))
        return Buf(name, t)

    def psum(self, name, shape, dtype):
        _UID[0] += 1
        name = "%s_u%d" % (name, _UID[0])
        t = self.stack.enter_context(self.nc.psum_tensor(name, list(shape), dtype))
        return Buf(name, t)

    def _collect(self, eng, reads, writes):
        deps = {}

        def add(d):
            for k, v in d.items():
                if deps.get(k, 0) < v:
                    deps[k] = v
        for r in reads:
            add(r.writers)
        for w in writes:
            add(w.readers)
            if not w.is_dram:
                add(w.writers)
        waits = []
        wd = self.waited[eng]
        for k, v in deps.items():
            if k == eng and (eng in NO_SELF_WAIT or not SAME_ENGINE_WAITS):
                continue
            if wd.get(k, 0) < v:
                wd[k] = v
                waits.append((k, v))
        return waits

    def _commit(self, ev, reads, writes):
        k, v = ev
        for r in reads:
            if r.readers.get(k, 0) < v:
                r.readers[k] = v
            r.dirty_read = True
        for w in writes:
            if w.dirty_read:
                w.writers = {}
                w.readers = {}
                w.dirty_read = False
            if w.writers.get(k, 0) < v:
                w.writers[k] = v

    def op(self, eng, fn, R=(), W=()):
        waits = self._collect(eng, R, W)
        self.semcnt[eng] += 1
        self._commit((eng, self.semcnt[eng]), R, W)
        self.streams[eng].append((waits, fn, (eng, 1)))

    def dma(self, out, in_, R=(), W=(), owner=None, q="sp"):
        key = owner.semkey
        for w in W:
            if w.is_dram:
                key = key + "_" + w.name
        if key not in self.semh:
            self.new_sem(key)
        waits = self._collect(q, R, W)
        self.semcnt[key] += 16
        self._commit((key, self.semcnt[key]), R, W)
        self.streams[q].append((waits, lambda e: e.dma_start(out=out, in_=in_), (key, 16)))

    def custom(self, eng, fn, key, inc, R=(), W=()):
        if key not in self.semh:
            self.new_sem(key)
        waits = self._collect(eng, R, W)
        self.semcnt[key] += inc
        self._commit((key, self.semcnt[key]), R, W)
        self.streams[eng].append((waits, fn, (key, inc)))

    def close(self):
        nc = self.nc
        semh = self.semh
        streams = self.streams
        final = [(k, v) for k, v in self.semcnt.items() if v > 0]
        for eng in ("sp", "pool"):
            streams[eng].append(([(k, v) for k, v in final if self.waited[eng].get(k, 0) < v], None, None))

        def run(engobj, lst):
            for waits, fn, inc in lst:
                for k, v in waits:
                    engobj.wait_ge(semh[k], v)
                if fn is not None:
                    fn(engobj).then_inc(semh[inc[0]], inc[1])

        with nc.Block() as block:
            @block.tensor
            def _(e):
                run(e, streams["pe"])

            @block.scalar
            def _(e):
                run(e, streams["act"])

            @block.vector
            def _(e):
                run(e, streams["dve"])

            @block.gpsimd
            def _(e):
                run(e, streams["pool"])

            @block.sync
            def _(e):
                run(e, streams["sp"])
        nc.all_engine_barrier()
        nc.clear_and_free_semaphores(list(self.semh.values()))
        nc.all_engine_barrier()
        self.stack.close()
        for b in self.persist:
            b.reset()

    def act(self, out, in_, func, R, W, scale=1.0, bias=0.0, accum=None):
        if accum is None:
            self.op("act", lambda e: e.activation(out=out, in_=in_, func=func, bias=bias, scale=scale), R, W)
        else:
            self.op("act", lambda e: e.activation(out=out, in_=in_, func=func, bias=bias, scale=scale,
                                                  accum_out=accum), R, W)

    def ts(self, out, in0, s1, s2, op0, op1, R, W, eng="dve"):
        if s2 is None:
            self.op(eng, lambda e: e.tensor_scalar(out=out, in0=in0, scalar1=s1, scalar2=None, op0=op0), R, W)
        else:
            self.op(eng, lambda e: e.tensor_scalar(out=out, in0=in0, scalar1=s1, scalar2=s2, op0=op0, op1=op1), R, W)

    def tt(self, out, in0, in1, op, R, W, eng="dve"):
        self.op(eng, lambda e: e.tensor_tensor(out=out, in0=in0, in1=in1, op=op), R, W)

    def stt(self, out, in0, scalar, in1, op0, op1, R, W):
        self.op("dve", lambda e: e.scalar_tensor_tensor(out=out, in0=in0, scalar=scalar, in1=in1, op0=op0, op1=op1),
                R, W)

    def copy(self, out, in_, R, W, eng="dve"):
        self.op(eng, lambda e: e.tensor_copy(out=out, in_=in_), R, W)

    def mm(self, out, lhsT, rhs, start, stop, R, W):
        self.op("pe", lambda e: e.matmul(out=out, lhsT=lhsT, rhs=rhs, start=start, stop=stop, skip_group_check=True),
                R, W)

    def tr(self, out, in_, ident, R, W):
        self.op("pe", lambda e: e.transpose(out=out, in_=in_, identity=ident), R, W)


class Rot:
    def __init__(self, bufs):
        self.bufs = bufs
        self.i = 0

    def next(self):
        b = self.bufs[self.i % len(self.bufs)]
        self.i += 1
        return b


def build_program(stop_after=None, dbg=False, opts=None):
    opts = opts or {}
    NHD = opts.get('heads', NH)
    SEC = opts.get('sections', 'hg,rt,da,mg')
    nc = bass.Bass("TRN2", target_bir_lowering=False)
    persist = []
    pstack = contextlib.ExitStack()

    def dram(name, shape, dtype, kind="Internal"):
        if dbg and kind == "Internal" and name in opts.get("dump", ()):
            kind = "ExternalOutput"
        b = Buf(name, nc.dram_tensor(name, list(shape), dtype, kind=kind), is_dram=True)
        persist.append(b)
        return b

    def psb(name, shape, dtype):
        b = Buf(name, pstack.enter_context(nc.sbuf_tensor(name, list(shape), dtype)))
        persist.append(b)
        return b

    EI = "ExternalInput"
    x_in = dram("x", [TL, D], F32, EI)
    ctx_in = dram("ctx", [TCX, D], F32, EI)
    cvec_d = dram("cvec", [128, KC * 2], F32, EI)
    wmod_d = dram("w_mod", [2, D, 3 * D], F32, EI)
    bmod_d = dram("bmod_fm", [128, 2 * 48], F32, EI)
    gpre_d = dram("gpre_fm", [128, 2 * KC], F32, EI)
    gpost_d = dram("gpost_fm", [128, 2 * KC], F32, EI)
    win_d = dram("w_in", [2, D, IN_COLS], F32, EI)
    hgl_d = dram("hgl_fm", [128, 2 * 2 * NH], F32, EI)
    rdec_d = dram("rdec_b", [128, 32], F32, EI)
    dlam_d = dram("dlam_b", [128, 512], F32, EI)
    gbr_d = dram("gbr_fm", [128, 2 * 3 * NH], F32, EI)
    wbr_d = dram("w_branch", [2, 3, 1024, D], F32, EI)
    wout_d = dram("w_out", [2, D, D], F32, EI)
    ident_d = dram("ident", [128, 128], BF16, EI)
    identf_d = dram("identf", [128, 128], F32, EI)
    maskf_d = dram("maskF", [128, 128], I32, EI)
    maskb_d = dram("maskB", [128, 128], I32, EI)
    permrt_d = dram("perm_rt", [128, 128], BF16, EI)
    permda_d = dram("perm_da", [128, 128], BF16, EI)
    tabs_d = dram("tabs", [4, 128, TT], BF16, EI)
    tau_d = dram("tau", [128, 512], F32, EI)
    flags_d = dram("flags", [128, 2], F32, EI)
    out_d = dram("out", [TL, D], F32, "ExternalOutput")

    x1_d = dram("x1", [TL, D], F32)
    c1_d = dram("c1", [TCX, D], F32)
    spf_d = dram("spf", [13, NH, 128, TT], BF16)
    spkp_d = dram("spkp", [2, 2, NH, TT, 128], BF16)
    spv_d = dram("spv", [3, NH, TT, 128], BF16)
    spdec_d = dram("spdec", [2, 2, NH, 128, 2 * NCH], F32)
    gm_d = dram("gm", [48, 128, TT], BF16)
    ys_d = dram("ys", [24, 128, TT], BF16)
    xk_in = [dram("xk_in%d" % i, [4 * 128, TL], BF16) for i in range(2)]
    xk_out = [dram("xk_out%d" % i, [2 * 4 * 128, TL], BF16) for i in range(2)]
    xv_in = [dram("xv_in%d" % i, [4 * TL, 128], BF16) for i in range(2)]
    xv_out = [dram("xv_out%d" % i, [2 * 4 * TL, 128], BF16) for i in range(2)]
    xs_in = dram("xs_in", [2 * 2 * NH * 128, 128], F32)
    xs_out = dram("xs_out", [2 * 2 * 2 * NH * 128, 128], F32)
    wbrb_d = dram("wbrb", [3, KC, 128, 8 * 128], BF16)
    woutb_d = dram("woutb", [128, KC * D], BF16)
    dbg_d = dram("dbg", [128, TT], F32, "ExternalOutput") if dbg else None

    ident = psb("ident_s", [128, 128], BF16)
    identf = psb("identf_s", [128, 128], F32)
    maskF = psb("maskF_s", [128, 128], I32)
    maskB = psb("maskB_s", [128, 128], I32)
    ones_b = psb("ones_b", [128, 128], BF16)
    onesm_b = psb("onesm_b", [128, 128], BF16)
    flags = psb("flags_s", [128, 2], F32)
    oms = psb("oms", [128, 2], F32)
    Avec = psb("Avec", [128, 2, 2, KC], F32)
    Bvec = psb("Bvec", [128, 2, 2, KC], F32)
    GGvec = psb("GGvec", [128, 2, 2, KC], F32)
    LBF = psb("LBF", [128, 2, 2, NH], F32)
    OML = psb("OML", [128, 2, 2, NH], F32)
    NOML = psb("NOML", [128, 2, 2, NH], F32)
    LD = psb("LD", [128, 5, 32], F32)
    NLAM = psb("NLAM", [128, 2], F32)
    GBR = psb("GBR", [128, 2, 3, NH], F32)

    def stopped(tag):
        return stop_after is not None and tag == stop_after

    S = Phase(nc, persist)
    cv = S.sbuf("cv", [128, KC, 2], F32)
    cact = S.sbuf("cact", [128, KC, 2], F32)
    bm = S.sbuf("bm", [128, 2, 48], F32)
    gpre = S.sbuf("gpre", [128, 2, KC], F32)
    gpost = S.sbuf("gpost", [128, 2, KC], F32)
    hgl = S.sbuf("hgl", [128, 2, 2, NH], F32)
    rdec = S.sbuf("rdec", [128, 32], F32)
    dlam = S.sbuf("dlam", [128, 2, 4, 64], F32)
    tmp0 = S.sbuf("tmp0", [128, 64], F32)
    sm0 = S.sbuf("sm0", [128, 4], F32)
    modT = S.sbuf("modT", [128, 2, 48, 2], F32)
    wm = Rot([S.sbuf("wm%d" % i, [128, 3 * D], F32) for i in range(2)])
    psm = S.psum("psm", [128, 512], F32)

    def ld(dst, src_ap, srcb, dst_ap=None):
        S.dma(dst[:] if dst_ap is None else dst_ap, src_ap, R=[srcb], W=[dst], owner=dst)
    ld(ident, ident_d[:, :], ident_d)
    ld(identf, identf_d[:, :], identf_d)
    ld(maskF, maskf_d[:, :], maskf_d)
    ld(maskB, maskb_d[:, :], maskb_d)
    ld(flags, flags_d[:, :], flags_d)
    ld(cv, cvec_d[:, :], cvec_d, cv[:].rearrange("p a b -> p (a b)"))
    ld(bm, bmod_d[:, :], bmod_d, bm[:].rearrange("p a b -> p (a b)"))
    ld(gpre, gpre_d[:, :], gpre_d, gpre[:].rearrange("p a b -> p (a b)"))
    ld(gpost, gpost_d[:, :], gpost_d, gpost[:].rearrange("p a b -> p (a b)"))
    ld(hgl, hgl_d[:, :], hgl_d, hgl[:].rearrange("p a b c -> p (a b c)"))
    ld(rdec, rdec_d[:, :], rdec_d)
    ld(dlam, dlam_d[:, :], dlam_d, dlam[:].rearrange("p a b c -> p (a b c)"))
    ld(GBR, gbr_d[:, :], gbr_d, GBR[:].rearrange("p a b c -> p (a b c)"))
    S.op("pool", lambda e: e.memset(ones_b[:], 1.0), W=[ones_b])
    S.op("pool", lambda e: e.memset(onesm_b[:], 1.0 / 128), W=[onesm_b])
    S.ts(oms[:], flags[:], -1.0, 1.0, ALU.mult, ALU.add, [flags], [oms])
    S.act(cact[:], cv[:], AF.Sigmoid, [cv], [cact])
    S.tt(cact[:], cact[:], cv[:], ALU.mult, [cv, cact], [cact])
    S.op("pool", lambda e: e.memset(LBF[:, 0], 1e-20), W=[LBF])
    S.op("pool", lambda e: e.memset(OML[:, 0], 1.0), W=[OML])
    S.tt(OML[:, 1], hgl[:, :, 1, :], hgl[:, :, 0, :], ALU.subtract, [hgl], [OML])
    S.act(LBF[:, 1], OML[:, 1], AF.Sigmoid, [OML], [LBF])
    S.ts(OML[:, 1], LBF[:, 1], -1.0, 1.0, ALU.mult, ALU.add, [LBF], [OML])
    S.ts(LBF[:, 1], LBF[:, 1], 1e-20, None, ALU.max, None, [LBF], [LBF])
    S.ts(NOML[:], OML[:], -1.0, None, ALU.mult, None, [OML], [NOML])
    S.act(LD[:, 1, :], rdec[:], AF.Exp, [rdec], [LD], scale=-1.0)
    S.act(LD[:, 1, :], LD[:, 1, :], AF.Ln, [LD], [LD], bias=1.0)
    S.ts(LD[:, 0, :], LD[:, 1, :], -1.0, None, ALU.mult, None, [LD], [LD])
    S.ts(LD[:, 2, :], LD[:, 0, :], 63.0, None, ALU.mult, None, [LD], [LD])
    S.ts(LD[:, 3, :], LD[:, 0, :], 64.0, None, ALU.mult, None, [LD], [LD])
    S.ts(LD[:, 4, :], LD[:, 0, :], -64.0, None, ALU.mult, None, [LD], [LD])
    for L in range(2):
        for j in range(2):
            S.tt(tmp0[:], dlam[:, L, 2 * j, :], dlam[:, L, 2 * j + 1, :], ALU.mult, [dlam], [tmp0])
            S.op("dve", lambda e, L=L, j=j: e.reduce_sum(out=sm0[:, 2 * L + j:2 * L + j + 1], in_=tmp0[:], axis=AX.X),
                 [tmp0], [sm0])
    S.act(sm0[:], sm0[:], AF.Exp, [sm0], [sm0])
    for L in range(2):
        S.tt(NLAM[:, L:L + 1], sm0[:, 2 * L + 1:2 * L + 2], sm0[:, 2 * L:2 * L + 1], ALU.subtract, [sm0], [NLAM])
        S.ts(NLAM[:, L:L + 1], NLAM[:, L:L + 1], -LAM_INIT[L], None, ALU.add, None, [NLAM], [NLAM])
        S.ts(GBR[:, L, 1, :], GBR[:, L, 1, :], 1.0 - LAM_INIT[L], None, ALU.mult, None, [GBR], [GBR])
    for L in range(2):
        for kc in range(KC):
            w = wm.next()
            S.dma(w[:, 0:3072], wmod_d[L, kc * 128:(kc + 1) * 128, 0:3072], R=[wmod_d], W=[w], owner=w)
            S.dma(w[:, 3072:6144], wmod_d[L, kc * 128:(kc + 1) * 128, 3072:6144], R=[wmod_d], W=[w], owner=w)
            for cc in range(48):
                S.mm(psm[:, 2 * cc:2 * cc + 2], w[:, cc * 128:(cc + 1) * 128], cact[:, kc, :],
                     (kc == 0 and cc == 0), (kc == KC - 1), [w, cact], [psm])
        for j in range(2):
            S.tt(modT[:, L, :, j], psm[:, 0:96].rearrange("p (a b) -> p a b", b=2)[:, :, j], bm[:, L, :], ALU.add,
                 [psm, bm], [modT])
        for j in range(2):
            S.stt(Avec[:, L, j, :], modT[:, L, 16:32, j], 1.0, gpre[:, L, :], ALU.add, ALU.mult, [modT, gpre], [Avec])
            S.copy(Bvec[:, L, j, :], modT[:, L, 0:16, j], [modT], [Bvec])
            S.tt(GGvec[:, L, j, :], modT[:, L, 32:48, j], gpost[:, L, :], ALU.mult, [modT, gpost], [GGvec])
    S.close()

    for L in opts.get('layers', range(2)):
        xsrc = x_in if (L == 0 or 'layers' in opts) else x1_d
        csrc = ctx_in if (L == 0 or 'layers' in opts) else c1_d
        lstack = contextlib.ExitStack()
        hT = Buf("hT", lstack.enter_context(nc.sbuf_tensor("hT%d_%d" % (L, _UID[0]), [128, KC, TT], BF16)))
        persist.append(hT)

        S = Phase(nc, persist)
        xt = Rot([S.sbuf("xt%d" % i, [128, D], F32) for i in range(2)])
        xn = Rot([S.sbuf("xn%d" % i, [128, D], BF16) for i in range(2)])
        junk = S.sbuf("junk", [128, D], BF16)
        ssq = Rot([S.sbuf("ssq%d" % i, [128, 1], F32) for i in range(2)])
        pst = Rot([S.psum("pst%d" % i, [128, 1024], BF16) for i in range(4)])
        for i in range(NPAIR):
            a = xt.next()
            if i < 2:
                S.dma(a[:], csrc[i * 128:(i + 1) * 128, :], R=[csrc], W=[a], owner=a)
            else:
                S.dma(a[:], xsrc[(i - 2) * 128:(i - 1) * 128, :], R=[xsrc], W=[a], owner=a)
            s = ssq.next()
            S.act(junk[:], a[:], AF.Square, [a], [junk, s], accum=s[:])
            S.ts(s[:], s[:], 1.0 / D, EPS, ALU.mult, ALU.add, [s], [s])
            S.act(s[:], s[:], AF.Sqrt, [s], [s])
            S.op("dve", lambda e, s=s: e.reciprocal(out=s[:], in_=s[:]), [s], [s])
            b = xn.next()
            S.ts(b[:], a[:], s[:], None, ALU.mult, None, [a, s], [b])
            j = 1 if i < 2 else 0
            for q in range(4):
                p = pst.next()
                for r in range(4):
                    kc = q * 4 + r
                    S.tr(p[:, r * 128:(r + 1) * 128], b[:, kc * 128:(kc + 1) * 128], ident[:], [b, ident], [p])
                for r in range(4):
                    kc = q * 4 + r
                    S.act(hT[:, kc, i * 128:(i + 1) * 128], p[:, r * 128:(r + 1) * 128], AF.Identity, [p, Avec, Bvec],
                          [hT], scale=Avec[:, L, j, kc:kc + 1], bias=Bvec[:, L, j, kc:kc + 1])
        S.close()
        if stopped("S1_%d" % L):
            lstack.close()
            break

        S = Phase(nc, persist)
        tabt = Rot([S.sbuf("tabt%d" % i, [128, 512], BF16) for i in range(4)])
        tau = S.sbuf("tau_s", [128, 64], F32)
        ones64 = S.sbuf("ones64", [128, 64], F32)
        S.op("pool", lambda e: e.memset(ones64[:], 1.0), W=[ones64])
        permrt = S.sbuf("permrt", [128, 128], BF16)
        permda = S.sbuf("permda", [128, 128], BF16)
        S.dma(tau[:], tau_d[:, 0:64], R=[tau_d], W=[tau], owner=tau)
        S.dma(permrt[:], permrt_d[:, :], R=[permrt_d], W=[permrt], owner=permrt)
        S.dma(permda[:], permda_d[:, :], R=[permda_d], W=[permda], owner=permda)
        wst = Rot([S.sbuf("wst%d" % i, [128, KC, 128], F32) for i in range(2)])
        wbf = Rot([S.sbuf("wbf%d" % i, [128, KC, 128], BF16) for i in range(2)])
        wstM = Rot([S.sbuf("wstM%d" % i, [128, KC, 128], F32) for i in range(1)])
        wbfM = Rot([S.sbuf("wbfM%d" % i, [128, KC, 128], BF16) for i in range(2)])
        sig = [S.sbuf("sig%d" % i, [128, TT], F32) for i in range(2)]
        zq = S.sbuf("zq", [128, TT], BF16)
        zr = zq
        rq = S.sbuf("rq", [128, TT], BF16)
        rk = S.sbuf("rk", [128, TT], BF16)
        t512 = Rot([S.sbuf("t512_%d" % i, [128, 512], F32) for i in range(7)])
        o512 = Rot([S.sbuf("o512_%d" % i, [128, 512], BF16) for i in range(8)])
        o2304 = Rot([S.sbuf("o2304_%d" % i, [128, TT], BF16) for i in range(2)])
        etab = [S.sbuf("etab%d" % i, [128, 64], F32) for i in range(6)]
        vtok = Rot([S.sbuf("vtok%d" % i, [128, NPAIR, 128], BF16) for i in range(2)])
        kpt = [Rot([S.sbuf("kpt%d_%d" % (d_, i), [128, NPAIR, 128], BF16) for i in range(1)]) for d_ in range(2)]
        dec = [Rot([S.sbuf("dec%d_%d" % (d_, i), [128, 2 * NCH], F32) for i in range(2)]) for d_ in range(2)]
        ntot = Rot([S.sbuf("ntot%d" % i, [128, 8], F32) for i in range(4)])
        Sst = [[S.sbuf("Sst%d_%d" % (d_, i), [128, 128], F32) for i in range(2)] for d_ in range(2)]
        psA = Rot([S.psum("psA%d" % i, [128, 512], F32) for i in range(4)])
        psM = Rot([S.psum("psM%d" % i, [128, 512], F32) for i in range(2)])
        psT = Rot([S.psum("psT%d" % i, [128, 1024], BF16) for i in range(2)])

        def load_w(col0, wst=wst, wbf=wbf):
            a = wst.next()
            S.dma(a[:, 0:8, :], win_d[L, 0:1024, col0:col0 + 128].rearrange("(k p) c -> p k c", p=128),
                  R=[win_d], W=[a], owner=a)
            S.dma(a[:, 8:16, :], win_d[L, 1024:2048, col0:col0 + 128].rearrange("(k p) c -> p k c", p=128),
                  R=[win_d], W=[a], owner=a)
            b = wbf.next()
            S.copy(b[:], a[:], [a], [b], eng="pool")
            return b

        def proj_fm(w, evac):
            for (t0, wd) in TG:
                p = psA.next()
                for kc in range(KC):
                    S.mm(p[:, 0:wd], w[:, kc, :], hT[:, kc, t0:t0 + wd], kc == 0, kc == KC - 1, [w, hT], [p])
                evac(p, t0, wd)

        def proj_tm(w, dst):
            for i4 in range(0, NPAIR, 4):
                n = min(4, NPAIR - i4)
                p = psA.next()
                for r in range(n):
                    i = i4 + r
                    for kc in range(KC):
                        S.mm(p[:, r * 128:(r + 1) * 128], hT[:, kc, i * 128:(i + 1) * 128], w[:, kc, :],
                             (kc == 0 and r == 0), kc == KC - 1, [w, hT], [p])
                S.copy(dst[:, i4:i4 + n, :], p[:, 0:n * 128].rearrange("p (a b) -> p a b", b=128), [p], [dst])

        def spill_fm(kind, h, src, t0=0, wd=TT, src_ap=None):
            S.dma(spf_d[kind, h, :, t0:t0 + wd], src[:, 0:wd] if src_ap is None else src_ap,
                  R=[src], W=[spf_d], owner=src)

        def kp_transpose(src, t0, wd, dst):
            p = psT.next()
            n = wd // 128
            for r in range(n):
                S.tr(p[:, r * 128:(r + 1) * 128], src[:, r * 128:(r + 1) * 128], ident[:], [src, ident], [p])
            S.copy(dst[:, t0 // 128:t0 // 128 + n, :], p[:, 0:n * 128].rearrange("p (a b) -> p a b", b=128),
                   [p], [dst])

        def rope(dst, tcos, tsin, perm, post_scale=None):
            for (t0, wd) in TG:
                p = psA.next()
                S.mm(p[:, 0:wd], perm[:], zr[:, t0:t0 + wd], True, True, [perm, zr], [p])
                tc_ = tabt.next()
                S.dma(tc_[:, 0:wd], tabs_d[tcos, :, t0:t0 + wd], R=[tabs_d], W=[tc_], owner=tc_)
                tsn = tabt.next()
                S.dma(tsn[:, 0:wd], tabs_d[tsin, :, t0:t0 + wd], R=[tabs_d], W=[tsn], owner=tsn)
                a = t512.next()
                S.tt(a[:, 0:wd], zr[:, t0:t0 + wd], tc_[:, 0:wd], ALU.mult, [zr, tc_], [a])
                b = t512.next()
                S.tt(b[:, 0:wd], p[:, 0:wd], tsn[:, 0:wd], ALU.mult, [p, tsn], [b])
                if post_scale is None:
                    S.tt(dst[:, t0:t0 + wd], a[:, 0:wd], b[:, 0:wd], ALU.add, [a, b], [dst], eng="pool")
                else:
                    S.tt(a[:, 0:wd], a[:, 0:wd], b[:, 0:wd], ALU.add, [a, b], [a], eng="pool")
                    S.ts(dst[:, t0:t0 + wd], a[:, 0:wd], post_scale, None, ALU.mult, None, [a], [dst], eng="pool")

        def run1(mix, h, kp, v, dc):
            orders = [list(range(NCH)), [3, 2, 1, 0] + list(range(NCH - 1, 3, -1))]
            cur = [0, 0]
            for d_ in range(2):
                S.op("pool", lambda e, d_=d_: e.memset(Sst[d_][0][:], 0.0), W=[Sst[d_][0]])
            for st in range(0, NCH, 4):
                if (st // 4) % 2 == 0:
                    bgstep()
                for d_ in range(2):
                    pp = [psA.next(), psA.next()]
                    for r in range(4):
                        j = orders[d_][st + r]
                        pr, hf = j // 2, (j % 2) * 64
                        p = pp[j % 2]
                        S.mm(p[:, r * 128:(r + 1) * 128], kp[d_][hf:hf + 64, pr, :], v[hf:hf + 64, pr, :],
                             r < 2, True, [kp[d_], v], [p])
                    for r in range(4):
                        j = orders[d_][st + r]
                        p = pp[j % 2]
                        a, b = Sst[d_][cur[d_]], Sst[d_][1 - cur[d_]]
                        S.stt(b[:], a[:], dc[d_][:, j:j + 1], p[:, r * 128:(r + 1) * 128], ALU.mult, ALU.add,
                              [a, dc[d_], p], [b])
                        cur[d_] = 1 - cur[d_]
            for d_ in range(2):
                a = Sst[d_][cur[d_]]
                r0 = ((mix * 2 + d_) * NH + h) * 128
                S.dma(xs_in[r0:r0 + 128, :], a[:], R=[a], W=[xs_in], owner=a)

        jobs = []
        for h in range(NH):
            for g in (HG_FF, HG_FB, HG_Q, HG_I, HG_G, RT_Q, RT_K, RT_V, RT_G, DA_Q, DA_K, DA_V, DA_G):
                jobs.append(g * 1024 + h * 128)
        wq = []

        def merge_gen():
            if 'mg' not in SEC:
                return
            wcur = load_w(MERGE0, wstM, wbfM)
            pend = None

            def evac(pd):
                p, t0, wd, fc = pd
                o = o512.next()
                S.copy(o[:, 0:wd], p[:, 0:wd], [p], [o])
                S.dma(gm_d[fc, :, t0:t0 + wd], o[:, 0:wd], R=[o], W=[gm_d], owner=o)
            for fc in range(48):
                w = wcur
                if fc + 1 < 48:
                    wcur = load_w(MERGE0 + (fc + 1) * 128, wstM, wbfM)
                for (t0, wd) in TG:
                    p = psM.next()
                    for kc in range(KC):
                        S.mm(p[:, 0:wd], w[:, kc, :], hT[:, kc, t0:t0 + wd], kc == 0, kc == KC - 1, [w, hT], [p])
                    if pend is not None:
                        evac(pend)
                    pend = (p, t0, wd, fc)
                    yield
            evac(pend)
        bg = merge_gen()

        def bgstep(n=1):
            for _ in range(n):
                try:
                    next(bg)
                except StopIteration:
                    return

        def next_w():
            if not wq and jobs:
                wq.append(load_w(jobs.pop(0)))
            w = wq.pop(0)
            if jobs:
                wq.append(load_w(jobs.pop(0)))
            return w

        for h in range(NHD):
            li = L * 16
            for d_ in range(2):
                w = next_w()
                proj_fm(w, lambda p, t0, wd, d_=d_: S.act(sig[d_][:, t0:t0 + wd], p[:, 0:wd], AF.Sigmoid, [p], [sig[d_]]))
            w = next_w()
            proj_fm(w, lambda p, t0, wd: S.act(zq[:, t0:t0 + wd], p[:, 0:wd], AF.Copy, [p], [zq], scale=128 ** -0.5))
            w = next_w()
            vt = vtok.next()
            proj_tm(w, vt)
            S.dma(spv_d[0, h].rearrange("(a p) c -> p a c", p=128), vt[:], R=[vt], W=[spv_d], owner=vt)
            kp = [kpt[0].next(), kpt[1].next()]
            dc = [dec[0].next(), dec[1].next()]
            if opts.get('upto') == 'hg_proj':
                continue
            for (t0, wd) in TG:
                nch = wd // 64
                c0 = t0 // 64
                for d_ in range(2):
                    bgstep()
                    oml = OML[:, L, d_, h:h + 1]
                    noml = NOML[:, L, d_, h:h + 1]
                    lbf = LBF[:, L, d_, h:h + 1]
                    fg = t512.next()
                    S.ts(fg[:, 0:wd], sig[d_][:, t0:t0 + wd], oml, lbf, ALU.mult, ALU.add, [sig[d_], OML, LBF], [fg])
                    kk = t512.next()
                    S.ts(kk[:, 0:wd], sig[d_][:, t0:t0 + wd], noml, oml, ALU.mult, ALU.add, [sig[d_], OML, NOML], [kk])
                    S.act(fg[:, 0:wd], fg[:, 0:wd], AF.Ln, [fg], [fg])
                    cum = t512.next()
                    for c in range(nch):
                        S.op("dve", lambda e, c=c, cum=cum, fg=fg: e.tensor_tensor_scan(
                            out=cum[:, c * 64:(c + 1) * 64], data0=ones64[:], data1=fg[:, c * 64:(c + 1) * 64],
                            initial=0.0, op0=ALU.mult, op1=ALU.add), [fg, ones64], [cum])
                    tot = cum[:, 0:wd].rearrange("p (a b) -> p a b", b=64)[:, :, 63]
                    S.act(dc[d_][:, c0:c0 + nch], tot, AF.Exp, [cum], [dc[d_]])
                    e1 = t512.next()
                    e2 = t512.next()
                    oq = o512.next()
                    ok = o512.next()
                    okp = o512.next()
                    if d_ == 0:
                        midv = cum[:, 0:wd].rearrange("p (a b) -> p a b", b=64)[:, :, 31]
                        nm = ntot.next()
                        S.ts(nm[:, 0:nch], midv, -1.0, None, ALU.mult, None, [cum], [nm])
                        S.act(dc[d_][:, NCH + c0:NCH + c0 + nch], midv, AF.Exp, [cum], [dc[d_]])
                        for c in range(nch):
                            S.act(e1[:, c * 64:(c + 1) * 64], cum[:, c * 64:(c + 1) * 64], AF.Exp, [cum, nm], [e1],
                                  bias=nm[:, c:c + 1])
                            S.act(e2[:, c * 64:(c + 1) * 64], cum[:, c * 64:(c + 1) * 64], AF.Exp, [cum], [e2],
                                  scale=-1.0, bias=cum[:, c * 64 + 31:c * 64 + 32])
                        S.tt(oq[:, 0:wd], zq[:, t0:t0 + wd], e1[:, 0:wd], ALU.mult, [zq, e1], [oq])
                        S.tt(ok[:, 0:wd], kk[:, 0:wd], e2[:, 0:wd], ALU.mult, [kk, e2], [ok])
                        e3 = t512.next()
                        for c in range(nch):
                            S.act(e3[:, c * 64:(c + 1) * 64], cum[:, c * 64:(c + 1) * 64], AF.Exp, [cum], [e3],
                                  scale=-1.0, bias=cum[:, c * 64 + 63:c * 64 + 64])
                        S.tt(okp[:, 0:wd], kk[:, 0:wd], e3[:, 0:wd], ALU.mult, [kk, e3], [okp])
                    else:
                        u = t512.next()
                        S.stt(u[:, 0:wd], cum[:, 0:wd], -1.0, fg[:, 0:wd], ALU.mult, ALU.add, [cum, fg], [u])
                        umid = u[:, 0:wd].rearrange("p (a b) -> p a b", b=64)[:, :, 32]
                        nm = ntot.next()
                        S.ts(nm[:, 0:nch], umid, -1.0, None, ALU.mult, None, [u], [nm])
                        em = ntot.next()
                        S.tt(em[:, 0:nch], umid, tot, ALU.add, [u, cum], [em])
                        S.act(dc[d_][:, NCH + c0:NCH + c0 + nch], em[:, 0:nch], AF.Exp, [em], [dc[d_]])
                        for c in range(nch):
                            S.act(e1[:, c * 64:(c + 1) * 64], u[:, c * 64:(c + 1) * 64], AF.Exp, [u, nm], [e1],
                                  bias=nm[:, c:c + 1])
                            S.act(e2[:, c * 64:(c + 1) * 64], u[:, c * 64:(c + 1) * 64], AF.Exp, [u], [e2],
                                  scale=-1.0, bias=u[:, c * 64 + 32:c * 64 + 33])
                        S.tt(oq[:, 0:wd], zq[:, t0:t0 + wd], e1[:, 0:wd], ALU.mult, [zq, e1], [oq])
                        S.tt(ok[:, 0:wd], kk[:, 0:wd], e2[:, 0:wd], ALU.mult, [kk, e2], [ok])
                        e3 = t512.next()
                        S.act(e3[:, 0:wd], u[:, 0:wd], AF.Exp, [u], [e3], scale=-1.0)
                        S.tt(okp[:, 0:wd], kk[:, 0:wd], e3[:, 0:wd], ALU.mult, [kk, e3], [okp])
                    spill_fm(K_HGQF + d_, h, oq, t0, wd)
                    spill_fm(K_HGKF + d_, h, ok, t0, wd)
                    kp_transpose(okp, t0, wd, kp[d_])
            for d_ in range(2):
                S.dma(spkp_d[0, d_, h].rearrange("(a p) c -> p a c", p=128), kp[d_][:], R=[kp[d_]], W=[spkp_d],
                      owner=kp[d_])
                S.dma(spdec_d[0, d_, h], dc[d_][:], R=[dc[d_]], W=[spdec_d], owner=dc[d_])
            if opts.get('upto') == 'hg_prep':
                continue
            run1(0, h, kp, vt, dc)
            w = next_w()
            og = o2304.next()
            proj_fm(w, lambda p, t0, wd, og=og: S.act(og[:, t0:t0 + wd], p[:, 0:wd], AF.Silu, [p], [og]))
            spill_fm(K_HGG, h, og)
            if opts.get('upto') == 'hg':
                continue
            w = next_w()
            proj_fm(w, lambda p, t0, wd: S.act(zr[:, t0:t0 + wd], p[:, 0:wd], AF.Copy, [p], [zr]))
            rope(rq, 0, 1, permrt)
            w = next_w()
            proj_fm(w, lambda p, t0, wd: S.act(zr[:, t0:t0 + wd], p[:, 0:wd], AF.Copy, [p], [zr], scale=128 ** -0.5))
            rope(rk, 0, 1, permrt)
            w = next_w()
            vt = vtok.next()
            proj_tm(w, vt)
            S.dma(spv_d[2, h].rearrange("(a p) c -> p a c", p=128), vt[:], R=[vt], W=[spv_d], owner=vt)
            kp = [kpt[0].next(), kpt[1].next()]
            dc = [dec[0].next(), dec[1].next()]
            for d_ in range(2):
                ix = li + d_ * 8 + h
                ldc, nld, ld63, ld64, nld64 = [LD[:, q, ix:ix + 1] for q in range(5)]
                if d_ == 0:
                    S.act(etab[0][:], tau[:], AF.Exp, [tau, LD], [etab[0]], scale=ldc, bias=ldc)
                    S.act(etab[1][:], tau[:], AF.Exp, [tau, LD], [etab[1]], scale=nld, bias=nld)
                    S.act(etab[2][:], tau[:], AF.Exp, [tau, LD], [etab[2]], scale=nld, bias=ld63)
                else:
                    S.act(etab[3][:], tau[:], AF.Exp, [tau, LD], [etab[3]], scale=nld, bias=ld64)
                    S.act(etab[4][:], tau[:], AF.Exp, [tau, LD], [etab[4]], scale=ldc, bias=nld64)
                    S.act(etab[5][:], tau[:], AF.Exp, [tau, LD], [etab[5]], scale=ldc)
                S.act(dc[d_][:, 0:NCH], tau[:, 0:NCH], AF.Exp, [tau, LD], [dc[d_]], scale=0.0, bias=ld64)
                S.op("pool", lambda e, d_=d_, dc=dc: e.memset(dc[d_][:, NCH:2 * NCH], 1.0), W=[dc[d_]])
                for (t0, wd) in TG:
                    bgstep()
                    oq = o512.next()
                    ok = o512.next()
                    okp = o512.next()
                    nch = wd // 64

                    def v3d(ap):
                        return ap.rearrange("p (a b) -> p a b", b=64)

                    def eb(i):
                        return etab[i][:, :].unsqueeze(1).to_broadcast([128, nch, 64])
                    S.tt(v3d(oq[:, 0:wd]), v3d(rq[:, t0:t0 + wd]), eb(3 * d_), ALU.mult, [rq, etab[3 * d_]], [oq])
                    S.tt(v3d(ok[:, 0:wd]), v3d(rk[:, t0:t0 + wd]), eb(3 * d_ + 1), ALU.mult, [rk, etab[3 * d_ + 1]], [ok])
                    S.tt(v3d(okp[:, 0:wd]), v3d(rk[:, t0:t0 + wd]), eb(3 * d_ + 2), ALU.mult, [rk, etab[3 * d_ + 2]],
                         [okp])
                    spill_fm(K_RTQF + d_, h, oq, t0, wd)
                    spill_fm(K_RTKF + d_, h, ok, t0, wd)
                    kp_transpose(okp, t0, wd, kp[d_])
                S.dma(spkp_d[1, d_, h].rearrange("(a p) c -> p a c", p=128), kp[d_][:], R=[kp[d_]], W=[spkp_d],
                      owner=kp[d_])
                S.dma(spdec_d[1, d_, h], dc[d_][:], R=[dc[d_]], W=[spdec_d], owner=dc[d_])
            run1(1, h, kp, vt, dc)
            w = next_w()
            og = o2304.next()
            proj_fm(w, lambda p, t0, wd, og=og: S.act(og[:, t0:t0 + wd], p[:, 0:wd], AF.Silu, [p], [og]))
            spill_fm(K_RTG, h, og)
            if opts.get('upto') == 'rt':
                continue
            w = next_w()
            proj_fm(w, lambda p, t0, wd: S.act(zr[:, t0:t0 + wd], p[:, 0:wd], AF.Copy, [p], [zr]))
            og = o2304.next()
            rope(og, 2, 3, permda)
            spill_fm(K_DAQ, h, og)
            w = next_w()
            proj_fm(w, lambda p, t0, wd: S.act(zr[:, t0:t0 + wd], p[:, 0:wd], AF.Copy, [p], [zr]))
            og = o2304.next()
            rope(og, 2, 3, permda)
            spill_fm(K_DAK, h, og)
            S.dma(xk_in[h // 4][(h % 4) * 128:(h % 4 + 1) * 128, :], og[:, TCX:TT], R=[og], W=[xk_in[h // 4]], owner=og)
            w = next_w()
            vt = vtok.next()
            proj_tm(w, vt)
            S.dma(spv_d[1, h].rearrange("(a p) c -> p a c", p=128), vt[:], R=[vt], W=[spv_d], owner=vt)
            S.dma(xv_in[h // 4][(h % 4) * TL:(h % 4 + 1) * TL, :].rearrange("(a p) c -> p a c", p=128), vt[:, 2:NPAIR, :],
                  R=[vt], W=[xv_in[h // 4]], owner=vt)
            w = next_w()
            og = o2304.next()
            proj_fm(w, lambda p, t0, wd, og=og: S.act(og[:, t0:t0 + wd], p[:, 0:wd], AF.Silu, [p], [og]))
            spill_fm(K_DAG, h, og)
        for _ in bg:
            pass
        RG = [[0, 1], [2, 3], [4, 5], [6, 7]]
        for (a, b, key) in (((xk_in[0], xk_out[0], "cc1"), (xk_in[1], xk_out[1], "cc1"), (xv_in[0], xv_out[0], "cc1"),
                             (xv_in[1], xv_out[1], "cc1"), (xs_in, xs_out, "cc1")) if not opts.get("no_cc") else ()):
            S.custom("pool", lambda e, a=a, b=b: e.collective_compute("AllGather", ALU.bypass, replica_groups=RG,
                                                                      ins=[a.t.ap().opt()], outs=[b.t.ap().opt()]),
                     key, 1, R=[a], W=[b])
        if opts.get("no_cc"):
            ccb = Buf("ccb")
            for (a, b) in ((xk_in[0], xk_out[0]), (xk_in[1], xk_out[1]), (xv_in[0], xv_out[0]), (xv_in[1], xv_out[1]),
                           (xs_in, xs_out)):
                n = a.t.shape[0]
                for sl in range(2):
                    S.dma(b[sl * n:(sl + 1) * n, :], a[:, :], R=[a], W=[b], owner=ccb)
        S.close()
        lstack.close()
        persist.remove(hT)
        if stopped("S2_%d" % L):
            break

        S = Phase(nc, persist)
        ctx_on = (L == 0)
        qk = [[S.sbuf("qk%d_%d" % (a, d_), [128, TT], BF16) for d_ in range(2)] for a in range(2)]
        kp3 = [S.sbuf("kp3_%d" % d_, [128, NPAIR, 128], BF16) for d_ in range(2)]
        v3 = S.sbuf("v3", [128, NPAIR, 128], BF16)
        g3 = S.sbuf("g3", [128, TT], BF16)
        dc3 = [S.sbuf("dc3_%d" % d_, [128, 2 * NCH], F32) for d_ in range(2)]
        Sall = [S.sbuf("Sall%d" % d_, [128, NCH + 1, 128], BF16) for d_ in range(2)]
        Sst = [[S.sbuf("S3st%d_%d" % (d_, i), [128, 128], F32) for i in range(2)] for d_ in range(2)]
        Rst = [S.sbuf("Rst%d" % d_, [128, 128], F32) for d_ in range(2)]
        ATf = Rot([S.sbuf("ATf%d" % i, [128, 128], BF16) for i in range(4)])
        ATb = Rot([S.sbuf("ATb%d" % i, [128, 128], BF16) for i in range(4)])
        for a_ in ATf.bufs + ATb.bufs:
            S.op("pool", lambda e, a_=a_: e.memset(a_[:], 0.0), W=[a_])
        sq = Rot([S.sbuf("sq%d" % i, [128, 512], BF16) for i in range(2)])
        f512 = Rot([S.sbuf("f512_%d" % i, [128, 512], F32) for i in range(8)])
        yo = Rot([S.sbuf("yo%d" % i, [128, 512], BF16) for i in range(3)])
        kall = S.sbuf("kall", [128, TCX + 2 * TL], BF16)
        vall = S.sbuf("vall", [128, 34, 128], BF16)
        pexp = Rot([S.sbuf("pexp%d" % i, [128, 512], BF16) for i in range(6)])
        qz = [S.sbuf("qz%d" % m, [128, TT], BF16) for m in range(2)]
        S.op("pool", lambda e: e.memset(qz[0][64:128, :], 0.0), W=[qz[0]])
        S.op("pool", lambda e: e.memset(qz[1][0:64, :], 0.0), W=[qz[1]])
        psS = Rot([S.psum("psS%d" % i, [128, 512], F32) for i in range(4)])
        psO = [S.psum("psO%d" % i, [128, 512], F32) for i in range(4)]
        tgs = TG if ctx_on else TG[1:]
        pok = [0]

        def mkset(tag):
            return dict(
                qk=[[S.sbuf("qk%s%d_%d" % (tag, a, d_), [128, TT], BF16) for d_ in range(2)] for a in range(2)],
                kp3=[S.sbuf("kp3%s_%d" % (tag, d_), [128, NPAIR, 128], BF16) for d_ in range(2)],
                v3=S.sbuf("v3%s" % tag, [128, NPAIR, 128], BF16),
                g3=S.sbuf("g3%s" % tag, [128, TT], BF16),
                dc3=[S.sbuf("dc3%s_%d" % (tag, d_), [128, 2 * NCH], F32) for d_ in range(2)],
                Sall=[S.sbuf("Sall%s%d" % (tag, d_), [128, NCH + 1, 128], BF16) for d_ in range(2)],
                Sst=[[S.sbuf("S3st%s%d_%d" % (tag, d_, i), [128, 128], F32) for i in range(2)] for d_ in range(2)],
                Rst=[S.sbuf("Rst%s%d" % (tag, d_), [128, 128], F32) for d_ in range(2)])
        sets = [dict(qk=qk, kp3=kp3, v3=v3, g3=g3, dc3=dc3, Sall=Sall, Sst=Sst, Rst=Rst), mkset("B")]
        gda = S.sbuf("gda", [128, TT], BF16)

        def gla_load(h, mix):
            T = sets[mix]
            kq = (K_HGQF, K_HGKF) if mix == 0 else (K_RTQF, K_RTKF)
            for d_ in range(2):
                S.dma(T["qk"][0][d_][:], spf_d[kq[0] + d_, h], R=[spf_d], W=[T["qk"][0][d_]], owner=T["qk"][0][d_])
                S.dma(T["qk"][1][d_][:], spf_d[kq[1] + d_, h], R=[spf_d], W=[T["qk"][1][d_]], owner=T["qk"][1][d_])
                S.dma(T["kp3"][d_][:], spkp_d[mix, d_, h].rearrange("(a p) c -> p a c", p=128), R=[spkp_d],
                      W=[T["kp3"][d_]], owner=T["kp3"][d_])
                S.dma(T["dc3"][d_][:], spdec_d[mix, d_, h], R=[spdec_d], W=[T["dc3"][d_]], owner=T["dc3"][d_])
                r0 = (((d_ * 2 + mix) * 2 + d_) * NH + h) * 128
                S.dma(T["Rst"][d_][:], xs_out[r0:r0 + 128, :], R=[xs_out], W=[T["Rst"][d_]], owner=T["Rst"][d_])
            S.dma(T["v3"][:], spv_d[0 if mix == 0 else 2, h].rearrange("(a p) c -> p a c", p=128), R=[spv_d],
                  W=[T["v3"]], owner=T["v3"])
            S.dma(T["g3"][:], spf_d[K_HGG if mix == 0 else K_RTG, h], R=[spf_d], W=[T["g3"]], owner=T["g3"])

        def gla_run2(h):
            orders = [list(range(NCH)), [3, 2, 1, 0] + list(range(NCH - 1, 3, -1))]
            cur = [[0, 0], [0, 0]]
            for mix in range(2):
                for d_ in range(2):
                    S.op("pool", lambda e, mix=mix, d_=d_: e.memset(sets[mix]["Sst"][d_][0][:], 0.0),
                         W=[sets[mix]["Sst"][d_][0]])
            for st in range(0, NCH, 4):
                for mix in range(2):
                    T = sets[mix]
                    for d_ in range(2):
                        Sst_, Rst_, dc_ = T["Sst"][d_], T["Rst"][d_], T["dc3"][d_]
                        if st == 4:
                            a, b = Sst_[cur[mix][d_]], Sst_[1 - cur[mix][d_]]
                            sel = flags[:, 0:1] if d_ == 0 else oms[:, 0:1]
                            nsel = oms[:, 0:1] if d_ == 0 else flags[:, 0:1]
                            S.ts(Rst_[:], Rst_[:], sel, None, ALU.mult, None, [Rst_, flags, oms], [Rst_])
                            S.stt(b[:], a[:], nsel, Rst_[:], ALU.mult, ALU.add, [a, Rst_, flags, oms], [b])
                            cur[mix][d_] = 1 - cur[mix][d_]
                        pp = [psS.next(), psS.next()]
                        for r in range(4):
                            j = orders[d_][st + r]
                            pr, hf = j // 2, (j % 2) * 64
                            p = pp[j % 2]
                            S.mm(p[:, r * 128:(r + 1) * 128], T["kp3"][d_][hf:hf + 64, pr, :], T["v3"][hf:hf + 64, pr, :],
                                 r < 2, True, [T["kp3"][d_], T["v3"]], [p])
                        for r in range(4):
                            j = orders[d_][st + r]
                            p = pp[j % 2]
                            a, b = Sst_[cur[mix][d_]], Sst_[1 - cur[mix][d_]]
                            S.act(T["Sall"][d_][:, j, :], a[:], AF.Identity, [a, dc_], [T["Sall"][d_]],
                                  scale=dc_[:, NCH + j:NCH + j + 1])
                            S.stt(b[:], a[:], dc_[:, j:j + 1], p[:, r * 128:(r + 1) * 128], ALU.mult, ALU.add,
                                  [a, dc_, p], [b])
                            cur[mix][d_] = 1 - cur[mix][d_]

        def gla_pass2(h, mix):
            T = sets[mix]
            qk_, v3_, g3_, Sall_ = T["qk"], T["v3"], T["g3"], T["Sall"]
            for (t0, wd) in tgs:
                po = psO[pok[0] % 4]
                pok[0] += 1
                prs = list(range(t0 // 128, (t0 + wd) // 128))
                pend = {}
                first = True
                for idx in range(len(prs) + 1):
                    if idx < len(prs):
                        c0 = prs[idx] * 128
                        ats = []
                        for d_ in range(2):
                            p = psS.next()
                            S.mm(p[:, 0:128], qk_[1][d_][:, c0:c0 + 128], qk_[0][d_][:, c0:c0 + 128], True, True,
                                 [qk_[1][d_], qk_[0][d_]], [p])
                            a = (ATf if d_ == 0 else ATb).next()
                            mk = maskF if d_ == 0 else maskB
                            S.op("dve", lambda e, a=a, p=p, mk=mk: e.copy_predicated(out=a[:], mask=mk[:], data=p[:, 0:128]),
                                 [p, mk], [a])
                            ats.append(a)
                        pend[idx] = ats
                    if idx >= 1:
                        pr = prs[idx - 1]
                        ats = pend.pop(idx - 1)
                        c0 = pr * 128
                        oc = c0 - t0
                        for d_ in range(2):
                            S.mm(po[:, oc:oc + 128], v3_[:, pr, :], ats[d_][:], first, False, [v3_, ats[d_]], [po])
                            first = False
                        for c in range(2):
                            j = pr * 2 + c
                            for d_ in range(2):
                                S.mm(po[:, oc + c * 64:oc + c * 64 + 64], Sall_[d_][:, j, :],
                                     qk_[0][d_][:, c0 + c * 64:c0 + c * 64 + 64], False, (c == 1 and d_ == 1),
                                     [Sall_[d_], qk_[0][d_]], [po])
                br = 0 if mix == 0 else 2
                s_ = sq.next()
                S.act(s_[:, 0:wd], po[:, 0:wd], AF.Square, [po], [s_])
                p = psS.next()
                S.mm(p[:, 0:wd], onesm_b[:], s_[:, 0:wd], True, True, [onesm_b, s_], [p])
                r = f512.next()
                S.act(r[:, 0:wd], p[:, 0:wd], AF.Ln, [p], [r], bias=EPS)
                S.act(r[:, 0:wd], r[:, 0:wd], AF.Exp, [r], [r], scale=-0.5)
                y = f512.next()
                S.tt(y[:, 0:wd], po[:, 0:wd], r[:, 0:wd], ALU.mult, [po, r], [y])
                o = yo.next()
                S.stt(o[:, 0:wd], y[:, 0:wd], GBR[:, L, br, h:h + 1], g3_[:, t0:t0 + wd], ALU.mult, ALU.mult,
                      [y, GBR, g3_], [o])
                S.dma(ys_d[br * 8 + h, :, t0:t0 + wd], o[:, 0:wd], R=[o], W=[ys_d], owner=o)

        def da_load(h):
            for m in range(2):
                S.dma(qz[m][m * 64:(m + 1) * 64, :], spf_d[K_DAQ, h, m * 64:(m + 1) * 64, :], R=[spf_d], W=[qz[m]],
                      owner=qz[m])
            S.dma(gda[:], spf_d[K_DAG, h], R=[spf_d], W=[gda], owner=gda)
            S.dma(kall[:, 0:TCX], spf_d[K_DAK, h, :, 0:TCX], R=[spf_d], W=[kall], owner=kall)
            for sl in range(2):
                S.dma(kall[:, TCX + sl * TL:TCX + (sl + 1) * TL],
                      xk_out[h // 4][(sl * 4 + h % 4) * 128:(sl * 4 + h % 4 + 1) * 128, :],
                      R=[xk_out[h // 4]], W=[kall], owner=kall)
                S.dma(vall[:, 2 + sl * 16:2 + (sl + 1) * 16, :],
                      xv_out[h // 4][(sl * 4 + h % 4) * TL:(sl * 4 + h % 4 + 1) * TL, :].rearrange("(a p) c -> p a c", p=128),
                      R=[xv_out[h // 4]], W=[vall], owner=vall)
            S.dma(vall[:, 0:2, :], spv_d[1, h, 0:TCX, :].rearrange("(a p) c -> p a c", p=128), R=[spv_d], W=[vall],
                  owner=vall)

        gla_load(0, 0)
        gla_load(0, 1)
        for h in range(NH):
            gla_run2(h)
            da_load(h)
            gla_pass2(h, 0)
            gla_pass2(h, 1)
            if h + 1 < NH:
                gla_load(h + 1, 0)
                gla_load(h + 1, 1)
            for (t0, wd) in tgs:
                nkt = 2 if t0 == 0 else 34
                steps = [(kt, m) for kt in range(nkt) for m in range(2)]
                LOOK = 3
                pend = {}
                for i in range(len(steps) + LOOK):
                    if i < len(steps):
                        kt, m = steps[i]
                        p = psS.next()
                        S.mm(p[:, 0:wd], kall[:, kt * 128:(kt + 1) * 128], qz[m][:, t0:t0 + wd], True, True,
                             [kall, qz[m]], [p])
                        e = pexp.next()
                        S.act(e[:, 0:wd], p[:, 0:wd], AF.Exp, [p], [e], scale=0.125)
                        pend[i] = e
                    if i >= LOOK:
                        kt, m = steps[i - LOOK]
                        e = pend.pop(i - LOOK)
                        S.mm(psO[m][:, 0:wd], vall[:, kt, :], e[:, 0:wd], kt == 0, kt == nkt - 1, [vall, e], [psO[m]])
                        S.mm(psO[2 + m][:, 0:wd], ones_b[:], e[:, 0:wd], kt == 0, kt == nkt - 1, [ones_b, e], [psO[2 + m]])
                r1 = f512.next()
                S.act(r1[:, 0:wd], psO[2][:, 0:wd], AF.Ln, [psO[2]], [r1])
                S.act(r1[:, 0:wd], r1[:, 0:wd], AF.Exp, [r1], [r1], scale=-1.0)
                r2 = f512.next()
                S.act(r2[:, 0:wd], psO[3][:, 0:wd], AF.Ln, [psO[3]], [r2])
                S.act(r2[:, 0:wd], r2[:, 0:wd], AF.Exp, [r2], [r2], scale=-1.0)
                S.tt(r1[:, 0:wd], psO[0][:, 0:wd], r1[:, 0:wd], ALU.mult, [psO[0], r1], [r1])
                S.tt(r2[:, 0:wd], psO[1][:, 0:wd], r2[:, 0:wd], ALU.mult, [psO[1], r2], [r2])
                oo = f512.next()
                S.stt(oo[:, 0:wd], r2[:, 0:wd], NLAM[:, L:L + 1], r1[:, 0:wd], ALU.mult, ALU.add, [r1, r2, NLAM], [oo])
                s = sq.next()
                S.act(s[:, 0:wd], oo[:, 0:wd], AF.Square, [oo], [s])
                p = psS.next()
                S.mm(p[:, 0:wd], onesm_b[:], s[:, 0:wd], True, True, [onesm_b, s], [p])
                r = f512.next()
                S.act(r[:, 0:wd], p[:, 0:wd], AF.Ln, [p], [r], bias=EPS)
                S.act(r[:, 0:wd], r[:, 0:wd], AF.Exp, [r], [r], scale=-0.5)
                S.tt(oo[:, 0:wd], oo[:, 0:wd], r[:, 0:wd], ALU.mult, [oo, r], [oo])
                o = yo.next()
                S.stt(o[:, 0:wd], oo[:, 0:wd], GBR[:, L, 1, h:h + 1], gda[:, t0:t0 + wd], ALU.mult, ALU.mult,
                      [oo, GBR, gda], [o])
                S.dma(ys_d[8 + h, :, t0:t0 + wd], o[:, 0:wd], R=[o], W=[ys_d], owner=o)
        S.close()
        if stopped("S3_%d" % L):
            break

        S = Phase(nc, persist)
        wstg = Rot([S.sbuf("wstg%d" % i, [128, 8, 128], F32) for i in range(2)])
        wcb = Rot([S.sbuf("wcb%d" % i, [128, 8, 128], BF16) for i in range(2)])
        wob = S.sbuf("wob", [128, KC, D], BF16)
        wos = Rot([S.sbuf("wos%d" % i, [128, D], F32) for i in range(1)])
        for br in range(3):
            for fc in range(KC):
                a = wstg.next()
                S.dma(a[:], wbr_d[L, br, :, fc * 128:(fc + 1) * 128].rearrange("(k p) c -> p k c", p=128),
                      R=[wbr_d], W=[a], owner=a)
                b = wcb.next()
                if (br * KC + fc) % 2 == 0:
                    S.act(b[:], a[:], AF.Copy, [a], [b])
                else:
                    S.copy(b[:], a[:], [a], [b])
                S.dma(wbrb_d[br, fc], b[:].rearrange("p a b -> p (a b)"), R=[b], W=[wbrb_d], owner=b)
        for kc in range(KC):
            a = wos.next()
            S.dma(a[:], wout_d[L, kc * 128:(kc + 1) * 128, :], R=[wout_d], W=[a], owner=a)
            if kc % 2:
                S.act(wob[:, kc, :], a[:], AF.Copy, [a], [wob])
            else:
                S.copy(wob[:, kc, :], a[:], [a], [wob])
        GGb = [S.sbuf("GGb%d" % j, [128, D], F32) for j in range(2)]
        onesf4 = S.sbuf("onesf4", [128, 128], F32)
        S.op("pool", lambda e: e.memset(onesf4[:], 1.0), W=[onesf4])
        dg = Rot([S.sbuf("dg%d" % i, [128, 128], F32) for i in range(2)])
        ps4 = Rot([S.psum("ps4_%d" % i, [128, 512], F32) for i in range(8)])
        for j in range(2):
            for q in range(4):
                p = ps4.next()
                for r in range(4):
                    kc = q * 4 + r
                    dgt = dg.next()
                    S.ts(dgt[:], identf[:], GGvec[:, L, j, kc:kc + 1], None, ALU.mult, None, [identf, GGvec], [dgt])
                    S.mm(p[:, r * 128:(r + 1) * 128], onesf4[:], dgt[:], r == 0, True, [dgt, onesf4], [p])
                S.copy(GGb[j][:, q * 512:(q + 1) * 512], p[:], [p], [GGb[j]])
        ysT = Rot([S.sbuf("ysT%d" % i, [128, 24, 512], BF16) for i in range(1)])
        gmt = Rot([S.sbuf("gmt%d" % i, [128, 3, 512], BF16) for i in range(3)])
        wbt = Rot([S.sbuf("wbt%d" % i, [128, 3, 8, 128], BF16) for i in range(2)])
        mrg = Rot([S.sbuf("mrg%d" % i, [128, KC, 512], BF16) for i in range(1)])
        m512 = Rot([S.sbuf("m512_%d" % i, [128, 512], F32) for i in range(4)])
        xres = Rot([S.sbuf("xres%d" % i, [128, D], F32) for i in range(1)])
        yout = Rot([S.sbuf("yout%d" % i, [128, D], F32) for i in range(2)])
        ss4 = Rot([S.sbuf("ss4_%d" % i, [128, 4], F32) for i in range(2)])
        junk4 = S.sbuf("junk4", [128, 512], BF16)
        for (t0, wd) in (TG if ctx_on else TG[1:]):
            if opts.get('upto') == 's4_prep':
                break
            if opts.get('upto') in ('s4_proj', 's4_tile') and t0 > 0:
                break
            j = 1 if t0 == 0 else 0
            yt = ysT.next()
            for br in range(3):
                S.dma(yt[:, br * 8:(br + 1) * 8, 0:wd], ys_d[br * 8:(br + 1) * 8, :, t0:t0 + wd].rearrange("a p t -> p a t"),
                      R=[ys_d], W=[yt], owner=yt)
            mg = mrg.next()
            for fc in range(KC):
                wb = wbt.next()
                S.dma(wb[:].rearrange("p a b c -> p a (b c)"), wbrb_d[:, fc].rearrange("a p x -> p a x"), R=[wbrb_d],
                      W=[wb], owner=wb)
                gt = gmt.next()
                S.dma(gt[:, :, 0:wd], gm_d[:, :, t0:t0 + wd].rearrange("(a f) p t -> f p a t", f=KC)[fc],
                      R=[gm_d], W=[gt], owner=gt)
                for br in range(3):
                    S.act(gt[:, br, 0:wd], gt[:, br, 0:wd], AF.Sigmoid, [gt], [gt])
                ms = []
                for br in range(3):
                    p = ps4.next()
                    for k8 in range(8):
                        S.mm(p[:, 0:wd], wb[:, br, k8, :], yt[:, br * 8 + k8, 0:wd], k8 == 0, k8 == 7, [wb, yt], [p])
                    m = m512.next()
                    S.tt(m[:, 0:wd], p[:, 0:wd], gt[:, br, 0:wd], ALU.mult, [p, gt], [m])
                    ms.append(m)
                S.tt(ms[0][:, 0:wd], ms[0][:, 0:wd], ms[1][:, 0:wd], ALU.add, [ms[0], ms[1]], [ms[0]])
                S.tt(mg[:, fc, 0:wd], ms[0][:, 0:wd], ms[2][:, 0:wd], ALU.add, [ms[0], ms[2]], [mg])
            for ts_ in range(wd // 128):
                if opts.get('upto') == 's4_proj':
                    break
                tok0 = t0 + ts_ * 128
                xr = xres.next()
                if tok0 < TCX:
                    S.dma(xr[:], csrc[tok0:tok0 + 128, :], R=[csrc], W=[xr], owner=xr)
                else:
                    S.dma(xr[:], xsrc[tok0 - TCX:tok0 - TCX + 128, :], R=[xsrc], W=[xr], owner=xr)
                yo_ = yout.next()
                s4 = ss4.next()
                for q in range(4):
                    p = ps4.next()
                    for kc in range(KC):
                        S.mm(p[:], mg[:, kc, ts_ * 128:(ts_ + 1) * 128], wob[:, kc, q * 512:(q + 1) * 512], kc == 0,
                             kc == KC - 1, [mg, wob], [p])
                    S.copy(yo_[:, q * 512:(q + 1) * 512], p[:], [p], [yo_])
                    S.act(junk4[:], yo_[:, q * 512:(q + 1) * 512], AF.Square, [yo_], [junk4, s4], accum=s4[:, q:q + 1])
                S.op("dve", lambda e, s4=s4: e.reduce_sum(out=s4[:, 0:1], in_=s4[:, 0:4], axis=AX.X), [s4], [s4])
                S.ts(s4[:, 0:1], s4[:, 0:1], 1.0 / D, EPS, ALU.mult, ALU.add, [s4], [s4])
                S.act(s4[:, 0:1], s4[:, 0:1], AF.Sqrt, [s4], [s4])
                S.op("dve", lambda e, s4=s4: e.reciprocal(out=s4[:, 0:1], in_=s4[:, 0:1]), [s4], [s4])
                S.stt(yo_[:], yo_[:], s4[:, 0:1], GGb[j][:], ALU.mult, ALU.mult, [yo_, s4, GGb[j]], [yo_])
                S.tt(yo_[:], yo_[:], xr[:], ALU.add, [yo_, xr], [yo_])
                if tok0 < TCX:
                    dst, r0 = c1_d, tok0
                else:
                    dst, r0 = (x1_d if L == 0 else out_d), tok0 - TCX
                S.dma(dst[r0:r0 + 128, :], yo_[:], R=[yo_], W=[dst], owner=yo_)
        S.close()
        if stopped("S4_%d" % L):
            break
    pstack.close()
    return nc


def _rope_tables(half):
    f32 = np.float32
    pos_rt = np.concatenate([np.arange(TCX, dtype=f32), TCX + half * TL + np.arange(TL, dtype=f32)])
    inv = (10000.0 ** (-np.arange(0, 128, 2, dtype=f32) / 128)).astype(f32)
    ang = pos_rt[None, :] * np.concatenate([inv, inv])[:, None]
    rt_cos = np.cos(ang)
    rt_sin = np.sin(ang) * np.concatenate([-np.ones(64), np.ones(64)])[:, None]
    n = half * TL + np.arange(TL)
    row = (n // 64).astype(f32)
    col = (n % 64).astype(f32)
    inv_a = (10000.0 ** (-np.arange(0, 32, 2, dtype=f32) / 32)).astype(f32)
    da_cos = np.ones((128, TT), f32)
    da_sin = np.zeros((128, TT), f32)
    for i in range(128):
        d = i % 64
        pos = row if d < 32 else col
        dd = d % 32
        a = pos * inv_a[dd % 16]
        da_cos[i, TCX:] = np.cos(a)
        da_sin[i, TCX:] = np.sin(a) * (-1.0 if dd < 16 else 1.0)
    return np.stack([rt_cos, rt_sin, da_cos, da_sin]).astype(f32).astype(ml_dtypes.bfloat16)


def _consts():
    bf = ml_dtypes.bfloat16
    ident = np.eye(128, dtype=np.float32)
    s = np.arange(128)[:, None]
    t = np.arange(128)[None, :]
    same = (s // 64) == (t // 64)
    maskF = (same & (s <= t)).astype(np.int32)
    maskB = (same & (s >= t)).astype(np.int32)
    perm_rt = np.zeros((128, 128), np.float32)
    perm_da = np.zeros((128, 128), np.float32)
    for m in range(128):
        perm_rt[(m + 64) % 128, m] = 1.0
        dd = m % 32
        partner = m + 16 if dd < 16 else m - 16
        perm_da[partner, m] = 1.0
    tau = np.tile((np.arange(512) % 64).astype(np.float32)[None, :], (128, 1))
    return dict(ident=ident.astype(bf), identf=ident, maskF=maskF, maskB=maskB, perm_rt=perm_rt.astype(bf),
                perm_da=perm_da.astype(bf), tau=tau)


def _fm(v, nchunk):
    v = np.asarray(v, np.float32)
    lead = v.shape[:-1]
    r = v.reshape(lead + (nchunk, 128))
    r = np.moveaxis(r, -1, 0)
    return np.ascontiguousarray(r)


_NC_CACHE = {}


def make_in_maps(x, c, ctx, c_ctx, w_mod, b_mod, g_pre, g_post, w_in, hg_lower, ret_decay, diff_lambda, g_branch,
                 w_branch, w_out):
    f32 = np.float32
    consts = _consts()
    shared = dict(
        w_mod=np.ascontiguousarray(w_mod, f32), w_in=np.ascontiguousarray(w_in, f32),
        w_branch=np.ascontiguousarray(w_branch, f32), w_out=np.ascontiguousarray(w_out, f32),
        bmod_fm=_fm(b_mod, 48).reshape(128, 96), gpre_fm=_fm(g_pre, KC).reshape(128, 32),
        gpost_fm=_fm(g_post, KC).reshape(128, 32),
        hgl_fm=_fm(hg_lower, NH).reshape(128, 32),
        rdec_b=np.ascontiguousarray(np.broadcast_to(np.asarray(ret_decay, f32).reshape(1, 32), (128, 32))),
        dlam_b=np.ascontiguousarray(np.broadcast_to(np.asarray(diff_lambda, f32).reshape(1, 512), (128, 512))),
        gbr_fm=_fm(g_branch, NH).reshape(128, 48), **consts)
    tabs = [_rope_tables(0), _rope_tables(1)]
    in_maps = []
    for core in range(8):
        b, half = core // 2, core % 2
        cv = np.stack([np.asarray(c[b], f32), np.asarray(c_ctx, f32)], axis=-1)
        cv = np.ascontiguousarray(cv.reshape(KC, 128, 2).transpose(1, 0, 2)).reshape(128, 32)
        fl = np.zeros((128, 2), f32)
        fl[:, 0] = half
        fl[:, 1] = half
        m = dict(shared)
        m.update(x=np.ascontiguousarray(x[b, half * TL:(half + 1) * TL], f32), ctx=np.ascontiguousarray(ctx[b], f32),
                 cvec=cv, tabs=tabs[half], flags=fl)
        in_maps.append(m)
    return in_maps


def kernel(x, c, ctx, c_ctx, w_mod, b_mod, g_pre, g_post, w_in, hg_lower, ret_decay, diff_lambda, g_branch,
           w_branch, w_out):
    in_maps = make_in_maps(x, c, ctx, c_ctx, w_mod, b_mod, g_pre, g_post, w_in, hg_lower, ret_decay, diff_lambda,
                           g_branch, w_branch, w_out)
    if "nc" not in _NC_CACHE:
        _NC_CACHE["nc"] = build_program()
    res = run_bass_kernel_spmd(_NC_CACHE["nc"], in_maps, core_ids=list(range(8)))
    out = np.empty((4, 4096, D), np.float32)
    for core in range(8):
        b, half = core // 2, core % 2
        out[b, half * TL:(half + 1) * TL] = res.results[core]["out"]
    return out
```

```python
import contextlib
import math
import numpy as np
import ml_dtypes
import concourse.bass as bass
import concourse.mybir as mybir
from concourse.bass_utils import run_bass_kernel_spmd

F32 = mybir.dt.float32
BF16 = mybir.dt.bfloat16
I32 = mybir.dt.int32
AF = mybir.ActivationFunctionType
ALU = mybir.AluOpType
AX = mybir.AxisListType

D = 2048
KC = 16
TCX = 256
TL = 2048
TT = 2304
NPAIR = 18
NCH = 36
NH = 8
IN_COLS = 19456
(HG_Q, HG_FF, HG_FB, HG_I, HG_G, DA_Q, DA_K, DA_V, DA_G, RT_Q, RT_K, RT_V, RT_G) = range(13)
MERGE0 = 13 * 1024
LAM_INIT = [0.8 - 0.6 * math.exp(-0.3 * l) for l in range(2)]
EPS = 1e-6
TG = [(0, 256), (256, 512), (768, 512), (1280, 512), (1792, 512)]
SAME_ENGINE_WAITS = True
NO_SELF_WAIT = ("pe", "act")
(K_HGQF, K_HGQB, K_HGKF, K_HGKB, K_HGG, K_RTQF, K_RTQB, K_RTKF, K_RTKB, K_RTG, K_DAQ, K_DAG, K_DAK) = range(13)


class Buf:
    __slots__ = ("name", "t", "writers", "readers", "dirty_read", "semkey", "is_dram")

    def __init__(self, name, t=None, is_dram=False):
        self.name = name
        self.t = t
        self.is_dram = is_dram
        self.writers = {}
        self.readers = {}
        self.dirty_read = False
        self.semkey = "d_" + name

    def reset(self):
        self.writers = {}
        self.readers = {}
        self.dirty_read = False

    def __getitem__(self, idx):
        return self.t[idx]


_UID = [0]


class Phase:
    ENGINES = ("pe", "act", "dve", "pool", "sp")

    def __init__(self, nc, persist):
        self.nc = nc
        self.persist = persist
        self.stack = contextlib.ExitStack()
        self.streams = {e: [] for e in self.ENGINES}
        self.semcnt = {}
        self.semh = {}
        self.waited = {e: {} for e in self.ENGINES}
        for e in ("pe", "act", "dve", "pool"):
            self.new_sem(e)
        self.n = 0

    def new_sem(self, key):
        self.semcnt[key] = 0
        _UID[0] += 1
        self.semh[key] = self.nc.alloc_semaphore(name="s%d_%s" % (_UID[0], key))

    def sbuf(self, name, shape, dtype):
        _UID[0] += 1
        name = "%s_u%d" % (name, _UID[0])
        t = self.stack.enter_context(self.nc.sbuf_tensor(name, list(shape), dtype))
        return Buf(name, t)

    def psum(self, name, shape, dtype):
        _UID[0] += 1
        name = "%s_u%d" % (name, _UID[0])
        t = self.stack.enter_context(self.nc.psum_tensor(name, list(shape), dtype))
        return Buf(name, t)

    def _collect(self, eng, reads, writes):
        deps = {}

        def add(d):
            for k, v in d.items():
                if deps.get(k, 0) < v:
                    deps[k] = v
        for r in reads:
            add(r.writers)
        for w in writes:
            add(w.readers)
            if not w.is_dram:
                add(w.writers)
        waits = []
        wd = self.waited[eng]
        for k, v in deps.items():
            if k == eng and (eng in NO_SELF_WAIT or not SAME_ENGINE_WAITS):
                continue
            if wd.get(k, 0) < v:
                wd[k] = v
                waits.append((k, v))
        return waits

    def _commit(self, ev, reads, writes):
        k, v = ev
        for r in reads:
            if r.readers.get(k, 0) < v:
                r.readers[k] = v
            r.dirty_read = True
        for w in writes:
            if w.dirty_read:
                w.writers = {}
                w.readers = {}
                w.dirty_read = False
            if w.writers.get(k, 0) < v:
                w.writers[k] = v

    def op(self, eng, fn, R=(), W=()):
        waits = self._collect(eng, R, W)
        self.semcnt[eng] += 1
        self._commit((eng, self.semcnt[eng]), R, W)
        self.streams[eng].append((waits, fn, (eng, 1)))

    def dma(self, out, in_, R=(), W=(), owner=None, q="sp"):
        key = owner.semkey
        for w in W:
            if w.is_dram:
                key = key + "_" + w.name
        if key not in self.semh:
            self.new_sem(key)
        waits = self._collect(q, R, W)
        self.semcnt[key] += 16
        self._commit((key, self.semcnt[key]), R, W)
        self.streams[q].append((waits, lambda e: e.dma_start(out=out, in_=in_), (key, 16)))

    def custom(self, eng, fn, key, inc, R=(), W=()):
        if key not in self.semh:
            self.new_sem(key)
        waits = self._collect(eng, R, W)
        self.semcnt[key] += inc
        self._commit((key, self.semcnt[key]), R, W)
        self.streams[eng].append((waits, fn, (key, inc)))

    def close(self):
        nc = self.nc
        semh = self.semh
        streams = self.streams
        final = [(k, v) for k, v in self.semcnt.items() if v > 0]
        for eng in ("sp", "pool"):
            streams[eng].append(([(k, v) for k, v in final if self.waited[eng].get(k, 0) < v], None, None))

        def run(engobj, lst):
            for waits, fn, inc in lst:
                for k, v in waits:
                    engobj.wait_ge(semh[k], v)
                if fn is not None:
                    fn(engobj).then_inc(semh[inc[0]], inc[1])

        with nc.Block() as block:
            @block.tensor
            def _(e):
                run(e, streams["pe"])

            @block.scalar
            def _(e):
                run(e, streams["act"])

            @block.vector
            def _(e):
                run(e, streams["dve"])

            @block.gpsimd
            def _(e):
                run(e, streams["pool"])

            @block.sync
            def _(e):
                run(e, streams["sp"])
        nc.all_engine_barrier()
        nc.clear_and_free_semaphores(list(self.semh.values()))
        nc.all_engine_barrier()
        self.stack.close()
        for b in self.persist:
            b.reset()

    def act(self, out, in_, func, R, W, scale=1.0, bias=0.0, accum=None):
        if accum is None:
            self.op("act", lambda e: e.activation(out=out, in_=in_, func=func, bias=bias, scale=scale), R, W)
        else:
            self.op("act", lambda e: e.activation(out=out, in_=in_, func=func, bias=bias, scale=scale,
                                                  accum_out=accum), R, W)

    def ts(self, out, in0, s1, s2, op0, op1, R, W, eng="dve"):
        if s2 is None:
            self.op(eng, lambda e: e.tensor_scalar(out=out, in0=in0, scalar1=s1, scalar2=None, op0=op0), R, W)
        else:
            self.op(eng, lambda e: e.tensor_scalar(out=out, in0=in0, scalar1=s1, scalar2=s2, op0=op0, op1=op1), R, W)

    def tt(self, out, in0, in1, op, R, W, eng="dve"):
        self.op(eng, lambda e: e.tensor_tensor(out=out, in0=in0, in1=in1, op=op), R, W)

    def stt(self, out, in0, scalar, in1, op0, op1, R, W):
        self.op("dve", lambda e: e.scalar_tensor_tensor(out=out, in0=in0, scalar=scalar, in1=in1, op0=op0, op1=op1),
                R, W)

    def copy(self, out, in_, R, W, eng="dve"):
        self.op(eng, lambda e: e.tensor_copy(out=out, in_=in_), R, W)

    def mm(self, out, lhsT, rhs, start, stop, R, W):
        self.op("pe", lambda e: e.matmul(out=out, lhsT=lhsT, rhs=rhs, start=start, stop=stop, skip_group_check=True),
                R, W)

    def tr(self, out, in_, ident, R, W):
        self.op("pe", lambda e: e.transpose(out=out, in_=in_, identity=ident), R, W)


class Rot:
    def __init__(self, bufs):
        self.bufs = bufs
        self.i = 0

    def next(self):
        b = self.bufs[self.i % len(self.bufs)]
        self.i += 1
        return b


def build_program(stop_after=None, dbg=False, opts=None):
    opts = opts or {}
    NHD = opts.get('heads', NH)
    SEC = opts.get('sections', 'hg,rt,da,mg')
    nc = bass.Bass("TRN2", target_bir_lowering=False)
    persist = []
    pstack = contextlib.ExitStack()

    def dram(name, shape, dtype, kind="Internal"):
        if dbg and kind == "Internal" and name in opts.get("dump", ()):
            kind = "ExternalOutput"
        b = Buf(name, nc.dram_tensor(name, list(shape), dtype, kind=kind), is_dram=True)
        persist.append(b)
        return b

    def psb(name, shape, dtype):
        b = Buf(name, pstack.enter_context(nc.sbuf_tensor(name, list(shape), dtype)))
        persist.append(b)
        return b

    EI = "ExternalInput"
    x_in = dram("x", [TL, D], F32, EI)
    ctx_in = dram("ctx", [TCX, D], F32, EI)
    cvec_d = dram("cvec", [128, KC * 2], F32, EI)
    wmod_d = dram("w_mod", [2, D, 3 * D], F32, EI)
    bmod_d = dram("bmod_fm", [128, 2 * 48], F32, EI)
    gpre_d = dram("gpre_fm", [128, 2 * KC], F32, EI)
    gpost_d = dram("gpost_fm", [128, 2 * KC], F32, EI)
    win_d = dram("w_in", [2, D, IN_COLS], F32, EI)
    hgl_d = dram("hgl_fm", [128, 2 * 2 * NH], F32, EI)
    rdec_d = dram("rdec_b", [128, 32], F32, EI)
    dlam_d = dram("dlam_b", [128, 512], F32, EI)
    gbr_d = dram("gbr_fm", [128, 2 * 3 * NH], F32, EI)
    wbr_d = dram("w_branch", [2, 3, 1024, D], F32, EI)
    wout_d = dram("w_out", [2, D, D], F32, EI)
    ident_d = dram("ident", [128, 128], BF16, EI)
    identf_d = dram("identf", [128, 128], F32, EI)
    maskf_d = dram("maskF", [128, 128], I32, EI)
    maskb_d = dram("maskB", [128, 128], I32, EI)
    permrt_d = dram("perm_rt", [128, 128], BF16, EI)
    permda_d = dram("perm_da", [128, 128], BF16, EI)
    tabs_d = dram("tabs", [4, 128, TT], BF16, EI)
    tau_d = dram("tau", [128, 512], F32, EI)
    flags_d = dram("flags", [128, 2], F32, EI)
    out_d = dram("out", [TL, D], F32, "ExternalOutput")

    x1_d = dram("x1", [TL, D], F32)
    c1_d = dram("c1", [TCX, D], F32)
    spf_d = dram("spf", [13, NH, 128, TT], BF16)
    spkp_d = dram("spkp", [2, 2, NH, TT, 128], BF16)
    spv_d = dram("spv", [3, NH, TT, 128], BF16)
    spdec_d = dram("spdec", [2, 2, NH, 128, 2 * NCH], F32)
    gm_d = dram("gm", [48, 128, TT], BF16)
    ys_d = dram("ys", [24, 128, TT], BF16)
    xk_in = [dram("xk_in%d" % i, [4 * 128, TL], BF16) for i in range(2)]
    xk_out = [dram("xk_out%d" % i, [2 * 4 * 128, TL], BF16) for i in range(2)]
    xv_in = [dram("xv_in%d" % i, [4 * TL, 128], BF16) for i in range(2)]
    xv_out = [dram("xv_out%d" % i, [2 * 4 * TL, 128], BF16) for i in range(2)]
    xs_in = dram("xs_in", [2 * 2 * NH * 128, 128], F32)
    xs_out = dram("xs_out", [2 * 2 * 2 * NH * 128, 128], F32)
    wbrb_d = dram("wbrb", [3, KC, 128, 8 * 128], BF16)
    woutb_d = dram("woutb", [128, KC * D], BF16)
    dbg_d = dram("dbg", [128, TT], F32, "ExternalOutput") if dbg else None

    ident = psb("ident_s", [128, 128], BF16)
    identf = psb("identf_s", [128, 128], F32)
    maskF = psb("maskF_s", [128, 128], I32)
    maskB = psb("maskB_s", [128, 128], I32)
    ones_b = psb("ones_b", [128, 128], BF16)
    onesm_b = psb("onesm_b", [128, 128], BF16)
    flags = psb("flags_s", [128, 2], F32)
    oms = psb("oms", [128, 2], F32)
    Avec = psb("Avec", [128, 2, 2, KC], F32)
    Bvec = psb("Bvec", [128, 2, 2, KC], F32)
    GGvec = psb("GGvec", [128, 2, 2, KC], F32)
    LBF = psb("LBF", [128, 2, 2, NH], F32)
    OML = psb("OML", [128, 2, 2, NH], F32)
    NOML = psb("NOML", [128, 2, 2, NH], F32)
    LD = psb("LD", [128, 5, 32], F32)
    NLAM = psb("NLAM", [128, 2], F32)
    GBR = psb("GBR", [128, 2, 3, NH], F32)

    def stopped(tag):
        return stop_after is not None and tag == stop_after

    S = Phase(nc, persist)
    cv = S.sbuf("cv", [128, KC, 2], F32)
    cact = S.sbuf("cact", [128, KC, 2], F32)
    bm = S.sbuf("bm", [128, 2, 48], F32)
    gpre = S.sbuf("gpre", [128, 2, KC], F32)
    gpost = S.sbuf("gpost", [128, 2, KC], F32)
    hgl = S.sbuf("hgl", [128, 2, 2, NH], F32)
    rdec = S.sbuf("rdec", [128, 32], F32)
    dlam = S.sbuf("dlam", [128, 2, 4, 64], F32)
    tmp0 = S.sbuf("tmp0", [128, 64], F32)
    sm0 = S.sbuf("sm0", [128, 4], F32)
    modT = S.sbuf("modT", [128, 2, 48, 2], F32)
    wm = Rot([S.sbuf("wm%d" % i, [128, 3 * D], F32) for i in range(2)])
    psm = S.psum("psm", [128, 512], F32)

    def ld(dst, src_ap, srcb, dst_ap=None):
        S.dma(dst[:] if dst_ap is None else dst_ap, src_ap, R=[srcb], W=[dst], owner=dst)
    ld(ident, ident_d[:, :], ident_d)
    ld(identf, identf_d[:, :], identf_d)
    ld(maskF, maskf_d[:, :], maskf_d)
    ld(maskB, maskb_d[:, :], maskb_d)
    ld(flags, flags_d[:, :], flags_d)
    ld(cv, cvec_d[:, :], cvec_d, cv[:].rearrange("p a b -> p (a b)"))
    ld(bm, bmod_d[:, :], bmod_d, bm[:].rearrange("p a b -> p (a b)"))
    ld(gpre, gpre_d[:, :], gpre_d, gpre[:].rearrange("p a b -> p (a b)"))
    ld(gpost, gpost_d[:, :], gpost_d, gpost[:].rearrange("p a b -> p (a b)"))
    ld(hgl, hgl_d[:, :], hgl_d, hgl[:].rearrange("p a b c -> p (a b c)"))
    ld(rdec, rdec_d[:, :], rdec_d)
    ld(dlam, dlam_d[:, :], dlam_d, dlam[:].rearrange("p a b c -> p (a b c)"))
    ld(GBR, gbr_d[:, :], gbr_d, GBR[:].rearrange("p a b c -> p (a b c)"))
    S.op("pool", lambda e: e.memset(ones_b[:], 1.0), W=[ones_b])
    S.op("pool", lambda e: e.memset(onesm_b[:], 1.0 / 128), W=[onesm_b])
    S.ts(oms[:], flags[:], -1.0, 1.0, ALU.mult, ALU.add, [flags], [oms])
    S.act(cact[:], cv[:], AF.Sigmoid, [cv], [cact])
    S.tt(cact[:], cact[:], cv[:], ALU.mult, [cv, cact], [cact])
    S.op("pool", lambda e: e.memset(LBF[:, 0], 1e-20), W=[LBF])
    S.op("pool", lambda e: e.memset(OML[:, 0], 1.0), W=[OML])
    S.tt(OML[:, 1], hgl[:, :, 1, :], hgl[:, :, 0, :], ALU.subtract, [hgl], [OML])
    S.act(LBF[:, 1], OML[:, 1], AF.Sigmoid, [OML], [LBF])
    S.ts(OML[:, 1], LBF[:, 1], -1.0, 1.0, ALU.mult, ALU.add, [LBF], [OML])
    S.ts(LBF[:, 1], LBF[:, 1], 1e-20, None, ALU.max, None, [LBF], [LBF])
    S.ts(NOML[:], OML[:], -1.0, None, ALU.mult, None, [OML], [NOML])
    S.act(LD[:, 1, :], rdec[:], AF.Exp, [rdec], [LD], scale=-1.0)
    S.act(LD[:, 1, :], LD[:, 1, :], AF.Ln, [LD], [LD], bias=1.0)
    S.ts(LD[:, 0, :], LD[:, 1, :], -1.0, None, ALU.mult, None, [LD], [LD])
    S.ts(LD[:, 2, :], LD[:, 0, :], 63.0, None, ALU.mult, None, [LD], [LD])
    S.ts(LD[:, 3, :], LD[:, 0, :], 64.0, None, ALU.mult, None, [LD], [LD])
    S.ts(LD[:, 4, :], LD[:, 0, :], -64.0, None, ALU.mult, None, [LD], [LD])
    for L in range(2):
        for j in range(2):
            S.tt(tmp0[:], dlam[:, L, 2 * j, :], dlam[:, L, 2 * j + 1, :], ALU.mult, [dlam], [tmp0])
            S.op("dve", lambda e, L=L, j=j: e.reduce_sum(out=sm0[:, 2 * L + j:2 * L + j + 1], in_=tmp0[:], axis=AX.X),
                 [tmp0], [sm0])
    S.act(sm0[:], sm0[:], AF.Exp, [sm0], [sm0])
    for L in range(2):
        S.tt(NLAM[:, L:L + 1], sm0[:, 2 * L + 1:2 * L + 2], sm0[:, 2 * L:2 * L + 1], ALU.subtract, [sm0], [NLAM])
        S.ts(NLAM[:, L:L + 1], NLAM[:, L:L + 1], -LAM_INIT[L], None, ALU.add, None, [NLAM], [NLAM])
        S.ts(GBR[:, L, 1, :], GBR[:, L, 1, :], 1.0 - LAM_INIT[L], None, ALU.mult, None, [GBR], [GBR])
    for L in range(2):
        for kc in range(KC):
            w = wm.next()
            S.dma(w[:, 0:3072], wmod_d[L, kc * 128:(kc + 1) * 128, 0:3072], R=[wmod_d], W=[w], owner=w)
            S.dma(w[:, 3072:6144], wmod_d[L, kc * 128:(kc + 1) * 128, 3072:6144], R=[wmod_d], W=[w], owner=w)
            for cc in range(48):
                S.mm(psm[:, 2 * cc:2 * cc + 2], w[:, cc * 128:(cc + 1) * 128], cact[:, kc, :],
                     (kc == 0 and cc == 0), (kc == KC - 1), [w, cact], [psm])
        for j in range(2):
            S.tt(modT[:, L, :, j], psm[:, 0:96].rearrange("p (a b) -> p a b", b=2)[:, :, j], bm[:, L, :], ALU.add,
                 [psm, bm], [modT])
        for j in range(2):
            S.stt(Avec[:, L, j, :], modT[:, L, 16:32, j], 1.0, gpre[:, L, :], ALU.add, ALU.mult, [modT, gpre], [Avec])
            S.copy(Bvec[:, L, j, :], modT[:, L, 0:16, j], [modT], [Bvec])
            S.tt(GGvec[:, L, j, :], modT[:, L, 32:48, j], gpost[:, L, :], ALU.mult, [modT, gpost], [GGvec])
    S.close()

    for L in opts.get('layers', range(2)):
        xsrc = x_in if (L == 0 or 'layers' in opts) else x1_d
        csrc = ctx_in if (L == 0 or 'layers' in opts) else c1_d
        lstack = contextlib.ExitStack()
        hT = Buf("hT", lstack.enter_context(nc.sbuf_tensor("hT%d_%d" % (L, _UID[0]), [128, KC, TT], BF16)))
        persist.append(hT)

        S = Phase(nc, persist)
        xt = Rot([S.sbuf("xt%d" % i, [128, D], F32) for i in range(2)])
        xn = Rot([S.sbuf("xn%d" % i, [128, D], BF16) for i in range(2)])
        junk = S.sbuf("junk", [128, D], BF16)
        ssq = Rot([S.sbuf("ssq%d" % i, [128, 1], F32) for i in range(2)])
        pst = Rot([S.psum("pst%d" % i, [128, 1024], BF16) for i in range(4)])
        for i in range(NPAIR):
            a = xt.next()
            if i < 2:
                S.dma(a[:], csrc[i * 128:(i + 1) * 128, :], R=[csrc], W=[a], owner=a)
            else:
                S.dma(a[:], xsrc[(i - 2) * 128:(i - 1) * 128, :], R=[xsrc], W=[a], owner=a)
            s = ssq.next()
            S.act(junk[:], a[:], AF.Square, [a], [junk, s], accum=s[:])
            S.ts(s[:], s[:], 1.0 / D, EPS, ALU.mult, ALU.add, [s], [s])
            S.act(s[:], s[:], AF.Sqrt, [s], [s])
            S.op("dve", lambda e, s=s: e.reciprocal(out=s[:], in_=s[:]), [s], [s])
            b = xn.next()
            S.ts(b[:], a[:], s[:], None, ALU.mult, None, [a, s], [b])
            j = 1 if i < 2 else 0
            for q in range(4):
                p = pst.next()
                for r in range(4):
                    kc = q * 4 + r
                    S.tr(p[:, r * 128:(r + 1) * 128], b[:, kc * 128:(kc + 1) * 128], ident[:], [b, ident], [p])
                for r in range(4):
                    kc = q * 4 + r
                    S.act(hT[:, kc, i * 128:(i + 1) * 128], p[:, r * 128:(r + 1) * 128], AF.Identity, [p, Avec, Bvec],
                          [hT], scale=Avec[:, L, j, kc:kc + 1], bias=Bvec[:, L, j, kc:kc + 1])
        S.close()
        if stopped("S1_%d" % L):
            lstack.close()
            break

        S = Phase(nc, persist)
        tabt = Rot([S.sbuf("tabt%d" % i, [128, 512], BF16) for i in range(4)])
        tau = S.sbuf("tau_s", [128, 64], F32)
        ones64 = S.sbuf("ones64", [128, 64], F32)
        S.op("pool", lambda e: e.memset(ones64[:], 1.0), W=[ones64])
        permrt = S.sbuf("permrt", [128, 128], BF16)
        permda = S.sbuf("permda", [128, 128], BF16)
        S.dma(tau[:], tau_d[:, 0:64], R=[tau_d], W=[tau], owner=tau)
        S.dma(permrt[:], permrt_d[:, :], R=[permrt_d], W=[permrt], owner=permrt)
        S.dma(permda[:], permda_d[:, :], R=[permda_d], W=[permda], owner=permda)
        wst = Rot([S.sbuf("wst%d" % i, [128, KC, 128], F32) for i in range(2)])
        wbf = Rot([S.sbuf("wbf%d" % i, [128, KC, 128], BF16) for i in range(2)])
        wstM = Rot([S.sbuf("wstM%d" % i, [128, KC, 128], F32) for i in range(1)])
        wbfM = Rot([S.sbuf("wbfM%d" % i, [128, KC, 128], BF16) for i in range(2)])
        sig = [S.sbuf("sig%d" % i, [128, TT], F32) for i in range(2)]
        zq = S.sbuf("zq", [128, TT], BF16)
        zr = zq
        rq = S.sbuf("rq", [128, TT], BF16)
        rk = S.sbuf("rk", [128, TT], BF16)
        t512 = Rot([S.sbuf("t512_%d" % i, [128, 512], F32) for i in range(7)])
        o512 = Rot([S.sbuf("o512_%d" % i, [128, 512], BF16) for i in range(8)])
        o2304 = Rot([S.sbuf("o2304_%d" % i, [128, TT], BF16) for i in range(2)])
        etab = [S.sbuf("etab%d" % i, [128, 64], F32) for i in range(6)]
        vtok = Rot([S.sbuf("vtok%d" % i, [128, NPAIR, 128], BF16) for i in range(2)])
        kpt = [Rot([S.sbuf("kpt%d_%d" % (d_, i), [128, NPAIR, 128], BF16) for i in range(1)]) for d_ in range(2)]
        dec = [Rot([S.sbuf("dec%d_%d" % (d_, i), [128, 2 * NCH], F32) for i in range(2)]) for d_ in range(2)]
        ntot = Rot([S.sbuf("ntot%d" % i, [128, 8], F32) for i in range(4)])
        Sst = [[S.sbuf("Sst%d_%d" % (d_, i), [128, 128], F32) for i in range(2)] for d_ in range(2)]
        psA = Rot([S.psum("psA%d" % i, [128, 512], F32) for i in range(4)])
        psM = Rot([S.psum("psM%d" % i, [128, 512], F32) for i in range(2)])
        psT = Rot([S.psum("psT%d" % i, [128, 1024], BF16) for i in range(2)])

        def load_w(col0, wst=wst, wbf=wbf):
            a = wst.next()
            S.dma(a[:, 0:8, :], win_d[L, 0:1024, col0:col0 + 128].rearrange("(k p) c -> p k c", p=128),
                  R=[win_d], W=[a], owner=a, q="act")
            S.dma(a[:, 8:16, :], win_d[L, 1024:2048, col0:col0 + 128].rearrange("(k p) c -> p k c", p=128),
                  R=[win_d], W=[a], owner=a, q="act")
            b = wbf.next()
            S.copy(b[:], a[:], [a], [b], eng="pool")
            return b

        def proj_fm(w, evac):
            for (t0, wd) in TG:
                p = psA.next()
                for kc in range(KC):
                    S.mm(p[:, 0:wd], w[:, kc, :], hT[:, kc, t0:t0 + wd], kc == 0, kc == KC - 1, [w, hT], [p])
                evac(p, t0, wd)

        def proj_tm(w, dst):
            for i4 in range(0, NPAIR, 4):
                n = min(4, NPAIR - i4)
                p = psA.next()
                for r in range(n):
                    i = i4 + r
                    for kc in range(KC):
                        S.mm(p[:, r * 128:(r + 1) * 128], hT[:, kc, i * 128:(i + 1) * 128], w[:, kc, :],
                             (kc == 0 and r == 0), kc == KC - 1, [w, hT], [p])
                S.copy(dst[:, i4:i4 + n, :], p[:, 0:n * 128].rearrange("p (a b) -> p a b", b=128), [p], [dst])

        def spill_fm(kind, h, src, t0=0, wd=TT, src_ap=None):
            S.dma(spf_d[kind, h, :, t0:t0 + wd], src[:, 0:wd] if src_ap is None else src_ap,
                  R=[src], W=[spf_d], owner=src)

        def kp_transpose(src, t0, wd, dst):
            p = psT.next()
            n = wd // 128
            for r in range(n):
                S.tr(p[:, r * 128:(r + 1) * 128], src[:, r * 128:(r + 1) * 128], ident[:], [src, ident], [p])
            S.copy(dst[:, t0 // 128:t0 // 128 + n, :], p[:, 0:n * 128].rearrange("p (a b) -> p a b", b=128),
                   [p], [dst])

        def rope(dst, tcos, tsin, perm, post_scale=None):
            for (t0, wd) in TG:
                p = psA.next()
                S.mm(p[:, 0:wd], perm[:], zr[:, t0:t0 + wd], True, True, [perm, zr], [p])
                tc_ = tabt.next()
                S.dma(tc_[:, 0:wd], tabs_d[tcos, :, t0:t0 + wd], R=[tabs_d], W=[tc_], owner=tc_, q="act")
                tsn = tabt.next()
                S.dma(tsn[:, 0:wd], tabs_d[tsin, :, t0:t0 + wd], R=[tabs_d], W=[tsn], owner=tsn, q="act")
                a = t512.next()
                S.tt(a[:, 0:wd], zr[:, t0:t0 + wd], tc_[:, 0:wd], ALU.mult, [zr, tc_], [a])
                b = t512.next()
                S.tt(b[:, 0:wd], p[:, 0:wd], tsn[:, 0:wd], ALU.mult, [p, tsn], [b])
                if post_scale is None:
                    S.tt(dst[:, t0:t0 + wd], a[:, 0:wd], b[:, 0:wd], ALU.add, [a, b], [dst], eng="pool")
                else:
                    S.tt(a[:, 0:wd], a[:, 0:wd], b[:, 0:wd], ALU.add, [a, b], [a], eng="pool")
                    S.ts(dst[:, t0:t0 + wd], a[:, 0:wd], post_scale, None, ALU.mult, None, [a], [dst], eng="pool")

        def run1(mix, h, kp, v, dc):
            orders = [list(range(NCH)), [3, 2, 1, 0] + list(range(NCH - 1, 3, -1))]
            cur = [0, 0]
            for d_ in range(2):
                S.op("pool", lambda e, d_=d_: e.memset(Sst[d_][0][:], 0.0), W=[Sst[d_][0]])
            for st in range(0, NCH, 4):
                if (st // 4) % 2 == 0:
                    bgstep()
                for d_ in range(2):
                    pp = [psA.next(), psA.next()]
                    for r in range(4):
                        j = orders[d_][st + r]
                        pr, hf = j // 2, (j % 2) * 64
                        p = pp[j % 2]
                        S.mm(p[:, r * 128:(r + 1) * 128], kp[d_][hf:hf + 64, pr, :], v[hf:hf + 64, pr, :],
                             r < 2, True, [kp[d_], v], [p])
                    for r in range(4):
                        j = orders[d_][st + r]
                        p = pp[j % 2]
                        a, b = Sst[d_][cur[d_]], Sst[d_][1 - cur[d_]]
                        S.stt(b[:], a[:], dc[d_][:, j:j + 1], p[:, r * 128:(r + 1) * 128], ALU.mult, ALU.add,
                              [a, dc[d_], p], [b])
                        cur[d_] = 1 - cur[d_]
            for d_ in range(2):
                a = Sst[d_][cur[d_]]
                r0 = ((mix * 2 + d_) * NH + h) * 128
                S.dma(xs_in[r0:r0 + 128, :], a[:], R=[a], W=[xs_in], owner=a)

        jobs = []
        for h in range(NH):
            for g in (HG_FF, HG_FB, HG_Q, HG_I, HG_G, RT_Q, RT_K, RT_V, RT_G, DA_Q, DA_K, DA_V, DA_G):
                jobs.append(g * 1024 + h * 128)
        wq = []

        def merge_gen():
            if 'mg' not in SEC:
                return
            wcur = load_w(MERGE0, wstM, wbfM)
            pend = None

            def evac(pd):
                p, t0, wd, fc = pd
                o = o512.next()
                S.copy(o[:, 0:wd], p[:, 0:wd], [p], [o])
                S.dma(gm_d[fc, :, t0:t0 + wd], o[:, 0:wd], R=[o], W=[gm_d], owner=o)
            for fc in range(48):
                w = wcur
                if fc + 1 < 48:
                    wcur = load_w(MERGE0 + (fc + 1) * 128, wstM, wbfM)
                for (t0, wd) in TG:
                    p = psM.next()
                    for kc in range(KC):
                        S.mm(p[:, 0:wd], w[:, kc, :], hT[:, kc, t0:t0 + wd], kc == 0, kc == KC - 1, [w, hT], [p])
                    if pend is not None:
                        evac(pend)
                    pend = (p, t0, wd, fc)
                    yield
            evac(pend)
        bg = merge_gen()

        def bgstep(n=1):
            for _ in range(n):
                try:
                    next(bg)
                except StopIteration:
                    return

        def next_w():
            if not wq and jobs:
                wq.append(load_w(jobs.pop(0)))
            w = wq.pop(0)
            if jobs:
                wq.append(load_w(jobs.pop(0)))
            return w

        for h in range(NHD):
            li = L * 16
            for d_ in range(2):
                w = next_w()
                proj_fm(w, lambda p, t0, wd, d_=d_: S.act(sig[d_][:, t0:t0 + wd], p[:, 0:wd], AF.Sigmoid, [p], [sig[d_]]))
            w = next_w()
            proj_fm(w, lambda p, t0, wd: S.act(zq[:, t0:t0 + wd], p[:, 0:wd], AF.Copy, [p], [zq], scale=128 ** -0.5))
            w = next_w()
            vt = vtok.next()
            proj_tm(w, vt)
            S.dma(spv_d[0, h].rearrange("(a p) c -> p a c", p=128), vt[:], R=[vt], W=[spv_d], owner=vt)
            kp = [kpt[0].next(), kpt[1].next()]
            dc = [dec[0].next(), dec[1].next()]
            if opts.get('upto') == 'hg_proj':
                continue
            for (t0, wd) in TG:
                nch = wd // 64
                c0 = t0 // 64
                for d_ in range(2):
                    bgstep()
                    oml = OML[:, L, d_, h:h + 1]
                    noml = NOML[:, L, d_, h:h + 1]
                    lbf = LBF[:, L, d_, h:h + 1]
                    fg = t512.next()
                    S.ts(fg[:, 0:wd], sig[d_][:, t0:t0 + wd], oml, lbf, ALU.mult, ALU.add, [sig[d_], OML, LBF], [fg])
                    kk = t512.next()
                    S.ts(kk[:, 0:wd], sig[d_][:, t0:t0 + wd], noml, oml, ALU.mult, ALU.add, [sig[d_], OML, NOML], [kk])
                    S.act(fg[:, 0:wd], fg[:, 0:wd], AF.Ln, [fg], [fg])
                    cum = t512.next()
                    for c in range(nch):
                        S.op("dve", lambda e, c=c, cum=cum, fg=fg: e.tensor_tensor_scan(
                            out=cum[:, c * 64:(c + 1) * 64], data0=ones64[:], data1=fg[:, c * 64:(c + 1) * 64],
                            initial=0.0, op0=ALU.mult, op1=ALU.add), [fg, ones64], [cum])
                    tot = cum[:, 0:wd].rearrange("p (a b) -> p a b", b=64)[:, :, 63]
                    S.act(dc[d_][:, c0:c0 + nch], tot, AF.Exp, [cum], [dc[d_]])
                    e1 = t512.next()
                    e2 = t512.next()
                    oq = o512.next()
                    ok = o512.next()
                    okp = o512.next()
                    if d_ == 0:
                        midv = cum[:, 0:wd].rearrange("p (a b) -> p a b", b=64)[:, :, 31]
                        nm = ntot.next()
                        S.ts(nm[:, 0:nch], midv, -1.0, None, ALU.mult, None, [cum], [nm])
                        S.act(dc[d_][:, NCH + c0:NCH + c0 + nch], midv, AF.Exp, [cum], [dc[d_]])
                        for c in range(nch):
                            S.act(e1[:, c * 64:(c + 1) * 64], cum[:, c * 64:(c + 1) * 64], AF.Exp, [cum, nm], [e1],
                                  bias=nm[:, c:c + 1])
                            S.act(e2[:, c * 64:(c + 1) * 64], cum[:, c * 64:(c + 1) * 64], AF.Exp, [cum], [e2],
                                  scale=-1.0, bias=cum[:, c * 64 + 31:c * 64 + 32])
                        S.tt(oq[:, 0:wd], zq[:, t0:t0 + wd], e1[:, 0:wd], ALU.mult, [zq, e1], [oq])
                        S.tt(ok[:, 0:wd], kk[:, 0:wd], e2[:, 0:wd], ALU.mult, [kk, e2], [ok])
                        e3 = t512.next()
                        for c in range(nch):
                            S.act(e3[:, c * 64:(c + 1) * 64], cum[:, c * 64:(c + 1) * 64], AF.Exp, [cum], [e3],
                                  scale=-1.0, bias=cum[:, c * 64 + 63:c * 64 + 64])
                        S.tt(okp[:, 0:wd], kk[:, 0:wd], e3[:, 0:wd], ALU.mult, [kk, e3], [okp])
                    else:
                        u = t512.next()
                        S.stt(u[:, 0:wd], cum[:, 0:wd], -1.0, fg[:, 0:wd], ALU.mult, ALU.add, [cum, fg], [u])
                        umid = u[:, 0:wd].rearrange("p (a b) -> p a b", b=64)[:, :, 32]
                        nm = ntot.next()
                        S.ts(nm[:, 0:nch], umid, -1.0, None, ALU.mult, None, [u], [nm])
                        em = ntot.next()
                        S.tt(em[:, 0:nch], umid, tot, ALU.add, [u, cum], [em])
                        S.act(dc[d_][:, NCH + c0:NCH + c0 + nch], em[:, 0:nch], AF.Exp, [em], [dc[d_]])
                        for c in range(nch):
                            S.act(e1[:, c * 64:(c + 1) * 64], u[:, c * 64:(c + 1) * 64], AF.Exp, [u, nm], [e1],
                                  bias=nm[:, c:c + 1])
                            S.act(e2[:, c * 64:(c + 1) * 64], u[:, c * 64:(c + 1) * 64], AF.Exp, [u], [e2],
                                  scale=-1.0, bias=u[:, c * 64 + 32:c * 64 + 33])
                        S.tt(oq[:, 0:wd], zq[:, t0:t0 + wd], e1[:, 0:wd], ALU.mult, [zq, e1], [oq])
                        S.tt(ok[:, 0:wd], kk[:, 0:wd], e2[:, 0:wd], ALU.mult, [kk, e2], [ok])
                        e3 = t512.next()
                        S.act(e3[:, 0:wd], u[:, 0:wd], AF.Exp, [u], [e3], scale=-1.0)
                        S.tt(okp[:, 0:wd], kk[:, 0:wd], e3[:, 0:wd], ALU.mult, [kk, e3], [okp])
                    spill_fm(K_HGQF + d_, h, oq, t0, wd)
                    spill_fm(K_HGKF + d_, h, ok, t0, wd)
                    kp_transpose(okp, t0, wd, kp[d_])
            for d_ in range(2):
                S.dma(spkp_d[0, d_, h].rearrange("(a p) c -> p a c", p=128), kp[d_][:], R=[kp[d_]], W=[spkp_d],
                      owner=kp[d_])
                S.dma(spdec_d[0, d_, h], dc[d_][:], R=[dc[d_]], W=[spdec_d], owner=dc[d_])
            if opts.get('upto') == 'hg_prep':
                continue
            run1(0, h, kp, vt, dc)
            w = next_w()
            og = o2304.next()
            proj_fm(w, lambda p, t0, wd, og=og: S.act(og[:, t0:t0 + wd], p[:, 0:wd], AF.Silu, [p], [og]))
            spill_fm(K_HGG, h, og)
            if opts.get('upto') == 'hg':
                continue
            w = next_w()
            proj_fm(w, lambda p, t0, wd: S.act(zr[:, t0:t0 + wd], p[:, 0:wd], AF.Copy, [p], [zr]))
            rope(rq, 0, 1, permrt)
            w = next_w()
            proj_fm(w, lambda p, t0, wd: S.act(zr[:, t0:t0 + wd], p[:, 0:wd], AF.Copy, [p], [zr], scale=128 ** -0.5))
            rope(rk, 0, 1, permrt)
            w = next_w()
            vt = vtok.next()
            proj_tm(w, vt)
            S.dma(spv_d[2, h].rearrange("(a p) c -> p a c", p=128), vt[:], R=[vt], W=[spv_d], owner=vt)
            kp = [kpt[0].next(), kpt[1].next()]
            dc = [dec[0].next(), dec[1].next()]
            for d_ in range(2):
                ix = li + d_ * 8 + h
                ldc, nld, ld63, ld64, nld64 = [LD[:, q, ix:ix + 1] for q in range(5)]
                if d_ == 0:
                    S.act(etab[0][:], tau[:], AF.Exp, [tau, LD], [etab[0]], scale=ldc, bias=ldc)
                    S.act(etab[1][:], tau[:], AF.Exp, [tau, LD], [etab[1]], scale=nld, bias=nld)
                    S.act(etab[2][:], tau[:], AF.Exp, [tau, LD], [etab[2]], scale=nld, bias=ld63)
                else:
                    S.act(etab[3][:], tau[:], AF.Exp, [tau, LD], [etab[3]], scale=nld, bias=ld64)
                    S.act(etab[4][:], tau[:], AF.Exp, [tau, LD], [etab[4]], scale=ldc, bias=nld64)
                    S.act(etab[5][:], tau[:], AF.Exp, [tau, LD], [etab[5]], scale=ldc)
                S.act(dc[d_][:, 0:NCH], tau[:, 0:NCH], AF.Exp, [tau, LD], [dc[d_]], scale=0.0, bias=ld64)
                S.op("pool", lambda e, d_=d_, dc=dc: e.memset(dc[d_][:, NCH:2 * NCH], 1.0), W=[dc[d_]])
                for (t0, wd) in TG:
                    bgstep()
                    oq = o512.next()
                    ok = o512.next()
                    okp = o512.next()
                    nch = wd // 64

                    def v3d(ap):
                        return ap.rearrange("p (a b) -> p a b", b=64)

                    def eb(i):
                        return etab[i][:, :].unsqueeze(1).to_broadcast([128, nch, 64])
                    S.tt(v3d(oq[:, 0:wd]), v3d(rq[:, t0:t0 + wd]), eb(3 * d_), ALU.mult, [rq, etab[3 * d_]], [oq])
                    S.tt(v3d(ok[:, 0:wd]), v3d(rk[:, t0:t0 + wd]), eb(3 * d_ + 1), ALU.mult, [rk, etab[3 * d_ + 1]], [ok])
                    S.tt(v3d(okp[:, 0:wd]), v3d(rk[:, t0:t0 + wd]), eb(3 * d_ + 2), ALU.mult, [rk, etab[3 * d_ + 2]],
                         [okp])
                    spill_fm(K_RTQF + d_, h, oq, t0, wd)
                    spill_fm(K_RTKF + d_, h, ok, t0, wd)
                    kp_transpose(okp, t0, wd, kp[d_])
                S.dma(spkp_d[1, d_, h].rearrange("(a p) c -> p a c", p=128), kp[d_][:], R=[kp[d_]], W=[spkp_d],
                      owner=kp[d_])
                S.dma(spdec_d[1, d_, h], dc[d_][:], R=[dc[d_]], W=[spdec_d], owner=dc[d_])
            run1(1, h, kp, vt, dc)
            w = next_w()
            og = o2304.next()
            proj_fm(w, lambda p, t0, wd, og=og: S.act(og[:, t0:t0 + wd], p[:, 0:wd], AF.Silu, [p], [og]))
            spill_fm(K_RTG, h, og)
            if opts.get('upto') == 'rt':
                continue
            w = next_w()
            proj_fm(w, lambda p, t0, wd: S.act(zr[:, t0:t0 + wd], p[:, 0:wd], AF.Copy, [p], [zr]))
            og = o2304.next()
            rope(og, 2, 3, permda)
            spill_fm(K_DAQ, h, og)
            w = next_w()
            proj_fm(w, lambda p, t0, wd: S.act(zr[:, t0:t0 + wd], p[:, 0:wd], AF.Copy, [p], [zr]))
            og = o2304.next()
            rope(og, 2, 3, permda)
            spill_fm(K_DAK, h, og)
            S.dma(xk_in[h // 4][(h % 4) * 128:(h % 4 + 1) * 128, :], og[:, TCX:TT], R=[og], W=[xk_in[h // 4]], owner=og)
            w = next_w()
            vt = vtok.next()
            proj_tm(w, vt)
            S.dma(spv_d[1, h].rearrange("(a p) c -> p a c", p=128), vt[:], R=[vt], W=[spv_d], owner=vt)
            S.dma(xv_in[h // 4][(h % 4) * TL:(h % 4 + 1) * TL, :].rearrange("(a p) c -> p a c", p=128), vt[:, 2:NPAIR, :],
                  R=[vt], W=[xv_in[h // 4]], owner=vt)
            w = next_w()
            og = o2304.next()
            proj_fm(w, lambda p, t0, wd, og=og: S.act(og[:, t0:t0 + wd], p[:, 0:wd], AF.Silu, [p], [og]))
            spill_fm(K_DAG, h, og)
        for _ in bg:
            pass
        RG = [[0, 1], [2, 3], [4, 5], [6, 7]]
        for (a, b, key) in (((xk_in[0], xk_out[0], "cc1"), (xk_in[1], xk_out[1], "cc1"), (xv_in[0], xv_out[0], "cc1"),
                             (xv_in[1], xv_out[1], "cc1"), (xs_in, xs_out, "cc1")) if not opts.get("no_cc") else ()):
            S.custom("pool", lambda e, a=a, b=b: e.collective_compute("AllGather", ALU.bypass, replica_groups=RG,
                                                                      ins=[a.t.ap().opt()], outs=[b.t.ap().opt()]),
                     key, 1, R=[a], W=[b])
        if opts.get("no_cc"):
            ccb = Buf("ccb")
            for (a, b) in ((xk_in[0], xk_out[0]), (xk_in[1], xk_out[1]), (xv_in[0], xv_out[0]), (xv_in[1], xv_out[1]),
                           (xs_in, xs_out)):
                n = a.t.shape[0]
                for sl in range(2):
                    S.dma(b[sl * n:(sl + 1) * n, :], a[:, :], R=[a], W=[b], owner=ccb)
        S.close()
        lstack.close()
        persist.remove(hT)
        if stopped("S2_%d" % L):
            break

        S = Phase(nc, persist)
        ctx_on = (L == 0)
        qk = [[S.sbuf("qk%d_%d" % (a, d_), [128, TT], BF16) for d_ in range(2)] for a in range(2)]
        kp3 = [S.sbuf("kp3_%d" % d_, [128, NPAIR, 128], BF16) for d_ in range(2)]
        v3 = S.sbuf("v3", [128, NPAIR, 128], BF16)
        g3 = S.sbuf("g3", [128, TT], BF16)
        dc3 = [S.sbuf("dc3_%d" % d_, [128, 2 * NCH], F32) for d_ in range(2)]
        Sall = [S.sbuf("Sall%d" % d_, [128, NCH + 1, 128], BF16) for d_ in range(2)]
        Sst = [[S.sbuf("S3st%d_%d" % (d_, i), [128, 128], F32) for i in range(2)] for d_ in range(2)]
        Rst = [S.sbuf("Rst%d" % d_, [128, 128], F32) for d_ in range(2)]
        ATf = Rot([S.sbuf("ATf%d" % i, [128, 128], BF16) for i in range(4)])
        ATb = Rot([S.sbuf("ATb%d" % i, [128, 128], BF16) for i in range(4)])
        for a_ in ATf.bufs + ATb.bufs:
            S.op("pool", lambda e, a_=a_: e.memset(a_[:], 0.0), W=[a_])
        sq = Rot([S.sbuf("sq%d" % i, [128, 512], BF16) for i in range(2)])
        f512 = Rot([S.sbuf("f512_%d" % i, [128, 512], F32) for i in range(8)])
        yo = Rot([S.sbuf("yo%d" % i, [128, 512], BF16) for i in range(3)])
        kall = S.sbuf("kall", [128, TCX + 2 * TL], BF16)
        vall = S.sbuf("vall", [128, 34, 128], BF16)
        pexp = Rot([S.sbuf("pexp%d" % i, [128, 512], BF16) for i in range(4)])
        qz = [S.sbuf("qz%d" % m, [128, TT], BF16) for m in range(2)]
        S.op("pool", lambda e: e.memset(qz[0][64:128, :], 0.0), W=[qz[0]])
        S.op("pool", lambda e: e.memset(qz[1][0:64, :], 0.0), W=[qz[1]])
        psS = Rot([S.psum("psS%d" % i, [128, 512], F32) for i in range(4)])
        psO = [S.psum("psO%d" % i, [128, 512], F32) for i in range(4)]
        tgs = TG if ctx_on else TG[1:]
        pok = [0]

        def mkset(tag):
            return dict(
                qk=[[S.sbuf("qk%s%d_%d" % (tag, a, d_), [128, TT], BF16) for d_ in range(2)] for a in range(2)],
                kp3=[S.sbuf("kp3%s_%d" % (tag, d_), [128, NPAIR, 128], BF16) for d_ in range(2)],
                v3=S.sbuf("v3%s" % tag, [128, NPAIR, 128], BF16),
                g3=S.sbuf("g3%s" % tag, [128, TT], BF16),
                dc3=[S.sbuf("dc3%s_%d" % (tag, d_), [128, 2 * NCH], F32) for d_ in range(2)],
                Sall=[S.sbuf("Sall%s%d" % (tag, d_), [128, NCH + 1, 128], BF16) for d_ in range(2)],
                Sst=[[S.sbuf("S3st%s%d_%d" % (tag, d_, i), [128, 128], F32) for i in range(2)] for d_ in range(2)],
                Rst=[S.sbuf("Rst%s%d" % (tag, d_), [128, 128], F32) for d_ in range(2)])
        sets = [dict(qk=qk, kp3=kp3, v3=v3, g3=g3, dc3=dc3, Sall=Sall, Sst=Sst, Rst=Rst), mkset("B")]
        gda = S.sbuf("gda", [128, TT], BF16)

        def gla_load(h, mix):
            T = sets[mix]
            kq = (K_HGQF, K_HGKF) if mix == 0 else (K_RTQF, K_RTKF)
            for d_ in range(2):
                S.dma(T["qk"][0][d_][:], spf_d[kq[0] + d_, h], R=[spf_d], W=[T["qk"][0][d_]], owner=T["qk"][0][d_])
                S.dma(T["qk"][1][d_][:], spf_d[kq[1] + d_, h], R=[spf_d], W=[T["qk"][1][d_]], owner=T["qk"][1][d_])
                S.dma(T["kp3"][d_][:], spkp_d[mix, d_, h].rearrange("(a p) c -> p a c", p=128), R=[spkp_d],
                      W=[T["kp3"][d_]], owner=T["kp3"][d_])
                S.dma(T["dc3"][d_][:], spdec_d[mix, d_, h], R=[spdec_d], W=[T["dc3"][d_]], owner=T["dc3"][d_])
                r0 = (((d_ * 2 + mix) * 2 + d_) * NH + h) * 128
                S.dma(T["Rst"][d_][:], xs_out[r0:r0 + 128, :], R=[xs_out], W=[T["Rst"][d_]], owner=T["Rst"][d_])
            S.dma(T["v3"][:], spv_d[0 if mix == 0 else 2, h].rearrange("(a p) c -> p a c", p=128), R=[spv_d],
                  W=[T["v3"]], owner=T["v3"])
            S.dma(T["g3"][:], spf_d[K_HGG if mix == 0 else K_RTG, h], R=[spf_d], W=[T["g3"]], owner=T["g3"])

        def gla_run2(h):
            orders = [list(range(NCH)), [3, 2, 1, 0] + list(range(NCH - 1, 3, -1))]
            cur = [[0, 0], [0, 0]]
            for mix in range(2):
                for d_ in range(2):
                    S.op("pool", lambda e, mix=mix, d_=d_: e.memset(sets[mix]["Sst"][d_][0][:], 0.0),
                         W=[sets[mix]["Sst"][d_][0]])
            for st in range(0, NCH, 4):
                for mix in range(2):
                    T = sets[mix]
                    for d_ in range(2):
                        Sst_, Rst_, dc_ = T["Sst"][d_], T["Rst"][d_], T["dc3"][d_]
                        if st == 4:
                            a, b = Sst_[cur[mix][d_]], Sst_[1 - cur[mix][d_]]
                            sel = flags[:, 0:1] if d_ == 0 else oms[:, 0:1]
                            nsel = oms[:, 0:1] if d_ == 0 else flags[:, 0:1]
                            S.ts(Rst_[:], Rst_[:], sel, None, ALU.mult, None, [Rst_, flags, oms], [Rst_])
                            S.stt(b[:], a[:], nsel, Rst_[:], ALU.mult, ALU.add, [a, Rst_, flags, oms], [b])
                            cur[mix][d_] = 1 - cur[mix][d_]
                        pp = [psS.next(), psS.next()]
                        for r in range(4):
                            j = orders[d_][st + r]
                            pr, hf = j // 2, (j % 2) * 64
                            p = pp[j % 2]
                            S.mm(p[:, r * 128:(r + 1) * 128], T["kp3"][d_][hf:hf + 64, pr, :], T["v3"][hf:hf + 64, pr, :],
                                 r < 2, True, [T["kp3"][d_], T["v3"]], [p])
                        for r in range(4):
                            j = orders[d_][st + r]
                            p = pp[j % 2]
                            a, b = Sst_[cur[mix][d_]], Sst_[1 - cur[mix][d_]]
                            S.act(T["Sall"][d_][:, j, :], a[:], AF.Identity, [a, dc_], [T["Sall"][d_]],
                                  scale=dc_[:, NCH + j:NCH + j + 1])
                            S.stt(b[:], a[:], dc_[:, j:j + 1], p[:, r * 128:(r + 1) * 128], ALU.mult, ALU.add,
                                  [a, dc_, p], [b])
                            cur[mix][d_] = 1 - cur[mix][d_]

        def gla_pass2(h, mix):
            T = sets[mix]
            qk_, v3_, g3_, Sall_ = T["qk"], T["v3"], T["g3"], T["Sall"]
            for (t0, wd) in tgs:
                po = psO[pok[0] % 4]
                pok[0] += 1
                prs = list(range(t0 // 128, (t0 + wd) // 128))
                pend = {}
                first = True
                for idx in range(len(prs) + 1):
                    if idx < len(prs):
                        c0 = prs[idx] * 128
                        ats = []
                        for d_ in range(2):
                            p = psS.next()
                            S.mm(p[:, 0:128], qk_[1][d_][:, c0:c0 + 128], qk_[0][d_][:, c0:c0 + 128], True, True,
                                 [qk_[1][d_], qk_[0][d_]], [p])
                            a = (ATf if d_ == 0 else ATb).next()
                            mk = maskF if d_ == 0 else maskB
                            S.op("dve", lambda e, a=a, p=p, mk=mk: e.copy_predicated(out=a[:], mask=mk[:], data=p[:, 0:128]),
                                 [p, mk], [a])
                            ats.append(a)
                        pend[idx] = ats
                    if idx >= 1:
                        pr = prs[idx - 1]
                        ats = pend.pop(idx - 1)
                        c0 = pr * 128
                        oc = c0 - t0
                        for d_ in range(2):
                            S.mm(po[:, oc:oc + 128], v3_[:, pr, :], ats[d_][:], first, False, [v3_, ats[d_]], [po])
                            first = False
                        for c in range(2):
                            j = pr * 2 + c
                            for d_ in range(2):
                                S.mm(po[:, oc + c * 64:oc + c * 64 + 64], Sall_[d_][:, j, :],
                                     qk_[0][d_][:, c0 + c * 64:c0 + c * 64 + 64], False, (c == 1 and d_ == 1),
                                     [Sall_[d_], qk_[0][d_]], [po])
                br = 0 if mix == 0 else 2
                s_ = sq.next()
                S.act(s_[:, 0:wd], po[:, 0:wd], AF.Square, [po], [s_])
                p = psS.next()
                S.mm(p[:, 0:wd], onesm_b[:], s_[:, 0:wd], True, True, [onesm_b, s_], [p])
                r = f512.next()
                S.act(r[:, 0:wd], p[:, 0:wd], AF.Ln, [p], [r], bias=EPS)
                S.act(r[:, 0:wd], r[:, 0:wd], AF.Exp, [r], [r], scale=-0.5)
                y = f512.next()
                S.tt(y[:, 0:wd], po[:, 0:wd], r[:, 0:wd], ALU.mult, [po, r], [y])
                o = yo.next()
                S.stt(o[:, 0:wd], y[:, 0:wd], GBR[:, L, br, h:h + 1], g3_[:, t0:t0 + wd], ALU.mult, ALU.mult,
                      [y, GBR, g3_], [o])
                S.dma(ys_d[br * 8 + h, :, t0:t0 + wd], o[:, 0:wd], R=[o], W=[ys_d], owner=o)

        def da_load(h):
            for m in range(2):
                S.dma(qz[m][m * 64:(m + 1) * 64, :], spf_d[K_DAQ, h, m * 64:(m + 1) * 64, :], R=[spf_d], W=[qz[m]],
                      owner=qz[m])
            S.dma(gda[:], spf_d[K_DAG, h], R=[spf_d], W=[gda], owner=gda)
            S.dma(kall[:, 0:TCX], spf_d[K_DAK, h, :, 0:TCX], R=[spf_d], W=[kall], owner=kall)
            for sl in range(2):
                S.dma(kall[:, TCX + sl * TL:TCX + (sl + 1) * TL],
                      xk_out[h // 4][(sl * 4 + h % 4) * 128:(sl * 4 + h % 4 + 1) * 128, :],
                      R=[xk_out[h // 4]], W=[kall], owner=kall)
                S.dma(vall[:, 2 + sl * 16:2 + (sl + 1) * 16, :],
                      xv_out[h // 4][(sl * 4 + h % 4) * TL:(sl * 4 + h % 4 + 1) * TL, :].rearrange("(a p) c -> p a c", p=128),
                      R=[xv_out[h // 4]], W=[vall], owner=vall)
            S.dma(vall[:, 0:2, :], spv_d[1, h, 0:TCX, :].rearrange("(a p) c -> p a c", p=128), R=[spv_d], W=[vall],
                  owner=vall)

        gla_load(0, 0)
        gla_load(0, 1)
        for h in range(NH):
            gla_run2(h)
            da_load(h)
            gla_pass2(h, 0)
            gla_pass2(h, 1)
            if h + 1 < NH:
                gla_load(h + 1, 0)
                gla_load(h + 1, 1)
            for (t0, wd) in tgs:
                nkt = 2 if t0 == 0 else 34
                steps = [(kt, m) for kt in range(nkt) for m in range(2)]
                LOOK = 2
                pend = {}
                for i in range(len(steps) + LOOK):
                    if i < len(steps):
                        kt, m = steps[i]
                        p = psS.next()
                        S.mm(p[:, 0:wd], kall[:, kt * 128:(kt + 1) * 128], qz[m][:, t0:t0 + wd], True, True,
                             [kall, qz[m]], [p])
                        e = pexp.next()
                        S.act(e[:, 0:wd], p[:, 0:wd], AF.Exp, [p], [e], scale=0.125)
                        pend[i] = e
                    if i >= LOOK:
                        kt, m = steps[i - LOOK]
                        e = pend.pop(i - LOOK)
                        S.mm(psO[m][:, 0:wd], vall[:, kt, :], e[:, 0:wd], kt == 0, kt == nkt - 1, [vall, e], [psO[m]])
                        S.mm(psO[2 + m][:, 0:wd], ones_b[:], e[:, 0:wd], kt == 0, kt == nkt - 1, [ones_b, e], [psO[2 + m]])
                r1 = f512.next()
                S.act(r1[:, 0:wd], psO[2][:, 0:wd], AF.Ln, [psO[2]], [r1])
                S.act(r1[:, 0:wd], r1[:, 0:wd], AF.Exp, [r1], [r1], scale=-1.0)
                r2 = f512.next()
                S.act(r2[:, 0:wd], psO[3][:, 0:wd], AF.Ln, [psO[3]], [r2])
                S.act(r2[:, 0:wd], r2[:, 0:wd], AF.Exp, [r2], [r2], scale=-1.0)
                S.tt(r1[:, 0:wd], psO[0][:, 0:wd], r1[:, 0:wd], ALU.mult, [psO[0], r1], [r1])
                S.tt(r2[:, 0:wd], psO[1][:, 0:wd], r2[:, 0:wd], ALU.mult, [psO[1], r2], [r2])
                oo = f512.next()
                S.stt(oo[:, 0:wd], r2[:, 0:wd], NLAM[:, L:L + 1], r1[:, 0:wd], ALU.mult, ALU.add, [r1, r2, NLAM], [oo])
                s = sq.next()
                S.act(s[:, 0:wd], oo[:, 0:wd], AF.Square, [oo], [s])
                p = psS.next()
                S.mm(p[:, 0:wd], onesm_b[:], s[:, 0:wd], True, True, [onesm_b, s], [p])
                r = f512.next()
                S.act(r[:, 0:wd], p[:, 0:wd], AF.Ln, [p], [r], bias=EPS)
                S.act(r[:, 0:wd], r[:, 0:wd], AF.Exp, [r], [r], scale=-0.5)
                S.tt(oo[:, 0:wd], oo[:, 0:wd], r[:, 0:wd], ALU.mult, [oo, r], [oo])
                o = yo.next()
                S.stt(o[:, 0:wd], oo[:, 0:wd], GBR[:, L, 1, h:h + 1], gda[:, t0:t0 + wd], ALU.mult, ALU.mult,
                      [oo, GBR, gda], [o])
                S.dma(ys_d[8 + h, :, t0:t0 + wd], o[:, 0:wd], R=[o], W=[ys_d], owner=o)
        S.close()
        if stopped("S3_%d" % L):
            break

        S = Phase(nc, persist)
        wstg = Rot([S.sbuf("wstg%d" % i, [128, 8, 128], F32) for i in range(2)])
        wcb = Rot([S.sbuf("wcb%d" % i, [128, 8, 128], BF16) for i in range(2)])
        wob = S.sbuf("wob", [128, KC, D], BF16)
        wos = Rot([S.sbuf("wos%d" % i, [128, D], F32) for i in range(1)])
        for br in range(3):
            for fc in range(KC):
                a = wstg.next()
                S.dma(a[:], wbr_d[L, br, :, fc * 128:(fc + 1) * 128].rearrange("(k p) c -> p k c", p=128),
                      R=[wbr_d], W=[a], owner=a)
                b = wcb.next()
                if (br * KC + fc) % 2 == 0:
                    S.act(b[:], a[:], AF.Copy, [a], [b])
                else:
                    S.copy(b[:], a[:], [a], [b])
                S.dma(wbrb_d[br, fc], b[:].rearrange("p a b -> p (a b)"), R=[b], W=[wbrb_d], owner=b)
        for kc in range(KC):
            a = wos.next()
            S.dma(a[:], wout_d[L, kc * 128:(kc + 1) * 128, :], R=[wout_d], W=[a], owner=a)
            if kc % 2:
                S.act(wob[:, kc, :], a[:], AF.Copy, [a], [wob])
            else:
                S.copy(wob[:, kc, :], a[:], [a], [wob])
        GGb = [S.sbuf("GGb%d" % j, [128, D], F32) for j in range(2)]
        onesf4 = S.sbuf("onesf4", [128, 128], F32)
        S.op("pool", lambda e: e.memset(onesf4[:], 1.0), W=[onesf4])
        dg = Rot([S.sbuf("dg%d" % i, [128, 128], F32) for i in range(2)])
        ps4 = Rot([S.psum("ps4_%d" % i, [128, 512], F32) for i in range(8)])
        for j in range(2):
            for q in range(4):
                p = ps4.next()
                for r in range(4):
                    kc = q * 4 + r
                    dgt = dg.next()
                    S.ts(dgt[:], identf[:], GGvec[:, L, j, kc:kc + 1], None, ALU.mult, None, [identf, GGvec], [dgt])
                    S.mm(p[:, r * 128:(r + 1) * 128], onesf4[:], dgt[:], r == 0, True, [dgt, onesf4], [p])
                S.copy(GGb[j][:, q * 512:(q + 1) * 512], p[:], [p], [GGb[j]])
        ysT = Rot([S.sbuf("ysT%d" % i, [128, 24, 512], BF16) for i in range(1)])
        gmt = Rot([S.sbuf("gmt%d" % i, [128, 3, 512], BF16) for i in range(3)])
        wbt = Rot([S.sbuf("wbt%d" % i, [128, 3, 8, 128], BF16) for i in range(2)])
        mrg = Rot([S.sbuf("mrg%d" % i, [128, KC, 512], BF16) for i in range(1)])
        m512 = Rot([S.sbuf("m512_%d" % i, [128, 512], F32) for i in range(4)])
        xres = Rot([S.sbuf("xres%d" % i, [128, D], F32) for i in range(1)])
        yout = Rot([S.sbuf("yout%d" % i, [128, D], F32) for i in range(2)])
        ss4 = Rot([S.sbuf("ss4_%d" % i, [128, 4], F32) for i in range(2)])
        junk4 = S.sbuf("junk4", [128, 512], BF16)
        for (t0, wd) in (TG if ctx_on else TG[1:]):
            if opts.get('upto') == 's4_prep':
                break
            if opts.get('upto') in ('s4_proj', 's4_tile') and t0 > 0:
                break
            j = 1 if t0 == 0 else 0
            yt = ysT.next()
            for br in range(3):
                S.dma(yt[:, br * 8:(br + 1) * 8, 0:wd], ys_d[br * 8:(br + 1) * 8, :, t0:t0 + wd].rearrange("a p t -> p a t"),
                      R=[ys_d], W=[yt], owner=yt)
            mg = mrg.next()
            for fc in range(KC):
                wb = wbt.next()
                S.dma(wb[:].rearrange("p a b c -> p a (b c)"), wbrb_d[:, fc].rearrange("a p x -> p a x"), R=[wbrb_d],
                      W=[wb], owner=wb)
                gt = gmt.next()
                S.dma(gt[:, :, 0:wd], gm_d[:, :, t0:t0 + wd].rearrange("(a f) p t -> f p a t", f=KC)[fc],
                      R=[gm_d], W=[gt], owner=gt)
                for br in range(3):
                    S.act(gt[:, br, 0:wd], gt[:, br, 0:wd], AF.Sigmoid, [gt], [gt])
                ms = []
                for br in range(3):
                    p = ps4.next()
                    for k8 in range(8):
                        S.mm(p[:, 0:wd], wb[:, br, k8, :], yt[:, br * 8 + k8, 0:wd], k8 == 0, k8 == 7, [wb, yt], [p])
                    m = m512.next()
                    S.tt(m[:, 0:wd], p[:, 0:wd], gt[:, br, 0:wd], ALU.mult, [p, gt], [m])
                    ms.append(m)
                S.tt(ms[0][:, 0:wd], ms[0][:, 0:wd], ms[1][:, 0:wd], ALU.add, [ms[0], ms[1]], [ms[0]])
                S.tt(mg[:, fc, 0:wd], ms[0][:, 0:wd], ms[2][:, 0:wd], ALU.add, [ms[0], ms[2]], [mg])
            for ts_ in range(wd // 128):
                if opts.get('upto') == 's4_proj':
                    break
                tok0 = t0 + ts_ * 128
                xr = xres.next()
                if tok0 < TCX:
                    S.dma(xr[:], csrc[tok0:tok0 + 128, :], R=[csrc], W=[xr], owner=xr)
                else:
                    S.dma(xr[:], xsrc[tok0 - TCX:tok0 - TCX + 128, :], R=[xsrc], W=[xr], owner=xr)
                yo_ = yout.next()
                s4 = ss4.next()
                for q in range(4):
                    p = ps4.next()
                    for kc in range(KC):
                        S.mm(p[:], mg[:, kc, ts_ * 128:(ts_ + 1) * 128], wob[:, kc, q * 512:(q + 1) * 512], kc == 0,
                             kc == KC - 1, [mg, wob], [p])
                    S.copy(yo_[:, q * 512:(q + 1) * 512], p[:], [p], [yo_])
                    S.act(junk4[:], yo_[:, q * 512:(q + 1) * 512], AF.Square, [yo_], [junk4, s4], accum=s4[:, q:q + 1])
                S.op("dve", lambda e, s4=s4: e.reduce_sum(out=s4[:, 0:1], in_=s4[:, 0:4], axis=AX.X), [s4], [s4])
                S.ts(s4[:, 0:1], s4[:, 0:1], 1.0 / D, EPS, ALU.mult, ALU.add, [s4], [s4])
                S.act(s4[:, 0:1], s4[:, 0:1], AF.Sqrt, [s4], [s4])
                S.op("dve", lambda e, s4=s4: e.reciprocal(out=s4[:, 0:1], in_=s4[:, 0:1]), [s4], [s4])
                S.stt(yo_[:], yo_[:], s4[:, 0:1], GGb[j][:], ALU.mult, ALU.mult, [yo_, s4, GGb[j]], [yo_])
                S.tt(yo_[:], yo_[:], xr[:], ALU.add, [yo_, xr], [yo_])
                if tok0 < TCX:
                    dst, r0 = c1_d, tok0
                else:
                    dst, r0 = (x1_d if L == 0 else out_d), tok0 - TCX
                S.dma(dst[r0:r0 + 128, :], yo_[:], R=[yo_], W=[dst], owner=yo_)
        S.close()
        if stopped("S4_%d" % L):
            break
    pstack.close()
    return nc


def _rope_tables(half):
    f32 = np.float32
    pos_rt = np.concatenate([np.arange(TCX, dtype=f32), TCX + half * TL + np.arange(TL, dtype=f32)])
    inv = (10000.0 ** (-np.arange(0, 128, 2, dtype=f32) / 128)).astype(f32)
    ang = pos_rt[None, :] * np.concatenate([inv, inv])[:, None]
    rt_cos = np.cos(ang)
    rt_sin = np.sin(ang) * np.concatenate([-np.ones(64), np.ones(64)])[:, None]
    n = half * TL + np.arange(TL)
    row = (n // 64).astype(f32)
    col = (n % 64).astype(f32)
    inv_a = (10000.0 ** (-np.arange(0, 32, 2, dtype=f32) / 32)).astype(f32)
    da_cos = np.ones((128, TT), f32)
    da_sin = np.zeros((128, TT), f32)
    for i in range(128):
        d = i % 64
        pos = row if d < 32 else col
        dd = d % 32
        a = pos * inv_a[dd % 16]
        da_cos[i, TCX:] = np.cos(a)
        da_sin[i, TCX:] = np.sin(a) * (-1.0 if dd < 16 else 1.0)
    return np.stack([rt_cos, rt_sin, da_cos, da_sin]).astype(f32).astype(ml_dtypes.bfloat16)


def _consts():
    bf = ml_dtypes.bfloat16
    ident = np.eye(128, dtype=np.float32)
    s = np.arange(128)[:, None]
    t = np.arange(128)[None, :]
    same = (s // 64) == (t // 64)
    maskF = (same & (s <= t)).astype(np.int32)
    maskB = (same & (s >= t)).astype(np.int32)
    perm_rt = np.zeros((128, 128), np.float32)
    perm_da = np.zeros((128, 128), np.float32)
    for m in range(128):
        perm_rt[(m + 64) % 128, m] = 1.0
        dd = m % 32
        partner = m + 16 if dd < 16 else m - 16
        perm_da[partner, m] = 1.0
    tau = np.tile((np.arange(512) % 64).astype(np.float32)[None, :], (128, 1))
    return dict(ident=ident.astype(bf), identf=ident, maskF=maskF, maskB=maskB, perm_rt=perm_rt.astype(bf),
                perm_da=perm_da.astype(bf), tau=tau)


def _fm(v, nchunk):
    v = np.asarray(v, np.float32)
    lead = v.shape[:-1]
    r = v.reshape(lead + (nchunk, 128))
    r = np.moveaxis(r, -1, 0)
    return np.ascontiguousarray(r)


_NC_CACHE = {}


def make_in_maps(x, c, ctx, c_ctx, w_mod, b_mod, g_pre, g_post, w_in, hg_lower, ret_decay, diff_lambda, g_branch,
                 w_branch, w_out):
    f32 = np.float32
    consts = _consts()
    shared = dict(
        w_mod=np.ascontiguousarray(w_mod, f32), w_in=np.ascontiguousarray(w_in, f32),
        w_branch=np.ascontiguousarray(w_branch, f32), w_out=np.ascontiguousarray(w_out, f32),
        bmod_fm=_fm(b_mod, 48).reshape(128, 96), gpre_fm=_fm(g_pre, KC).reshape(128, 32),
        gpost_fm=_fm(g_post, KC).reshape(128, 32),
        hgl_fm=_fm(hg_lower, NH).reshape(128, 32),
        rdec_b=np.ascontiguousarray(np.broadcast_to(np.asarray(ret_decay, f32).reshape(1, 32), (128, 32))),
        dlam_b=np.ascontiguousarray(np.broadcast_to(np.asarray(diff_lambda, f32).reshape(1, 512), (128, 512))),
        gbr_fm=_fm(g_branch, NH).reshape(128, 48), **consts)
    tabs = [_rope_tables(0), _rope_tables(1)]
    in_maps = []
    for core in range(8):
        b, half = core // 2, core % 2
        cv = np.stack([np.asarray(c[b], f32), np.asarray(c_ctx, f32)], axis=-1)
        cv = np.ascontiguousarray(cv.reshape(KC, 128, 2).transpose(1, 0, 2)).reshape(128, 32)
        fl = np.zeros((128, 2), f32)
        fl[:, 0] = half
        fl[:, 1] = half
        m = dict(shared)
        m.update(x=np.ascontiguousarray(x[b, half * TL:(half + 1) * TL], f32), ctx=np.ascontiguousarray(ctx[b], f32),
                 cvec=cv, tabs=tabs[half], flags=fl)
        in_maps.append(m)
    return in_maps


def kernel(x, c, ctx, c_ctx, w_mod, b_mod, g_pre, g_post, w_in, hg_lower, ret_decay, diff_lambda, g_branch,
           w_branch, w_out):
    in_maps = make_in_maps(x, c, ctx, c_ctx, w_mod, b_mod, g_pre, g_post, w_in, hg_lower, ret_decay, diff_lambda,
                           g_branch, w_branch, w_out)
    if "nc" not in _NC_CACHE:
        _NC_CACHE["nc"] = build_program()
    res = run_bass_kernel_spmd(_NC_CACHE["nc"], in_maps, core_ids=list(range(8)))
    out = np.empty((4, 4096, D), np.float32)
    for core in range(8):
        b, half = core // 2, core % 2
        out[b, half * TL:(half + 1) * TL] = res.results[core]["out"]
    return out
```
